# Optimizing a Trainium2 kernel written in Bass

```python
import jax, jax.numpy as jnp
from jax import lax
import numpy as np

D_MODEL = 2048
BATCH = 4
SEQ = 2048
DEPTH = 1
DEC_BATCH = 128
DEC_SEQ = 8
PAST_LEN = 16384
PAGE_SIZE = 128

HEAD_DIM = 128
C_A = D_MODEL // 2
C_B = D_MODEL - C_A
H_A = C_A // HEAD_DIM
H_B = C_B // HEAD_DIM
CONV_A = 31
CHUNK = 128
FFN_CONV = 3
D_FF = ((8 * D_MODEL // 3 + 255) // 256) * 256
D_PLE = 256
EPS = 1e-6

kernel_name = "hybrid_conformer_gmlp_decoder_step"


def rmsnorm(x, g):
    x32 = x.astype(jnp.float32)
    y = x32 * lax.rsqrt(jnp.mean(x32 * x32, axis=-1, keepdims=True) + EPS)
    return (y * g.astype(jnp.float32)).astype(x.dtype)


def head_layernorm(x, n_heads, g, b):
    shp = x.shape
    xh = x.astype(jnp.float32).reshape(shp[:-1] + (n_heads, shp[-1] // n_heads))
    mu = jnp.mean(xh, axis=-1, keepdims=True)
    xc = xh - mu
    var = jnp.mean(xc * xc, axis=-1, keepdims=True)
    y = (xc * lax.rsqrt(var + EPS)).reshape(shp)
    return (y * g.astype(jnp.float32) + b.astype(jnp.float32)).astype(x.dtype)


def causal_dwconv(x_ext, w, bias):
    c = x_ext.shape[-1]
    out = lax.conv_general_dilated(
        x_ext, w[:, None, :].astype(x_ext.dtype), window_strides=(1,), padding='VALID',
        dimension_numbers=('NWC', 'WIO', 'NWC'), feature_group_count=c)
    return out + bias.astype(out.dtype)


def chunk_spatial_gate(u, v, w_s, b_s):
    bsz, t, c = v.shape
    lc = min(t, CHUNK)
    n = -(-t // lc)
    pad = n * lc - t
    vp = jnp.pad(v, ((0, 0), (0, pad), (0, 0))).reshape(bsz, n, lc, H_B, HEAD_DIM)
    mask = jnp.tril(jnp.ones((lc, lc), dtype=bool))
    w = jnp.where(mask, w_s[:, :lc, :lc], 0).astype(v.dtype)
    mixed = jnp.einsum('hts,bnshd->bnthd', w, vp)
    mixed = mixed + b_s[:, :lc].T.astype(v.dtype)[None, None, :, :, None]
    mixed = mixed.reshape(bsz, n * lc, c)[:, :t]
    return u * mixed


def layer(x, p, conv_hist, ffn_hist, g_mix, w_in, w_dw_a, b_dw_a, g_ln_a, b_ln_a,
          g_ln_v, b_ln_v, w_s, b_s, w_out, g_ffn, w_up, w_dw_f, b_dw_f, w_down,
          g_ple, w_ple_gate, w_ple_proj):
    t = x.shape[1]
    h = rmsnorm(x, g_mix)
    z = h @ w_in
    a_val = z[..., :C_A]
    a_gate = z[..., C_A:2 * C_A]
    u = z[..., 2 * C_A:2 * C_A + C_B]
    v = z[..., 2 * C_A + C_B:]
    a = a_val * jax.nn.sigmoid(a_gate)
    a_ext = jnp.concatenate([conv_hist.astype(a.dtype), a], axis=1)
    a_out = jax.nn.silu(head_layernorm(causal_dwconv(a_ext, w_dw_a, b_dw_a), H_A, g_ln_a, b_ln_a))
    new_conv_hist = a_ext[:, -(CONV_A - 1):]
    u = jax.nn.gelu(u)
    v_n = head_layernorm(jax.nn.gelu(v), H_B, g_ln_v, b_ln_v)
    b_out = chunk_spatial_gate(u, v_n, w_s, b_s)
    last_start = ((t - 1) // CHUNK) * CHUNK
    chunk_v = v_n[:, last_start:]
    x = x + jnp.concatenate([a_out, b_out], axis=-1) @ w_out
    up = rmsnorm(x, g_ffn) @ w_up
    up_ext = jnp.concatenate([ffn_hist.astype(up.dtype), up], axis=1)
    upc = causal_dwconv(up_ext, w_dw_f, b_dw_f)
    x = x + (jax.nn.silu(upc[..., :D_FF]) * upc[..., D_FF:]) @ w_down
    new_ffn_hist = up_ext[:, -(FFN_CONV - 1):]
    gate = jax.nn.sigmoid(rmsnorm(x, g_ple) @ w_ple_gate)
    x = x + gate * (p @ w_ple_proj)
    return x, new_conv_hist, new_ffn_hist, chunk_v


def setup_inputs(seed: int = 0) -> dict:
    key = jax.random.key(seed)
    ks = jax.random.split(key, 32)
    f32 = jnp.float32
    nrm = lambda k, shp, s: jax.random.normal(k, shp, f32) * s
    return {
        "x_prompt": nrm(ks[0], (BATCH, SEQ, D_MODEL), 1.0),
        "x_sample": nrm(ks[1], (DEC_BATCH, DEC_SEQ, D_MODEL), 1.0),
        "p_prompt": nrm(ks[2], (DEPTH, BATCH, SEQ, D_PLE), 1.0),
        "p_sample": nrm(ks[3], (DEPTH, DEC_BATCH, DEC_SEQ, D_PLE), 1.0),
        "state_conv_a": nrm(ks[4], (DEPTH, DEC_BATCH, CONV_A - 1, C_A), 0.5),
        "state_ffn_conv": nrm(ks[5], (DEPTH, DEC_BATCH, FFN_CONV - 1, 2 * D_FF), 1.0),
        "g_mix": 1.0 + nrm(ks[6], (DEPTH, D_MODEL), 0.1),
        "w_in": nrm(ks[7], (DEPTH, D_MODEL, 2 * C_A + 2 * C_B), D_MODEL ** -0.5),
        "w_dw_a": nrm(ks[8], (DEPTH, CONV_A, C_A), CONV_A ** -0.5),
        "b_dw_a": nrm(ks[9], (DEPTH, C_A), 0.02),
        "g_ln_a": 1.0 + nrm(ks[10], (DEPTH, C_A), 0.1),
        "b_ln_a": nrm(ks[11], (DEPTH, C_A), 0.02),
        "g_ln_v": 1.0 + nrm(ks[12], (DEPTH, C_B), 0.1),
        "b_ln_v": nrm(ks[13], (DEPTH, C_B), 0.02),
        "w_s": nrm(ks[14], (DEPTH, H_B, CHUNK, CHUNK), CHUNK ** -0.5),
        "b_s": nrm(ks[15], (DEPTH, H_B, CHUNK), 0.02),
        "w_out": nrm(ks[16], (DEPTH, C_A + C_B, D_MODEL), (C_A + C_B) ** -0.5),
        "g_ffn": 1.0 + nrm(ks[17], (DEPTH, D_MODEL), 0.1),
        "w_up": nrm(ks[18], (DEPTH, D_MODEL, 2 * D_FF), D_MODEL ** -0.5),
        "w_dw_f": nrm(ks[19], (DEPTH, FFN_CONV, 2 * D_FF), FFN_CONV ** -0.5),
        "b_dw_f": nrm(ks[20], (DEPTH, 2 * D_FF), 0.02),
        "w_down": nrm(ks[21], (DEPTH, D_FF, D_MODEL), D_FF ** -0.5),
        "g_ple": 1.0 + nrm(ks[22], (DEPTH, D_MODEL), 0.1),
        "w_ple_gate": nrm(ks[23], (DEPTH, D_MODEL, D_MODEL), D_MODEL ** -0.5),
        "w_ple_proj": nrm(ks[24], (DEPTH, D_PLE, D_MODEL), D_PLE ** -0.5),
        "g_final": 1.0 + nrm(ks[25], (D_MODEL,), 0.1),
    }


def reference(x_prompt, x_sample, p_prompt, p_sample, state_conv_a, state_ffn_conv,
              g_mix, w_in, w_dw_a, b_dw_a, g_ln_a, b_ln_a, g_ln_v, b_ln_v, w_s, b_s,
              w_out, g_ffn, w_up, w_dw_f, b_dw_f, w_down, g_ple, w_ple_gate, w_ple_proj,
              g_final):
    xp, xs = x_prompt, x_sample
    conv_p, conv_s, ffn_p, ffn_s, cv_p, cv_s = [], [], [], [], [], []
    for i in range(DEPTH):
        params = (g_mix[i], w_in[i], w_dw_a[i], b_dw_a[i], g_ln_a[i], b_ln_a[i],
                  g_ln_v[i], b_ln_v[i], w_s[i], b_s[i], w_out[i], g_ffn[i], w_up[i],
                  w_dw_f[i], b_dw_f[i], w_down[i], g_ple[i], w_ple_gate[i], w_ple_proj[i])
        hist_a0 = jnp.zeros((xp.shape[0], CONV_A - 1, C_A), xp.dtype)
        hist_f0 = jnp.zeros((xp.shape[0], FFN_CONV - 1, 2 * D_FF), xp.dtype)
        xp, ca, cf, cv = layer(xp, p_prompt[i], hist_a0, hist_f0, *params)
        conv_p.append(ca); ffn_p.append(cf); cv_p.append(cv)
        xs, ca, cf, cv = layer(xs, p_sample[i], state_conv_a[i], state_ffn_conv[i], *params)
        conv_s.append(ca); ffn_s.append(cf); cv_s.append(cv)
    y_prompt = rmsnorm(xp, g_final)
    y_sample = rmsnorm(xs, g_final)
    return (y_prompt, y_sample, jnp.stack(conv_p), jnp.stack(conv_s), jnp.stack(ffn_p),
            jnp.stack(ffn_s), jnp.stack(cv_p), jnp.stack(cv_s))
```

```python
import numpy as np
from contextlib import ExitStack

import concourse.bass as bass
import concourse.mybir as mybir
from concourse.bass_utils import run_bass_kernel_spmd

F32 = mybir.dt.float32
BF16 = mybir.dt.bfloat16
AF = mybir.ActivationFunctionType
ALU = mybir.AluOpType

EPS = 1e-6
D = 2048
CA = 1024
DFF = 5632
DPLE = 256
NCOL = 1280
NT = 10
GRAN = 2048
NGRAN = 12
NPOOL_TAPS = 0

C_ID, C_ONES, C_MT, C_BDM, C_E = 0, 128, 256, 384, 512
C_DWA = 640
C_BDWA = C_DWA + 248
C_GLNA = C_BDWA + 8
C_BLNA = C_GLNA + 8
C_GLNV = C_BLNA + 8
C_BLNV = C_GLNV + 8
C_DWF = C_BLNV + 8
C_BDWF = C_DWF + 264
C_HM = C_BDWF + 88
C_EPS = C_HM + 1
NCP = C_EPS + 1 + 6


class Sched:
    ENG = ("pe", "act", "dve", "pool")
    SELF_DIST = 2

    def __init__(self, nc, es):
        self.nc = nc
        self.es = es
        self.engs = {"pe": nc.tensor, "act": nc.scalar, "dve": nc.vector, "pool": nc.gpsimd, "sp": nc.sync}
        self.prog = {k: [] for k in self.engs}
        self.sems = {}
        self.cnt = {}
        for k in self.ENG:
            self.sems[k] = es.enter_context(nc.semaphore("s_" + k))
        self.nops = {k: 0 for k in self.ENG}
        self.need = {k: set() for k in self.ENG}
        self.lastw = {}
        self.readers = {}
        self.waited = {k: {} for k in self.engs}
        self.base = {}
        self.phase = 0
        self.kphase = {}

    def _mksem(self, name):
        self.sems[name] = self.es.enter_context(self.nc.semaphore("s_" + name))
        self.cnt[name] = 0

    def _deps(self, reads, writes):
        deps = {}

        def add(s, v):
            if deps.get(s, -1) < v:
                deps[s] = v

        for k in reads:
            if k in self.lastw:
                add(*self.lastw[k])
        for k in writes:
            if k in self.lastw:
                add(*self.lastw[k])
            for s, v in self.readers.get(k, {}).items():
                add(s, v)
            if not self._static(k) and self.kphase.get(k, 0) != self.phase:
                self.kphase[k] = self.phase
                for s, v in self.base.items():
                    add(s, v)
        return deps

    STATIC = ("wr", "ps", "HT", "ss", "rs", "X1", "X2", "y", "CP", "IDB", "ONESB")

    def _static(self, k):
        kind = k[0] if isinstance(k, tuple) else k
        return kind in self.STATIC or kind.startswith("o_")

    def _waits(self, e, deps):
        for s, v in deps.items():
            if s in self.ENG:
                if s == e:
                    if e == "pe" or self.nops[e] - v >= self.SELF_DIST:
                        continue
                if self.waited[e].get(s, -1) >= v:
                    continue
                self.waited[e][s] = v
                self.need[s].add(v)
                self.prog[e].append(("we", s, v))
            else:
                if self.waited[e].get(s, 0) >= v:
                    continue
                self.waited[e][s] = v
                self.prog[e].append(("wd", s, v))

    def _record(self, rec, reads, writes):
        s, v = rec
        for k in reads:
            r = self.readers.setdefault(k, {})
            if r.get(s, -1) < v:
                r[s] = v
        for k in writes:
            self.lastw[k] = rec
            self.readers[k] = {}

    def op(self, e, fn, reads=(), writes=(), signal=True):
        self._waits(e, self._deps(reads, writes))
        idx = self.nops[e]
        self.nops[e] += 1
        self.prog[e].append(("op", fn, idx))
        self._record((e, idx), reads, writes)

    def dma(self, q, sem, out, in_, reads=(), writes=()):
        if sem not in self.sems:
            self._mksem(sem)
        self._waits(q, self._deps(reads, writes))
        self.cnt[sem] += 16
        v = self.cnt[sem]
        self.prog[q].append(("dma", out, in_, sem))
        self._record((sem, v), reads, writes)

    def _all(self):
        d = {s: v for s, v in self.cnt.items() if v > 0}
        for e in self.ENG:
            if self.nops[e] > 0:
                d[e] = self.nops[e] - 1
        return d

    def barrier(self):
        self.base = {s: v for s, v in self._all().items() if not s.startswith("wr")}
        self.phase += 1

    def finish(self, e="sp"):
        self._waits(e, self._all())

    def replay(self, e, eng):
        rank = {}
        for k in self.ENG:
            rank[k] = {idx: i + 1 for i, idx in enumerate(sorted(self.need[k]))}
        for item in self.prog[e]:
            if item[0] == "op":
                ins = item[1](eng)
                if item[2] in self.need[e]:
                    ins.then_inc(self.sems[e], 1)
            elif item[0] == "we":
                eng.wait_ge(self.sems[item[1]], rank[item[1]][item[2]])
            elif item[0] == "wd":
                eng.wait_ge(self.sems[item[1]], item[2])
            else:
                eng.dma_start(out=item[1], in_=item[2]).then_inc(self.sems[item[3]], 16)


class Arena:
    def __init__(self, nc, nbytes):
        self.t = nc.alloc_sbuf_tensor("arena", [128, nbytes // 4], F32)
        self.nbytes = nbytes
        self.off = 0

    def alloc(self, shape, dtype):
        n = int(np.prod(shape[1:]))
        esz = 4 if dtype == F32 else 2
        nb = (n * esz + 31) // 32 * 32
        assert self.off + nb <= self.nbytes, f"arena overflow {self.off + nb} > {self.nbytes}"
        ap = self.t[: shape[0], self.off // 4 : (self.off + nb) // 4]
        if dtype == BF16:
            ap = ap.bitcast(BF16)
        ap = ap[:, :n]
        self.off += nb
        if len(shape) == 3:
            ap = ap.rearrange("p (a b) -> p a b", b=shape[2])
        elif len(shape) == 4:
            ap = ap.rearrange("p (a b c) -> p a b c", b=shape[2], c=shape[3])
        return ap


class WStream:
    def __init__(self, S, wr):
        self.S = S
        self.wr = wr
        self.steps = []
        self.next = 0
        self.released = set()

    def add(self, n, emit):
        self.steps.append({"n": n, "emit": emit})
        return len(self.steps) - 1

    def plan(self):
        cur = 0
        occ = {}
        for i, st in enumerate(self.steps):
            if cur + st["n"] > NGRAN:
                cur = 0
            st["g0"] = cur
            prev = set()
            for g in range(cur, cur + st["n"]):
                if g in occ:
                    prev.add(occ[g])
                occ[g] = i
            st["prev"] = prev
            cur += st["n"]

    def pump(self):
        while self.next < len(self.steps) and self.steps[self.next]["prev"] <= self.released:
            st = self.steps[self.next]
            st["emit"](st["g0"])
            self.next += 1

    def use(self, i):
        self.pump()
        assert self.next > i, f"weight step {i} not loaded (next={self.next})"
        return self.steps[i]["g0"]

    def release(self, i):
        self.released.add(i)
        self.pump()

    def view(self, g0, nelem):
        return self.wr[:, g0 * GRAN : g0 * GRAN + nelem]


def build_program():
    nc = bass.Bass("TRN2", target_bir_lowering=False)

    def din(name, shape):
        return nc.dram_tensor(name, shape, F32, kind="ExternalInput").ap()

    def dout(name, shape):
        return nc.dram_tensor(name, shape, F32, kind="ExternalOutput").ap()

    xin = din("xin", [NCOL, D])
    pin = din("pin", [1152, DPLE])
    sca = din("sca", [480, CA])
    sff = din("sff", [32, 2 * DFF])
    cpack = din("cpack", [128, NCP])
    g_mix = din("g_mix", [D])
    g_ffn = din("g_ffn", [D])
    g_ple = din("g_ple", [D])
    g_fin = din("g_fin", [D])
    bs_p = din("bs_p", [1024])
    bs_s = din("bs_s", [1024])
    w_s = din("w_s", [8, 128, 128])
    w_in = din("w_in", [D, 4096])
    w_out = din("w_out", [D, D])
    w_up = din("w_up", [D, 2 * DFF])
    w_down = din("w_down", [DFF, D])
    w_pg = din("w_pg", [D, D])
    w_pp = din("w_pp", [DPLE, D])

    y = dout("y", [1152, D])
    o_cap = dout("o_cap", [30, CA])
    o_cas = dout("o_cas", [16, 30, CA])
    o_ffp = dout("o_ffp", [2, 2 * DFF])
    o_ffs = dout("o_ffs", [32, 2 * DFF])
    o_cvp = dout("o_cvp", [128, CA])
    o_cvs = dout("o_cvs", [128, CA])

    X1 = nc.dram_tensor("x1_scratch", [NCOL, D], F32).ap()
    X2 = nc.dram_tensor("x2_scratch", [1152, D], F32).ap()

    PS = nc.alloc_psum_tensor("PS", [128, 8, 512], F32)

    es = ExitStack()
    S = Sched(nc, es)

    total = nc.sbuf_bytes_remaining
    ar = Arena(nc, (total - 256) // 32 * 32)

    CP = ar.alloc([128, NCP], F32)
    IDF = CP[:, C_ID : C_ID + 128]
    ONESF = CP[:, C_ONES : C_ONES + 128]
    MT = CP[:, C_MT : C_MT + 128]
    BDM = CP[:, C_BDM : C_BDM + 128]
    EM = CP[0:8, C_E : C_E + 128]
    DWA = CP[:, C_DWA : C_DWA + 248].rearrange("p (j k) -> p j k", k=31)
    BDWA = CP[:, C_BDWA : C_BDWA + 8]
    GLNA = CP[:, C_GLNA : C_GLNA + 8]
    BLNA = CP[:, C_BLNA : C_BLNA + 8]
    GLNV = CP[:, C_GLNV : C_GLNV + 8]
    BLNV = CP[:, C_BLNV : C_BLNV + 8]
    DWF = CP[:, C_DWF : C_DWF + 264].rearrange("p (j k) -> p j k", k=3)
    BDWF = CP[:, C_BDWF : C_BDWF + 88]
    HM = CP[:, C_HM : C_HM + 1]
    EPSC = CP[:, C_EPS : C_EPS + 1]
    IDB = ar.alloc([128, 128], BF16)
    ONESB = ar.alloc([128, 128], BF16)
    SSC = ar.alloc([128, 128], F32)
    RSC = ar.alloc([128, 128], F32)
    HT = ar.alloc([128, 16, NCOL], BF16)
    PT8 = ar.alloc([128, 2, 128], BF16)
    WR = ar.alloc([128, NGRAN * GRAN], BF16)
    static_end = ar.off

    W = WStream(S, WR)

    mb_state = [0]
    xb_state = [0]

    def mbank():
        b = mb_state[0]
        mb_state[0] = (b + 1) % 4
        return b

    xb_list = [[4, 5, 6, 7]]
    xb2_state = [0]

    def xbank():
        lst = xb_list[0]
        b = lst[xb_state[0] % len(lst)]
        xb_state[0] += 1
        return b

    def xbank2():
        b = 4 + 2 * (xb2_state[0] % 2)
        xb2_state[0] += 1
        return b

    def psk(b):
        return ("ps", b)

    def ACT(out, in_, func, reads, writes, **kw):
        S.op("act", lambda e: e.activation(out=out, in_=in_, func=func, **kw), reads, writes)

    def TT(eng, out, in0, in1, op, reads, writes):
        S.op(eng, lambda e: e.tensor_tensor(out=out, in0=in0, in1=in1, op=op), reads, writes)

    def TS(eng, out, in0, s1, s2, op0, op1, reads, writes):
        if s2 is None:
            S.op(eng, lambda e: e.tensor_scalar(out=out, in0=in0, scalar1=s1, scalar2=None, op0=op0), reads, writes)
        else:
            S.op(eng, lambda e: e.tensor_scalar(out=out, in0=in0, scalar1=s1, scalar2=s2, op0=op0, op1=op1), reads, writes)

    def STT(eng, out, in0, sc, in1, op0, op1, reads, writes):
        S.op(eng, lambda e: e.scalar_tensor_tensor(out=out, in0=in0, scalar=sc, in1=in1, op0=op0, op1=op1), reads, writes)

    def PSTT(out, in0, sc, in1, tmp, tkey, reads, writes):
        TS("pool", tmp, in0, sc, None, ALU.mult, None, reads, [tkey])
        TT("pool", out, tmp, in1, ALU.add, reads + [tkey], writes)

    def CPY(eng, out, in_, reads, writes):
        S.op(eng, lambda e: e.tensor_copy(out=out, in_=in_), reads, writes)

    def MM(out, lhsT, rhs, start, stop, reads, writes, signal):
        S.op("pe", lambda e: e.matmul(out, lhsT=lhsT, rhs=rhs, start=start, stop=stop), reads, writes, signal=signal)

    def TR(out, in_, ident, reads, writes, signal):
        S.op("pe", lambda e: e.transpose(out=out, in_=in_, identity=ident), reads, writes, signal=signal)

    def mmgroup(out, K, lhs_fn, rhs_fn, reads, bank):
        for k in range(K):
            MM(out, lhs_fn(k), rhs_fn(k), k == 0, k == K - 1, reads, [psk(bank)], signal=(k == K - 1))

    import os
    STOP = os.environ.get("KSTOP", "")

    class _Stop(Exception):
        pass

    DUMP = os.environ.get("KDUMP", "").split(",")
    dbg_outs = []

    def dbg_dump(label, ap, reads):
        if label not in DUMP:
            return
        shp = [int(x) for x in ap.shape]
        dt = nc.dram_tensor("dbg_" + label, shp, ap.dtype, kind="ExternalOutput").ap()
        S.dma("sp", "dbg_" + label, dt, ap, reads=reads, writes=["o_dbg_" + label])

    def stop_at(name):
        if STOP == name:
            raise _Stop()

    try:
        S.dma("sp", "cp", CP, cpack[:, :], writes=["CP"])
        CPY("dve", IDB, IDF, ["CP"], ["IDB"])
        CPY("dve", ONESB, ONESF, ["CP"], ["ONESB"])
        TT("dve", ONESF, IDF, ONESF, ALU.subtract, ["CP", "ONESB"], ["CP"])

        def wload(dst, src, keys):
            S.dma("pool", f"wr{keys[0][1]}", dst, src, writes=keys)

        def mk_in(c0):
            def emit(g0):
                wload(W.view(g0, GRAN).rearrange("p (k n) -> p k n", n=128),
                      w_in[:, c0 : c0 + 128].rearrange("(k p) n -> p k n", p=128), [("wr", g0)])
            return emit

        def mk_in_pair(c0, c1):
            def emit(g0):
                wload(W.view(g0, GRAN).rearrange("p (k n) -> p k n", n=128),
                      w_in[:, c0 : c0 + 128].rearrange("(k p) n -> p k n", p=128), [("wr", g0)])
                wload(W.view(g0 + 1, GRAN).rearrange("p (k n) -> p k n", n=128),
                      w_in[:, c1 : c1 + 128].rearrange("(k p) n -> p k n", p=128), [("wr", g0 + 1)])
            return emit

        def mk_sq(wt, cg):
            def emit(g0):
                wload(W.view(g0, 4 * GRAN).rearrange("p (k n) -> p k n", n=512),
                      wt[:, cg * 512 : (cg + 1) * 512].rearrange("(k p) n -> p k n", p=128),
                      [("wr", g0 + i) for i in range(4)])
            return emit

        def mk_up_pair(j):
            def emit(g0):
                for i, c0 in enumerate((j * 128, DFF + j * 128)):
                    wload(W.view(g0 + i, GRAN).rearrange("p (k n) -> p k n", n=128),
                          w_up[:, c0 : c0 + 128].rearrange("(k p) n -> p k n", p=128), [("wr", g0 + i)])
            return emit

        def mk_down(cg, kh):
            def emit(g0):
                wload(W.view(g0, 22 * 256).rearrange("p (k n) -> p k n", n=256),
                      w_down[kh * 2816 : (kh + 1) * 2816, cg * 256 : (cg + 1) * 256].rearrange("(k p) n -> p k n", p=128),
                      [("wr", g0 + i) for i in range(3)])
            return emit

        st_a = []
        for j in range(8):
            sa = W.add(2, mk_in_pair(j * 128, 1024 + j * 128))
            su = W.add(1, mk_in(2048 + j * 128))
            sv = W.add(1, mk_in(3072 + j * 128))
            st_a.append((sa, su, sv))
        st_b = [W.add(4, mk_sq(w_out, cg)) for cg in range(4)]
        st_c = [W.add(2, mk_up_pair(j)) for j in range(44)]
        st_d = [[W.add(3, mk_down(cg, kh)) for kh in range(2)] for cg in range(8)]
        st_e = [W.add(4, mk_sq(w_pg, cg)) for cg in range(4)]
        W.plan()
        W.pump()

        SEGA = [(0, 512), (512, 1024), (1024, 1280)]
        SEGC = [(126, 638), (638, 1150), (1150, 1280)]

        def norm_a(XTt, XSt, GBC, col, xt_key, xs_key):
            ss = SSC[:, col : col + 1]
            rs = RSC[:, col : col + 1]
            ACT(XSt, XTt, AF.Square, [xt_key], [xs_key, ("ss", col)], scale=float(D ** -0.5), accum_out=ss)
            ACT(rs, ss, AF.Ln, [("ss", col), "CP"], [("rs", col)], bias=EPSC, scale=1.0)
            ACT(rs, rs, AF.Exp, [("rs", col)], [("rs", col)], scale=-0.5)
            STT("dve", XSt, XTt, rs, GBC, ALU.mult, ALU.mult, [xt_key, ("rs", col), "GBC"], [xs_key])

        def norm_b(XSt, xs_key, htcol):
            b = xbank2()
            pv = PS[:, b : b + 2, :].rearrange("p a b -> p (a b)").bitcast(BF16)
            for k in range(16):
                TR(pv[:, k * 128 : (k + 1) * 128], XSt[:, k * 128 : (k + 1) * 128], IDB,
                   [xs_key, "IDB"], [psk(b), psk(b + 1)], signal=(k == 15))
            ACT(HT[:, :, htcol : htcol + 128], pv.rearrange("p (k n) -> p k n", n=128), AF.Copy,
                [psk(b), psk(b + 1)], [("HT", htcol // 128)])

        m0 = ar.off
        HTALL0 = [('HT', i) for i in range(NT)]
        XT = [ar.alloc([128, D], F32) for _ in range(4)]
        XS = [ar.alloc([128, D], BF16) for _ in range(4)]
        GBC = ar.alloc([128, D], F32)
        S.dma("sp", "gbc", GBC, g_mix.partition_broadcast(128), writes=["GBC"])
        for i in range(NT + 2):
            if i < NT:
                sl = i % 4
                S.dma("sp", f"xt{sl}", XT[sl], xin[i * 128 : (i + 1) * 128, :], writes=[("XT", sl)])
                norm_a(XT[sl], XS[sl], GBC, i, ("XT", sl), ("XS", sl))
            if i >= 2:
                norm_b(XS[(i - 2) % 4], ("XS", (i - 2) % 4), (i - 2) * 128)
        dbg_dump('HT0', HT[:, :, :], HTALL0)
        ar.off = m0
        S.barrier()
        stop_at('0')

        MIXT = ar.alloc([128, 16, NCOL], BF16)
        mA = ar.off
        WTt = ar.alloc([128, 8, 128], BF16)
        WSt = ar.alloc([128, 8, 128], BF16)
        mA2 = ar.off
        W8 = ar.alloc([128, 8, 8], F32)
        WSF = ar.alloc([128, 8, 128], F32)
        C8 = ar.alloc([128, 128], F32)
        S.dma("sp", "wsf", WSF, w_s.rearrange("h t s -> t h s"), writes=["WSF"])
        for h in range(8):
            b = xbank()
            TR(PS[:, b, 0:128], WSF[:, h, :], IDF, ["WSF", "CP"], [psk(b)], signal=True)
            TT("dve", WTt[:, h, :], PS[:, b, 0:128], MT, ALU.mult, [psk(b), "CP"], [("WT", h)])
        S.dma("sp", "w8", W8[0:8, :, :], w_s[:, 0:8, 0:8].rearrange("h t s -> t h s"), writes=["W8"])
        for h in range(8):
            b = xbank()
            MM(PS[0:8, b, 0:128], W8[0:8, h, :], EM, True, True, ["W8", "CP"], [psk(b)], signal=True)
            CPY("dve", C8[0:8, :], PS[0:8, b, 0:128], [psk(b)], ["C8"])
            b2 = xbank()
            MM(PS[:, b2, 0:128], EM, C8[0:8, :], True, True, ["C8", "CP"], [psk(b2)], signal=True)
            TT("dve", WSt[:, h, :], PS[:, b2, 0:128], BDM, ALU.mult, [psk(b2), "CP"], [("WS", h)])

        ar.off = mA2
        S.barrier()
        stop_at('prep')
        BSH = [ar.alloc([128, 2, 128], F32) for _ in range(2)]
        Abuf = [ar.alloc([128, NCOL], F32) for _ in range(2)]
        ASX = [ar.alloc([128, 16, 38], F32) for _ in range(2)]
        SCAT = [ar.alloc([128, 4, 128], F32) for _ in range(2)]
        SIG = ar.alloc([128, NCOL], F32)
        CV = [ar.alloc([128, 1154], F32) for _ in range(2)]
        SQ = ar.alloc([128, NCOL], BF16)
        SQ2 = ar.alloc([128, NCOL], BF16)
        GV = ar.alloc([128, NCOL], F32)
        VNH = [ar.alloc([128, 10, 128], BF16) for _ in range(2)]
        CVO = [ar.alloc([128, 2, 128], F32) for _ in range(2)]
        CAS = ar.alloc([128, CA], F32)
        CAPt = ar.alloc([128, CA], F32)
        TMPB = [ar.alloc([128, 512], F32) for _ in range(1)]
        PTMP3 = ar.alloc([128, 16, 8], F32)
        CVB = ar.alloc([128, 1026], F32)
        print('phaseA arena', ar.off, ar.nbytes)
        S.op("pool", lambda e: e.memset(MIXT[:, 0:8, 0:126], 0.0), [], [("MIXa", j) for j in range(8)])

        S.dma("sp", "casd", o_cas[:, 0:22, :], sca.rearrange("(s r) c -> s r c", r=30)[:, 8:30, :], writes=["o_cas_hist"])

        HTALL = [("HT", i) for i in range(NT)]
        CVSEG_A = [(0, 512), (512, 1024), (1024, 1154)]

        def chain_ln(buf, segs, keys, SQx, sqn):
            for (c0, c1) in segs:
                w = c1 - c0
                b = xbank()
                MM(PS[:, b, 0:w], ONESF, buf[:, c0:c1], True, True, keys + ["CP"], [psk(b)], signal=True)
                ACT(SQx[:, c0:c1], PS[:, b, 0:w], AF.Square, [psk(b)], [(sqn, c0)])
                ACT(buf[:, c0:c1], PS[:, b, 0:w], AF.Copy, [psk(b)] + keys, keys)
                yield
                b2 = xbank()
                MM(PS[:, b2, 0:w], ONESB, SQx[:, c0:c1], True, True, [(sqn, c0), "ONESB"], [psk(b2)], signal=True)
                ACT(PS[:, b2, 0:w], PS[:, b2, 0:w], AF.Ln, [psk(b2), "CP"], [psk(b2)], bias=EPSC, scale=1.0)
                ACT(PS[:, b2, 0:w], PS[:, b2, 0:w], AF.Exp, [psk(b2)], [psk(b2)], scale=-0.5)
                TT("dve", buf[:, c0:c1], buf[:, c0:c1], PS[:, b2, 0:w], ALU.mult, keys + [psk(b2)], keys)
                yield

        def a_main(j):
            sa, su, sv = st_a[j]
            sl = j % 2
            A = Abuf[sl]
            akey = ("A", sl)
            g0 = W.use(sa)
            Wv = W.view(g0, GRAN).rearrange("p (k n) -> p k n", n=128)
            Wg = W.view(g0 + 1, GRAN).rearrange("p (k n) -> p k n", n=128)
            for s_, (c0, c1) in enumerate(SEGA):
                w = c1 - c0
                bv = mbank()
                mmgroup(PS[:, bv, 0:w], 16, lambda k: Wv[:, k, :], lambda k: HT[:, k, c0:c1], HTALL + [("wr", g0)], bv)
                bg = mbank()
                mmgroup(PS[:, bg, 0:w], 16, lambda k: Wg[:, k, :], lambda k: HT[:, k, c0:c1], HTALL + [("wr", g0 + 1)], bg)
                ACT(SIG[:, c0:c1], PS[:, bg, 0:w], AF.Sigmoid, [psk(bg)], [("SIG", s_)])
                ACT(A[:, c0:c1], PS[:, bv, 0:w], AF.Copy, [psk(bv)], [akey])
                yield
                yield
            W.release(sa)
            TT("dve", A[:, :], A[:, :], SIG[:, :], ALU.mult, [akey] + [("SIG", s_) for s_ in range(3)], [akey])
            g0 = W.use(su)
            Wu = W.view(g0, GRAN).rearrange("p (k n) -> p k n", n=128)
            for s_, (c0, c1) in enumerate(SEGA):
                w = c1 - c0
                bu = mbank()
                mmgroup(PS[:, bu, 0:w], 16, lambda k: Wu[:, k, :], lambda k: HT[:, k, c0:c1], HTALL + [("wr", g0)], bu)
                ACT(MIXT[:, 8 + j, c0:c1], PS[:, bu, 0:w], AF.Gelu_apprx_tanh, [psk(bu)], [("MIXb", j, s_)])
                yield
            W.release(su)
            g0 = W.use(sv)
            Wvv = W.view(g0, GRAN).rearrange("p (k n) -> p k n", n=128)
            for s_, (c0, c1) in enumerate(SEGA):
                w = c1 - c0
                bvv = mbank()
                mmgroup(PS[:, bvv, 0:w], 16, lambda k: Wvv[:, k, :], lambda k: HT[:, k, c0:c1], HTALL + [("wr", g0)], bvv)
                ACT(GV[:, c0:c1], PS[:, bvv, 0:w], AF.Gelu_apprx_tanh, [psk(bvv)], ["GV"])
                yield
            W.release(sv)

        def a_conv(j):
            sl = j % 2
            A = Abuf[sl]
            akey = ("A", sl)
            S.dma("sp", f"scat{sl}", SCAT[sl][0:120, :, :],
                  sca[:, j * 128 : (j + 1) * 128].rearrange("(t r) c -> r t c", r=120), writes=[("SCAT", sl)])
            b = xbank()
            for t in range(4):
                TR(PS[:, b, t * 120 : (t + 1) * 120], SCAT[sl][0:120, t, :], IDF[0:120, 0:120],
                   [("SCAT", sl), "CP"], [psk(b)], signal=(t == 3))
            ACT(ASX[sl][:, :, 0:30], PS[:, b, 0:480].rearrange("p (s r) -> p s r", r=30), AF.Copy, [psk(b)], [("ASXh", sl)])
            CPY("dve", ASX[sl][:, :, 30:38], A[:, 1152:1280].rearrange("p (s t) -> p s t", t=8), [akey], [("ASXn", sl)])
            b = xbank()
            TR(PS[0:30, b, 0:128], A[:, 1122:1152], IDF, [akey, "CP"], [psk(b)], signal=True)
            ACT(CAPt[0:30, j * 128 : (j + 1) * 128], PS[0:30, b, 0:128], AF.Copy, [psk(b)], [("CAP", j)])
            b = xbank()
            TR(PS[:, b, 0:128], A[:, 1152:1280], IDF, [akey, "CP"], [psk(b)], signal=True)
            ACT(CAS[:, j * 128 : (j + 1) * 128], PS[:, b, 0:128], AF.Copy, [psk(b)], [("CAS", j)])
            cv = CV[sl]
            ckm = ("CV", sl, "m")
            cks = ("CV", sl, "s")
            cvm = cv[:, 0:1026]
            cvs = cv[:, 1026:1154].rearrange("p (s t) -> p s t", t=8)
            ax = ASX[sl]
            skeys = [("ASXn", sl), ("ASXh", sl), "CP"]
            ACT(cvm, A[:, 126:1152], AF.Identity, [akey, "CP"], [ckm], scale=DWA[:, j, 30:31], bias=BDWA[:, j : j + 1])
            TS("dve", cvs, ax[:, :, 30:38], DWA[:, j, 30:31], BDWA[:, j : j + 1], ALU.mult, ALU.add, skeys, [cks])
            ACT(CVB, A[:, 96 + 29 : 96 + 29 + 1026], AF.Copy, [akey, "CP"], ["CVB"], scale=DWA[:, j, 29:30])
            STT("dve", cvs, ax[:, :, 29:37], DWA[:, j, 29:30], cvs, ALU.mult, ALU.add, skeys + [cks], [cks])
            yield
            for k in range(29):
                if k % 2 == 0:
                    STT("dve", cvm, A[:, 96 + k : 96 + k + 1026], DWA[:, j, k : k + 1], cvm, ALU.mult, ALU.add, [akey, "CP", ckm], [ckm])
                else:
                    STT("dve", CVB, A[:, 96 + k : 96 + k + 1026], DWA[:, j, k : k + 1], CVB, ALU.mult, ALU.add, [akey, "CP", "CVB"], ["CVB"])
                STT("dve", cvs, ax[:, :, k : k + 8], DWA[:, j, k : k + 1], cvs, ALU.mult, ALU.add, skeys + [cks], [cks])
                if k % 3 == 2:
                    yield
            TT("dve", cvm, cvm, CVB, ALU.add, [ckm, "CVB"], [ckm])
            yield

        def chain_a(j):
            sl = j % 2
            cv = CV[sl]
            ckm = ("CV", sl, "m")
            cks = ("CV", sl, "s")
            yield from chain_ln(cv, CVSEG_A, [ckm, cks], SQ, 'SQa')
            ACT(MIXT[:, j, 126:1280], cv[:, 0:1154], AF.Silu, [ckm, cks, "CP"], [("MIXa", j)],
                scale=GLNA[:, j : j + 1], bias=BLNA[:, j : j + 1])
            yield

        def chain_v(j):
            yield from chain_ln(GV, SEGA, ["GV"], SQ2, "SQv")
            ACT(GV[:, :], GV[:, :], AF.Identity, ["GV", "CP"], ["GV"], scale=GLNV[:, j : j + 1], bias=BLNV[:, j : j + 1])
            vs = j % 2
            vnh = VNH[vs]
            S.dma("sp", f"bsh{vs}a", BSH[vs][:, 0, :], bs_p[j * 128 : (j + 1) * 128].partition_broadcast(128), writes=[("BSH", vs, 0)])
            S.dma("sp", f"bsh{vs}b", BSH[vs][:, 1, :], bs_s[j * 128 : (j + 1) * 128].partition_broadcast(128), writes=[("BSH", vs, 1)])
            for (i0, i1) in ((0, 4), (4, 8), (8, 10)):
                n = i1 - i0
                b = xbank()
                for i in range(i0, i1):
                    TR(PS[:, b, (i - i0) * 128 : (i - i0 + 1) * 128], GV[:, i * 128 : (i + 1) * 128], IDF,
                       ["GV", "CP"], [psk(b)], signal=(i == i1 - 1))
                ACT(vnh[:, i0:i1, :], PS[:, b, 0 : n * 128].rearrange("p (a c) -> p a c", c=128), AF.Copy,
                    [psk(b)], [("VNH", vs, i0)])
                if i0 == 8:
                    ACT(CVO[vs][:, :, :], PS[:, b, 0:256].rearrange("p (a c) -> p a c", c=128), AF.Copy, [psk(b)], [("CVO", vs)])
                    S.dma("sp", f"cvo{vs}a", o_cvp[:, j * 128 : (j + 1) * 128], CVO[vs][:, 0, :], reads=[("CVO", vs)], writes=[("o_cvp", j)])
                    S.dma("sp", f"cvo{vs}b", o_cvs[:, j * 128 : (j + 1) * 128], CVO[vs][:, 1, :], reads=[("CVO", vs)], writes=[("o_cvs", j)])
                yield
            for gi, (i0, i1) in enumerate(((0, 4), (4, 8), (8, 10))):
                n = i1 - i0
                b = xbank()
                for i in range(i0, i1):
                    rhs = WTt[:, j, :] if i < 9 else WSt[:, j, :]
                    MM(PS[:, b, (i - i0) * 128 : (i - i0 + 1) * 128], vnh[:, i, :], rhs, True, True,
                       [("VNH", vs, i0), ("WT", j), ("WS", j)], [psk(b)], signal=(i == i1 - 1))
                tb = TMPB[0]
                tkey = ("TMPB", 0)
                if i0 < 8:
                    TT("dve", tb[:, 0 : n * 128].rearrange("p (a c) -> p a c", c=128),
                       PS[:, b, 0 : n * 128].rearrange("p (a c) -> p a c", c=128),
                       BSH[vs][:, 0:1, :].to_broadcast([128, n, 128]), ALU.add, [psk(b), ("BSH", vs, 0)], [tkey])
                else:
                    TT("dve", tb[:, 0:256].rearrange("p (a c) -> p a c", c=128),
                       PS[:, b, 0:256].rearrange("p (a c) -> p a c", c=128), BSH[vs][:, :, :], ALU.add,
                       [psk(b), ("BSH", vs, 0), ("BSH", vs, 1)], [tkey])
                mk = [("MIXb", j, s_) for s_ in range(3)]
                TT("dve", MIXT[:, 8 + j, i0 * 128 : i1 * 128], tb[:, 0 : n * 128], MIXT[:, 8 + j, i0 * 128 : i1 * 128],
                   ALU.mult, [tkey] + mk, mk)
                yield

        def advance(gens):
            for g in list(gens):
                try:
                    next(g)
                except StopIteration:
                    gens.remove(g)

        pending = []
        for it in range(8 + 2):
            if 1 <= it <= 8:
                pending.append(a_conv(it - 1))
                pending.append(chain_v(it - 1))
            if 2 <= it <= 9:
                pending.append(chain_a(it - 2))
            if it < 8:
                for _ in a_main(it):
                    advance(pending)
            while pending:
                advance(pending)

        stop_at('A10')
        S.dma("sp", "capo", o_cap[:, :], CAPt[0:30, :], reads=[("CAP", j) for j in range(8)], writes=["o_cap"])
        for s in range(16):
            S.dma("sp", f"caso{s % 4}", o_cas[s, 22:30, :], CAS[s * 8 : (s + 1) * 8, :],
                  reads=[("CAS", j) for j in range(8)], writes=[("o_cas", s)])

        dbg_dump('MIXT', MIXT[:, :, :], [('MIXa', j) for j in range(8)] + [('MIXb', j, s) for j in range(8) for s in range(3)])
        ar.off = mA
        S.barrier()
        stop_at('A')

        XB = [ar.alloc([128, 512], F32) for _ in range(3)]
        X1B = [ar.alloc([128, 512], F32) for _ in range(3)]
        XT = [ar.alloc([128, D], F32) for _ in range(3)]
        XS = [ar.alloc([128, D], BF16) for _ in range(6)]
        GBC = ar.alloc([128, D], F32)
        S.dma("sp", "gbc", GBC, g_ffn.partition_broadcast(128), writes=["GBC"])
        MIXALL = [("MIXa", j) for j in range(8)] + [("MIXb", j, s) for j in range(8) for s in range(3)]
        cnt = 0

        def phaseB_norm_a(i):
            sl = i % 3
            S.dma("sp", f"xt{sl}", XT[sl], X1[i * 128 : (i + 1) * 128, :], reads=[("X1", i, cg) for cg in range(4)], writes=[("XT", sl)])
            norm_a(XT[sl], XS[i % 6], GBC, 10 + i, ("XT", sl), ("XS", i % 6))

        def phaseB_norm_b(i):
            norm_b(XS[i % 6], ("XS", i % 6), i * 128)

        for cg in range(4):
            g0 = W.use(st_b[cg])
            Wo = W.view(g0, 4 * GRAN).rearrange("p (k n) -> p k n", n=512)
            wkeys = [("wr", g0 + i) for i in range(4)]
            for i in range(NT):
                r = cnt % 3
                cnt += 1
                S.dma("sp", f"xb{r}", XB[r], xin[i * 128 : (i + 1) * 128, cg * 512 : (cg + 1) * 512], writes=[("XB", r)])
                b = mbank()
                mmgroup(PS[:, b, :], 16, lambda k: MIXT[:, k, i * 128 : (i + 1) * 128], lambda k: Wo[:, k, :], MIXALL + wkeys, b)
                TT("dve", X1B[r], PS[:, b, :], XB[r], ALU.add, [psk(b), ("XB", r)], [("X1B", r)])
                S.dma("act", f"x1b{r}", X1[i * 128 : (i + 1) * 128, cg * 512 : (cg + 1) * 512], X1B[r],
                      reads=[("X1B", r)], writes=[("X1", i, cg)])
                if cg == 3 and i >= 1:
                    phaseB_norm_a(i - 1)
                if cg == 3 and i >= 5:
                    phaseB_norm_b(i - 5)
            W.release(st_b[cg])
        phaseB_norm_a(NT - 1)
        for i in range(NT - 5, NT):
            phaseB_norm_b(i)

        ar.off = static_end
        S.barrier()

        stop_at('B')
        G = ar.alloc([128, 44, 1152], BF16)
        mG = ar.off
        CX = [ar.alloc([128, 1152], F32) for _ in range(2)]
        SX = [ar.alloc([128, 16, 10], F32) for _ in range(2)]
        UPS = [ar.alloc([128, 34], F32) for _ in range(2)]
        SFB = [ar.alloc([128, 128], F32) for _ in range(2)]
        FFO = [ar.alloc([128, 128], F32) for _ in range(2)]
        xb_list[0] = [6, 7]
        g0c = {}

        def c_info(n):
            j, half = n // 2, n % 2
            blk = j + 44 * half
            pb = 3 * half
            P = PS[:, pb : pb + 3, :].rearrange("p a b -> p (a b)")
            pkeys = [psk(pb), psk(pb + 1), psk(pb + 2)]
            return j, half, blk, pb, P, pkeys

        def c_pre(n):
            j, half, blk, pb, P, pkeys = c_info(n)
            bs = n % 2
            S.dma("sp", f"sfb{bs}", SFB[bs][0:32, :], sff[:, blk * 128 : (blk + 1) * 128], writes=[("SFB", bs)])

        def c_tr(n):
            bs = n % 2
            xb = xbank()
            TR(PS[:, xb, 0:32], SFB[bs][0:32, :], IDF[0:32, 0:32], [("SFB", bs), "CP"], [psk(xb)], signal=True)
            ACT(SX[bs][:, :, 0:2], PS[:, xb, 0:32].rearrange("p (s r) -> p s r", r=2), AF.Copy, [psk(xb)], [("SXh", bs)])

        def c_mm(n):
            j, half, blk, pb, P, pkeys = c_info(n)
            bs = n % 2
            if half == 0:
                g0c[j] = W.use(st_c[j])
            g0 = g0c[j]
            Wb = W.view(g0 + half, GRAN).rearrange("p (k n) -> p k n", n=128)
            for s_, (c0, c1) in enumerate(SEGC):
                w = c1 - c0
                mmgroup(PS[:, pb + s_, 0:w], 16, lambda k: Wb[:, k, :], lambda k: HT[:, k, c0:c1], HTALL + [("wr", g0 + half)], pb + s_)
            if half == 1:
                W.release(st_c[j])
            TS("dve", P[:, 0:2], P[:, 0:2], HM, None, ALU.mult, None, [psk(pb), "CP"], [psk(pb)])

        def c_post(n):
            j, half, blk, pb, P, pkeys = c_info(n)
            bs = n % 2
            ACT(UPS[bs][:, :].rearrange("p (i r) -> p i r", r=2),
                P[:, 1024 : 1024 + 136].rearrange("p (i r) -> p i r", r=8)[:, :, 0:2], AF.Copy, pkeys, [("UPS", bs)])
            sx = SX[bs]
            ACT(sx[:, :, 2:10], P[:, 1026:1154].rearrange("p (s t) -> p s t", t=8), AF.Copy, pkeys, [("SXn", bs)])
            xb = xbank()
            TR(PS[0:34, xb, 0:128], UPS[bs][:, :], IDF, [("UPS", bs), "CP"], [psk(xb)], signal=True)
            ACT(FFO[bs][0:34, :], PS[0:34, xb, 0:128], AF.Copy, [psk(xb)], [("FFO", bs)])
            S.dma("act", f"ffo{bs}a", o_ffp[:, blk * 128 : (blk + 1) * 128], FFO[bs][0:2, :], reads=[("FFO", bs)], writes=[("o_ffp", blk)])
            S.dma("act", f"ffo{bs}b", o_ffs[:, blk * 128 : (blk + 1) * 128], FFO[bs][2:34, :], reads=[("FFO", bs)], writes=[("o_ffs", blk)])
            cx = CX[half]
            ckm = ("CX", half, "m")
            cks = ("CX", half, "s")
            cxs = cx[:, 1024:1152].rearrange("p (s t) -> p s t", t=8)
            skeys = [("SXh", bs), ("SXn", bs), "CP"]
            TS("dve", cx[:, 0:1024], P[:, 2:1026], DWF[:, blk, 2:3], BDWF[:, blk : blk + 1], ALU.mult, ALU.add, pkeys + ["CP"], [ckm])
            TS("dve", cxs, sx[:, :, 2:10], DWF[:, blk, 2:3], BDWF[:, blk : blk + 1], ALU.mult, ALU.add, skeys, [cks])
            STT("dve", cx[:, 0:1024], P[:, 1:1025], DWF[:, blk, 1:2], cx[:, 0:1024], ALU.mult, ALU.add, pkeys + ["CP", ckm], [ckm])
            STT("dve", cxs, sx[:, :, 1:9], DWF[:, blk, 1:2], cxs, ALU.mult, ALU.add, skeys + [cks], [cks])
            STT("dve", cx[:, 0:1024], P[:, 0:1024], DWF[:, blk, 0:1], cx[:, 0:1024], ALU.mult, ALU.add, pkeys + ["CP", ckm], [ckm])
            STT("dve", cxs, sx[:, :, 0:8], DWF[:, blk, 0:1], cxs, ALU.mult, ALU.add, skeys + [cks], [cks])
            if half == 0:
                ACT(cx[:, :], cx[:, :], AF.Silu, [ckm, cks], [ckm, cks])
            else:
                TT("dve", G[:, j, :], CX[0][:, :], CX[1][:, :], ALU.mult,
                   [("CX", 0, "m"), ("CX", 0, "s"), ("CX", 1, "m"), ("CX", 1, "s")], [("G", j)])

        c_pre(0)
        for n in range(89):
            if n < 88:
                if n + 1 < 88:
                    c_pre(n + 1)
                c_tr(n)
                c_mm(n)
            if n >= 1:
                c_post(n - 1)

        stop_at('C')
        xb_list[0] = [4, 5, 6, 7]
        ar.off = mG
        S.barrier()
        X1R = [ar.alloc([128, 256], F32) for _ in range(4)]
        X2B = [ar.alloc([128, 256], F32) for _ in range(4)]
        GPB = [ar.alloc([128, 256], F32) for _ in range(2)]
        XGB = [ar.alloc([128, 256], BF16) for _ in range(3)]
        SQJ = ar.alloc([128, 256], BF16)
        PIN1 = ar.alloc([128, DPLE], F32)
        PB1 = ar.alloc([128, DPLE], BF16)
        GALL = [("G", j) for j in range(44)]

        def d_pload(t):
            S.dma("sp", "pin1", PIN1, pin[t * 128 : (t + 1) * 128, :], writes=["PIN1"])
            CPY("dve", PB1, PIN1, ["PIN1"], ["PB1"])

        def d_ptr(t):
            xb = xbank()
            pv = PS[:, xb, 0:128].bitcast(BF16)
            for k in range(2):
                TR(pv[:, k * 128 : (k + 1) * 128], PB1[:, k * 128 : (k + 1) * 128], IDB, ["PB1", "IDB"], [psk(xb)], signal=(k == 1))
            if t < 8:
                ACT(HT[:, 2 * t : 2 * t + 2, 0:128], pv.rearrange("p (k n) -> p k n", n=128), AF.Copy, [psk(xb)], [("HT", 0), ("PTH", t)])
            else:
                ACT(PT8[:, :, :], pv.rearrange("p (k n) -> p k n", n=128), AF.Copy, [psk(xb)], [("PTH", t)])

        cnt = 0
        dq = []

        def d_tr(n, cg, t):
            slot = n % 3
            xb = xbank()
            pv = PS[:, xb, 0:128].bitcast(BF16)
            for k in range(2):
                TR(pv[:, k * 128 : (k + 1) * 128], XGB[slot][:, k * 128 : (k + 1) * 128], IDB, [("XGB", slot), "IDB"], [psk(xb)], signal=(k == 1))
            ACT(HT[:, 2 * cg : 2 * cg + 2, (t + 1) * 128 : (t + 2) * 128], pv.rearrange("p (k n) -> p k n", n=128), AF.Copy,
                [psk(xb)], [("HT", t + 1)])

        for cg in range(8):
            S.dma("sp", f"gpb{cg % 2}", GPB[cg % 2], g_ple[cg * 256 : (cg + 1) * 256].partition_broadcast(128), writes=[("GPB", cg % 2)])
            g0s = [W.use(st_d[cg][kh]) for kh in range(2)]
            Wd = [W.view(g0s[kh], 22 * 256).rearrange("p (k n) -> p k n", n=256) for kh in range(2)]
            wkeys = [("wr", g0s[kh] + i) for kh in range(2) for i in range(3)]
            for t in range(9):
                r = cnt % 4
                cnt += 1
                row0 = (t + 1) * 128
                S.dma("sp", f"x1r{r}", X1R[r], X1[row0 : row0 + 128, cg * 256 : (cg + 1) * 256],
                      reads=[("X1", t + 1, cg // 2)], writes=[("X1R", r)])
                b = mbank()
                mmgroup(PS[:, b, 0:256], 44, lambda k: G[:, k, t * 128 : (t + 1) * 128],
                        lambda k: Wd[k // 22][:, k % 22, :], GALL + wkeys, b)
                TT("dve", X2B[r], PS[:, b, 0:256], X1R[r], ALU.add, [psk(b), ("X1R", r)], [("X2B", r)])
                S.dma("act", f"x2b{r}", X2[t * 128 : (t + 1) * 128, cg * 256 : (cg + 1) * 256], X2B[r],
                      reads=[("X2B", r)], writes=[("X2", t, cg)])
                col = 32 + t * 8 + cg
                ACT(SQJ, X2B[r], AF.Square, [("X2B", r)], ["SQJ", ("ss", col)], scale=float(D ** -0.5), accum_out=SSC[:, col : col + 1])
                n = cg * 9 + t
                TT("dve", XGB[n % 3], X2B[r], GPB[cg % 2], ALU.mult, [("X2B", r), ("GPB", cg % 2)], [("XGB", n % 3)])
                dq.append((n, cg, t))
                if len(dq) > 2:
                    d_tr(*dq.pop(0))
                if cg == 1:
                    if t >= 1:
                        d_ptr(t - 1)
                    d_pload(t)
                if cg == 2 and t == 0:
                    d_ptr(8)
            W.release(st_d[cg][0])
            W.release(st_d[cg][1])
        while dq:
            d_tr(*dq.pop(0))

        ar.off = static_end
        S.barrier()

        stop_at('D')
        X3 = ar.alloc([128, 9, D], F32)
        GBC = ar.alloc([128, D], F32)
        XS = [ar.alloc([128, D], BF16) for _ in range(1)] * 2
        WP = ar.alloc([128, 2, D], BF16)
        SG = [ar.alloc([128, 512], F32) for _ in range(1)] * 2
        TP = [ar.alloc([128, 512], F32) for _ in range(1)] * 2
        S.dma("pool", "wp", WP, w_pp.rearrange("(k p) n -> p k n", p=128), writes=["WP"])
        S.dma("sp", "gbc", GBC, g_fin.partition_broadcast(128), writes=["GBC"])
        for t in range(9):
            ACT(RSC[:, 120:128], SSC[:, 32 + t * 8 : 40 + t * 8], AF.Copy, [("ss", 32 + t * 8 + c) for c in range(8)],
                ["junk8", ("ss", 104 + t)], accum_out=SSC[:, 104 + t : 105 + t])
        ACT(RSC[:, 104:113], SSC[:, 104:113], AF.Ln, [("ss", 104 + t) for t in range(9)] + ["CP"], ["rs3"], bias=EPSC, scale=1.0)
        ACT(RSC[:, 104:113], RSC[:, 104:113], AF.Exp, ["rs3"], ["rs3"], scale=-0.5)
        for t in range(9):
            sl = t % 2
            S.dma("sp", f"x3l{t}", X3[:, t, :], X2[t * 128 : (t + 1) * 128, :], reads=[("X2", t, cg) for cg in range(8)], writes=[("X3", t)])
        cnt = 0
        for cg in range(4):
            g0 = W.use(st_e[cg])
            Wg = W.view(g0, 4 * GRAN).rearrange("p (k n) -> p k n", n=512)
            wkeys = [("wr", g0 + i) for i in range(4)]
            for t in range(9):
                r = cnt % 2
                cnt += 1
                hc = (t + 1) * 128
                b = mbank()
                mmgroup(PS[:, b, :], 16, lambda k: HT[:, k, hc : hc + 128], lambda k: Wg[:, k, :], [("HT", t + 1)] + wkeys, b)
                b2 = mbank()
                mmgroup(PS[:, b2, :], 2, lambda k: (HT[:, 2 * t + k, 0:128] if t < 8 else PT8[:, k, :]),
                        lambda k: WP[:, k, cg * 512 : (cg + 1) * 512], [("PTH", t), "WP"], b2)
                ACT(SG[r], PS[:, b, :], AF.Sigmoid, [psk(b), "rs3"], [("SG", 0)], scale=RSC[:, 104 + t : 105 + t])
                TT("dve", TP[r], SG[r], PS[:, b2, :], ALU.mult, [("SG", 0), psk(b2)], [("TP", 0)])
                x3s = X3[:, t, cg * 512 : (cg + 1) * 512]
                TT("dve", x3s, x3s, TP[r], ALU.add, [("X3", t), ("TP", 0)], [("X3", t)])
                if cg == 3:
                    col = 20 + t
                    ss = SSC[:, col : col + 1]
                    ACT(XS[t % 2], X3[:, t, :], AF.Square, [("X3", t)], [("XS", t % 2), ("ss", col)], scale=float(D ** -0.5), accum_out=ss)
            W.release(st_e[cg])
        for t in range(9):
            col = 20 + t
            ACT(RSC[:, col : col + 1], SSC[:, col : col + 1], AF.Ln, [("ss", col), "CP"], [("rs", col)], bias=EPSC, scale=1.0)
        for t in range(9):
            col = 20 + t
            ACT(RSC[:, col : col + 1], RSC[:, col : col + 1], AF.Exp, [("rs", col)], [("rs", col)], scale=-0.5)
        for t in range(9):
            col = 20 + t
            x3t = X3[:, t, :]
            STT("dve", x3t, x3t, RSC[:, col : col + 1], GBC, ALU.mult, ALU.mult, [("X3", t), ("rs", col), "GBC"], [("X3", t)])
            S.dma("sp", f"yo{t % 2}", y[t * 128 : (t + 1) * 128, :], x3t, reads=[("X3", t)], writes=[("y", t)])
    except _Stop:
        pass

    S.finish("sp")
    print("n semaphores", len(S.sems))

    with nc.Block() as block:
        @block.tensor
        def _(e):
            S.replay("pe", e)

        @block.scalar
        def _(e):
            S.replay("act", e)

        @block.vector
        def _(e):
            S.replay("dve", e)

        @block.gpsimd
        def _(e):
            S.replay("pool", e)

        @block.sync
        def _(e):
            S.replay("sp", e)

    es.close()
    return nc


def _cpack(inp, half):
    cp = np.zeros((128, NCP), np.float32)
    cp[:, C_ID : C_ID + 128] = np.eye(128, dtype=np.float32)
    cp[:, C_ONES : C_ONES + 128] = 1.0 / 128.0
    s = np.arange(128)
    cp[:, C_MT : C_MT + 128] = (s[:, None] <= s[None, :]).astype(np.float32)
    cp[:, C_BDM : C_BDM + 128] = ((s[:, None] // 8 == s[None, :] // 8) & (s[:, None] % 8 <= s[None, :] % 8)).astype(np.float32)
    cp[0:8, C_E : C_E + 128] = (s[None, :] % 8 == np.arange(8)[:, None]).astype(np.float32)
    cp[:, C_DWA : C_DWA + 248] = inp["w_dw_a"][0].reshape(31, 8, 128).transpose(2, 1, 0).reshape(128, 248)

    def v8(v):
        return v.reshape(8, 128).T

    cp[:, C_BDWA : C_BDWA + 8] = v8(inp["b_dw_a"][0])
    cp[:, C_GLNA : C_GLNA + 8] = v8(inp["g_ln_a"][0])
    cp[:, C_BLNA : C_BLNA + 8] = v8(inp["b_ln_a"][0])
    cp[:, C_GLNV : C_GLNV + 8] = v8(inp["g_ln_v"][0])
    cp[:, C_BLNV : C_BLNV + 8] = v8(inp["b_ln_v"][0])
    cp[:, C_DWF : C_DWF + 264] = inp["w_dw_f"][0].reshape(3, 88, 128).transpose(2, 1, 0).reshape(128, 264)
    cp[:, C_BDWF : C_BDWF + 88] = inp["b_dw_f"][0].reshape(88, 128).T
    cp[:, C_HM] = float(half)
    cp[:, C_EPS] = EPS
    return cp


_NC_CACHE = {}


def kernel(**inp):
    inp = {k: np.asarray(v) for k, v in inp.items()}
    xp, xs = inp["x_prompt"], inp["x_sample"]
    pp, ps = inp["p_prompt"][0], inp["p_sample"][0]
    sca_all, sff_all = inp["state_conv_a"][0], inp["state_ffn_conv"][0]
    f32 = np.float32

    shared = {
        "g_mix": np.ascontiguousarray(inp["g_mix"][0], f32),
        "g_ffn": np.ascontiguousarray(inp["g_ffn"][0], f32),
        "g_ple": np.ascontiguousarray(inp["g_ple"][0], f32),
        "g_fin": np.ascontiguousarray(inp["g_final"], f32),
        "bs_p": np.ascontiguousarray(inp["b_s"][0].reshape(1024), f32),
        "bs_s": np.ascontiguousarray(np.tile(inp["b_s"][0][:, :8], (1, 16)).reshape(1024), f32),
        "w_s": np.ascontiguousarray(inp["w_s"][0], f32),
        "w_in": np.ascontiguousarray(inp["w_in"][0], f32),
        "w_out": np.ascontiguousarray(inp["w_out"][0], f32),
        "w_up": np.ascontiguousarray(inp["w_up"][0], f32),
        "w_down": np.ascontiguousarray(inp["w_down"][0], f32),
        "w_pg": np.ascontiguousarray(inp["w_ple_gate"][0], f32),
        "w_pp": np.ascontiguousarray(inp["w_ple_proj"][0], f32),
    }
    in_maps = []
    for c in range(8):
        b, half = c // 2, c % 2
        halo = xp[b, 896:1024] if half else np.zeros((128, D), f32)
        xin = np.concatenate([halo, xp[b, half * 1024 : (half + 1) * 1024], xs[c * 16 : (c + 1) * 16].reshape(128, D)], 0)
        pin = np.concatenate([pp[b, half * 1024 : (half + 1) * 1024], ps[c * 16 : (c + 1) * 16].reshape(128, DPLE)], 0)
        m = dict(shared)
        m["xin"] = np.ascontiguousarray(xin, f32)
        m["pin"] = np.ascontiguousarray(pin, f32)
        m["sca"] = np.ascontiguousarray(sca_all[c * 16 : (c + 1) * 16].reshape(480, CA), f32)
        m["sff"] = np.ascontiguousarray(sff_all[c * 16 : (c + 1) * 16].reshape(32, 2 * DFF), f32)
        m["cpack"] = _cpack(inp, half)
        in_maps.append(m)

    if "nc" not in _NC_CACHE:
        _NC_CACHE["nc"] = build_program()
    nc = _NC_CACHE["nc"]
    res = run_bass_kernel_spmd(nc, in_maps, core_ids=list(range(8)))
    R = res.results

    y_prompt = np.zeros((4, 2048, D), f32)
    y_sample = np.zeros((128, 8, D), f32)
    conv_a_p = np.zeros((1, 4, 30, CA), f32)
    conv_a_s = np.zeros((1, 128, 30, CA), f32)
    ffn_p = np.zeros((1, 4, 2, 2 * DFF), f32)
    ffn_s = np.zeros((1, 128, 2, 2 * DFF), f32)
    cv_p = np.zeros((1, 4, 128, CA), f32)
    cv_s = np.zeros((1, 128, 8, CA), f32)
    for c in range(8):
        b, half = c // 2, c % 2
        r = R[c]
        y_prompt[b, half * 1024 : (half + 1) * 1024] = r["y"][:1024]
        y_sample[c * 16 : (c + 1) * 16] = r["y"][1024:].reshape(16, 8, D)
        conv_a_s[0, c * 16 : (c + 1) * 16] = r["o_cas"]
        ffn_s[0, c * 16 : (c + 1) * 16] = r["o_ffs"].reshape(16, 2, 2 * DFF)
        cv_s[0, c * 16 : (c + 1) * 16] = r["o_cvs"].reshape(16, 8, CA)
        if half == 1:
            conv_a_p[0, b] = r["o_cap"]
            ffn_p[0, b] = r["o_ffp"]
            cv_p[0, b] = r["o_cvp"]
    return (y_prompt, y_sample, conv_a_p, conv_a_s, ffn_p, ffn_s, cv_p, cv_s)
```

```python
import numpy as np
from contextlib import ExitStack

import concourse.bass as bass
import concourse.mybir as mybir
from concourse.bass_utils import run_bass_kernel_spmd

F32 = mybir.dt.float32
BF16 = mybir.dt.bfloat16
AF = mybir.ActivationFunctionType
ALU = mybir.AluOpType

EPS = 1e-6
D = 2048
CA = 1024
DFF = 5632
DPLE = 256
NCOL = 1280
NT = 10
GRAN = 2048
NGRAN = 12
NPOOL_TAPS = 0

C_ID, C_ONES, C_MT, C_BDM, C_E = 0, 128, 256, 384, 512
C_DWA = 640
C_BDWA = C_DWA + 248
C_GLNA = C_BDWA + 8
C_BLNA = C_GLNA + 8
C_GLNV = C_BLNA + 8
C_BLNV = C_GLNV + 8
C_DWF = C_BLNV + 8
C_BDWF = C_DWF + 264
C_HM = C_BDWF + 88
C_EPS = C_HM + 1
NCP = C_EPS + 1 + 6


class Sched:
    ENG = ("pe", "act", "dve", "pool")
    SELF_DIST = 2

    def __init__(self, nc, es):
        self.nc = nc
        self.es = es
        self.engs = {"pe": nc.tensor, "act": nc.scalar, "dve": nc.vector, "pool": nc.gpsimd, "sp": nc.sync}
        self.prog = {k: [] for k in self.engs}
        self.sems = {}
        self.cnt = {}
        for k in self.ENG:
            self.sems[k] = es.enter_context(nc.semaphore("s_" + k))
        self.nops = {k: 0 for k in self.ENG}
        self.need = {k: set() for k in self.ENG}
        self.lastw = {}
        self.readers = {}
        self.waited = {k: {} for k in self.engs}
        self.base = {}
        self.phase = 0
        self.kphase = {}

    def _mksem(self, name):
        self.sems[name] = self.es.enter_context(self.nc.semaphore("s_" + name))
        self.cnt[name] = 0

    def _deps(self, reads, writes):
        deps = {}

        def add(s, v):
            if deps.get(s, -1) < v:
                deps[s] = v

        for k in reads:
            if k in self.lastw:
                add(*self.lastw[k])
        for k in writes:
            if k in self.lastw:
                add(*self.lastw[k])
            for s, v in self.readers.get(k, {}).items():
                add(s, v)
            if not self._static(k) and self.kphase.get(k, 0) != self.phase:
                self.kphase[k] = self.phase
                for s, v in self.base.items():
                    add(s, v)
        return deps

    STATIC = ("wr", "ps", "HT", "ss", "rs", "X1", "X2", "y", "CP", "IDB", "ONESB")

    def _static(self, k):
        kind = k[0] if isinstance(k, tuple) else k
        return kind in self.STATIC or kind.startswith("o_")

    def _waits(self, e, deps):
        for s, v in deps.items():
            if s in self.ENG:
                if s == e:
                    if e == "pe" or self.nops[e] - v >= self.SELF_DIST:
                        continue
                if self.waited[e].get(s, -1) >= v:
                    continue
                self.waited[e][s] = v
                self.need[s].add(v)
                self.prog[e].append(("we", s, v))
            else:
                if self.waited[e].get(s, 0) >= v:
                    continue
                self.waited[e][s] = v
                self.prog[e].append(("wd", s, v))

    def _record(self, rec, reads, writes):
        s, v = rec
        for k in reads:
            r = self.readers.setdefault(k, {})
            if r.get(s, -1) < v:
                r[s] = v
        for k in writes:
            self.lastw[k] = rec
            self.readers[k] = {}

    def op(self, e, fn, reads=(), writes=(), signal=True):
        self._waits(e, self._deps(reads, writes))
        idx = self.nops[e]
        self.nops[e] += 1
        self.prog[e].append(("op", fn, idx))
        self._record((e, idx), reads, writes)

    def dma(self, q, sem, out, in_, reads=(), writes=()):
        if sem not in self.sems:
            self._mksem(sem)
        self._waits(q, self._deps(reads, writes))
        self.cnt[sem] += 16
        v = self.cnt[sem]
        self.prog[q].append(("dma", out, in_, sem))
        self._record((sem, v), reads, writes)

    def _all(self):
        d = {s: v for s, v in self.cnt.items() if v > 0}
        for e in self.ENG:
            if self.nops[e] > 0:
                d[e] = self.nops[e] - 1
        return d

    def barrier(self):
        self.base = {s: v for s, v in self._all().items() if not s.startswith("wr")}
        self.phase += 1

    def finish(self, e="sp"):
        self._waits(e, self._all())

    def replay(self, e, eng):
        rank = {}
        for k in self.ENG:
            rank[k] = {idx: i + 1 for i, idx in enumerate(sorted(self.need[k]))}
        for item in self.prog[e]:
            if item[0] == "op":
                ins = item[1](eng)
                if item[2] in self.need[e]:
                    ins.then_inc(self.sems[e], 1)
            elif item[0] == "we":
                eng.wait_ge(self.sems[item[1]], rank[item[1]][item[2]])
            elif item[0] == "wd":
                eng.wait_ge(self.sems[item[1]], item[2])
            else:
                eng.dma_start(out=item[1], in_=item[2]).then_inc(self.sems[item[3]], 16)


class Arena:
    def __init__(self, nc, nbytes):
        self.t = nc.alloc_sbuf_tensor("arena", [128, nbytes // 4], F32)
        self.nbytes = nbytes
        self.off = 0

    def alloc(self, shape, dtype):
        n = int(np.prod(shape[1:]))
        esz = 4 if dtype == F32 else 2
        nb = (n * esz + 31) // 32 * 32
        assert self.off + nb <= self.nbytes, f"arena overflow {self.off + nb} > {self.nbytes}"
        ap = self.t[: shape[0], self.off // 4 : (self.off + nb) // 4]
        if dtype == BF16:
            ap = ap.bitcast(BF16)
        ap = ap[:, :n]
        self.off += nb
        if len(shape) == 3:
            ap = ap.rearrange("p (a b) -> p a b", b=shape[2])
        elif len(shape) == 4:
            ap = ap.rearrange("p (a b c) -> p a b c", b=shape[2], c=shape[3])
        return ap


class WStream:
    def __init__(self, S, wr):
        self.S = S
        self.wr = wr
        self.steps = []
        self.next = 0
        self.released = set()

    def add(self, n, emit):
        self.steps.append({"n": n, "emit": emit})
        return len(self.steps) - 1

    def plan(self):
        cur = 0
        occ = {}
        for i, st in enumerate(self.steps):
            if cur + st["n"] > NGRAN:
                cur = 0
            st["g0"] = cur
            prev = set()
            for g in range(cur, cur + st["n"]):
                if g in occ:
                    prev.add(occ[g])
                occ[g] = i
            st["prev"] = prev
            cur += st["n"]

    def pump(self):
        while self.next < len(self.steps) and self.steps[self.next]["prev"] <= self.released:
            st = self.steps[self.next]
            st["emit"](st["g0"])
            self.next += 1

    def use(self, i):
        self.pump()
        assert self.next > i, f"weight step {i} not loaded (next={self.next})"
        return self.steps[i]["g0"]

    def release(self, i):
        self.released.add(i)
        self.pump()

    def view(self, g0, nelem):
        return self.wr[:, g0 * GRAN : g0 * GRAN + nelem]


def build_program():
    nc = bass.Bass("TRN2", target_bir_lowering=False)

    def din(name, shape):
        return nc.dram_tensor(name, shape, F32, kind="ExternalInput").ap()

    def dout(name, shape):
        return nc.dram_tensor(name, shape, F32, kind="ExternalOutput").ap()

    xin = din("xin", [NCOL, D])
    pin = din("pin", [1152, DPLE])
    sca = din("sca", [480, CA])
    sff = din("sff", [32, 2 * DFF])
    cpack = din("cpack", [128, NCP])
    g_mix = din("g_mix", [D])
    g_ffn = din("g_ffn", [D])
    g_ple = din("g_ple", [D])
    g_fin = din("g_fin", [D])
    bs_p = din("bs_p", [1024])
    bs_s = din("bs_s", [1024])
    w_s = din("w_s", [8, 128, 128])
    w_in = din("w_in", [D, 4096])
    w_out = din("w_out", [D, D])
    w_up = din("w_up", [D, 2 * DFF])
    w_down = din("w_down", [DFF, D])
    w_pg = din("w_pg", [D, D])
    w_pp = din("w_pp", [DPLE, D])

    y = dout("y", [1152, D])
    o_cap = dout("o_cap", [30, CA])
    o_cas = dout("o_cas", [16, 30, CA])
    o_ffp = dout("o_ffp", [2, 2 * DFF])
    o_ffs = dout("o_ffs", [32, 2 * DFF])
    o_cvp = dout("o_cvp", [128, CA])
    o_cvs = dout("o_cvs", [128, CA])

    X1 = nc.dram_tensor("x1_scratch", [NCOL, D], F32).ap()
    X2 = nc.dram_tensor("x2_scratch", [1152, D], F32).ap()

    PS = nc.alloc_psum_tensor("PS", [128, 8, 512], F32)

    es = ExitStack()
    S = Sched(nc, es)

    total = nc.sbuf_bytes_remaining
    ar = Arena(nc, (total - 256) // 32 * 32)

    CP = ar.alloc([128, NCP], F32)
    IDF = CP[:, C_ID : C_ID + 128]
    ONESF = CP[:, C_ONES : C_ONES + 128]
    MT = CP[:, C_MT : C_MT + 128]
    BDM = CP[:, C_BDM : C_BDM + 128]
    EM = CP[0:8, C_E : C_E + 128]
    DWA = CP[:, C_DWA : C_DWA + 248].rearrange("p (j k) -> p j k", k=31)
    BDWA = CP[:, C_BDWA : C_BDWA + 8]
    GLNA = CP[:, C_GLNA : C_GLNA + 8]
    BLNA = CP[:, C_BLNA : C_BLNA + 8]
    GLNV = CP[:, C_GLNV : C_GLNV + 8]
    BLNV = CP[:, C_BLNV : C_BLNV + 8]
    DWF = CP[:, C_DWF : C_DWF + 264].rearrange("p (j k) -> p j k", k=3)
    BDWF = CP[:, C_BDWF : C_BDWF + 88]
    HM = CP[:, C_HM : C_HM + 1]
    EPSC = CP[:, C_EPS : C_EPS + 1]
    IDB = ar.alloc([128, 128], BF16)
    ONESB = ar.alloc([128, 128], BF16)
    SSC = ar.alloc([128, 128], F32)
    RSC = ar.alloc([128, 128], F32)
    HT = ar.alloc([128, 16, NCOL], BF16)
    PT8 = ar.alloc([128, 2, 128], BF16)
    WR = ar.alloc([128, NGRAN * GRAN], BF16)
    static_end = ar.off

    W = WStream(S, WR)

    mb_state = [0]
    xb_state = [0]

    def mbank():
        b = mb_state[0]
        mb_state[0] = (b + 1) % 4
        return b

    xb_list = [[4, 5, 6, 7]]
    xb2_state = [0]

    def xbank():
        lst = xb_list[0]
        b = lst[xb_state[0] % len(lst)]
        xb_state[0] += 1
        return b

    def xbank2():
        b = 4 + 2 * (xb2_state[0] % 2)
        xb2_state[0] += 1
        return b

    def psk(b):
        return ("ps", b)

    def ACT(out, in_, func, reads, writes, **kw):
        S.op("act", lambda e: e.activation(out=out, in_=in_, func=func, **kw), reads, writes)

    def TT(eng, out, in0, in1, op, reads, writes):
        S.op(eng, lambda e: e.tensor_tensor(out=out, in0=in0, in1=in1, op=op), reads, writes)

    def TS(eng, out, in0, s1, s2, op0, op1, reads, writes):
        if s2 is None:
            S.op(eng, lambda e: e.tensor_scalar(out=out, in0=in0, scalar1=s1, scalar2=None, op0=op0), reads, writes)
        else:
            S.op(eng, lambda e: e.tensor_scalar(out=out, in0=in0, scalar1=s1, scalar2=s2, op0=op0, op1=op1), reads, writes)

    def STT(eng, out, in0, sc, in1, op0, op1, reads, writes):
        S.op(eng, lambda e: e.scalar_tensor_tensor(out=out, in0=in0, scalar=sc, in1=in1, op0=op0, op1=op1), reads, writes)

    def PSTT(out, in0, sc, in1, tmp, tkey, reads, writes):
        TS("pool", tmp, in0, sc, None, ALU.mult, None, reads, [tkey])
        TT("pool", out, tmp, in1, ALU.add, reads + [tkey], writes)

    def CPY(eng, out, in_, reads, writes):
        S.op(eng, lambda e: e.tensor_copy(out=out, in_=in_), reads, writes)

    def MM(out, lhsT, rhs, start, stop, reads, writes, signal):
        S.op("pe", lambda e: e.matmul(out, lhsT=lhsT, rhs=rhs, start=start, stop=stop), reads, writes, signal=signal)

    def TR(out, in_, ident, reads, writes, signal):
        S.op("pe", lambda e: e.transpose(out=out, in_=in_, identity=ident), reads, writes, signal=signal)

    def mmgroup(out, K, lhs_fn, rhs_fn, reads, bank):
        for k in range(K):
            MM(out, lhs_fn(k), rhs_fn(k), k == 0, k == K - 1, reads, [psk(bank)], signal=(k == K - 1))

    import os
    STOP = os.environ.get("KSTOP", "")

    class _Stop(Exception):
        pass

    DUMP = os.environ.get("KDUMP", "").split(",")
    dbg_outs = []

    def dbg_dump(label, ap, reads):
        if label not in DUMP:
            return
        shp = [int(x) for x in ap.shape]
        dt = nc.dram_tensor("dbg_" + label, shp, ap.dtype, kind="ExternalOutput").ap()
        S.dma("sp", "dbg_" + label, dt, ap, reads=reads, writes=["o_dbg_" + label])

    def stop_at(name):
        if STOP == name:
            raise _Stop()

    try:
        S.dma("sp", "cp", CP, cpack[:, :], writes=["CP"])
        CPY("dve", IDB, IDF, ["CP"], ["IDB"])
        CPY("dve", ONESB, ONESF, ["CP"], ["ONESB"])
        TT("dve", ONESF, IDF, ONESF, ALU.subtract, ["CP", "ONESB"], ["CP"])

        def wload(dst, src, keys):
            S.dma("pool", f"wr{keys[0][1]}", dst, src, writes=keys)

        def mk_in(c0):
            def emit(g0):
                wload(W.view(g0, GRAN).rearrange("p (k n) -> p k n", n=128),
                      w_in[:, c0 : c0 + 128].rearrange("(k p) n -> p k n", p=128), [("wr", g0)])
            return emit

        def mk_in_pair(c0, c1):
            def emit(g0):
                wload(W.view(g0, GRAN).rearrange("p (k n) -> p k n", n=128),
                      w_in[:, c0 : c0 + 128].rearrange("(k p) n -> p k n", p=128), [("wr", g0)])
                wload(W.view(g0 + 1, GRAN).rearrange("p (k n) -> p k n", n=128),
                      w_in[:, c1 : c1 + 128].rearrange("(k p) n -> p k n", p=128), [("wr", g0 + 1)])
            return emit

        def mk_sq(wt, cg):
            def emit(g0):
                wload(W.view(g0, 4 * GRAN).rearrange("p (k n) -> p k n", n=512),
                      wt[:, cg * 512 : (cg + 1) * 512].rearrange("(k p) n -> p k n", p=128),
                      [("wr", g0 + i) for i in range(4)])
            return emit

        def mk_up_pair(j):
            def emit(g0):
                for i, c0 in enumerate((j * 128, DFF + j * 128)):
                    wload(W.view(g0 + i, GRAN).rearrange("p (k n) -> p k n", n=128),
                          w_up[:, c0 : c0 + 128].rearrange("(k p) n -> p k n", p=128), [("wr", g0 + i)])
            return emit

        def mk_down(cg, kh):
            def emit(g0):
                wload(W.view(g0, 22 * 256).rearrange("p (k n) -> p k n", n=256),
                      w_down[kh * 2816 : (kh + 1) * 2816, cg * 256 : (cg + 1) * 256].rearrange("(k p) n -> p k n", p=128),
                      [("wr", g0 + i) for i in range(3)])
            return emit

        st_a = []
        for j in range(8):
            sa = W.add(2, mk_in_pair(j * 128, 1024 + j * 128))
            su = W.add(1, mk_in(2048 + j * 128))
            sv = W.add(1, mk_in(3072 + j * 128))
            st_a.append((sa, su, sv))
        st_b = [W.add(4, mk_sq(w_out, cg)) for cg in range(4)]
        st_c = [W.add(2, mk_up_pair(j)) for j in range(44)]
        st_d = [[W.add(3, mk_down(cg, kh)) for kh in range(2)] for cg in range(8)]
        st_e = [W.add(4, mk_sq(w_pg, cg)) for cg in range(4)]
        W.plan()
        W.pump()

        SEGA = [(0, 512), (512, 1024), (1024, 1280)]
        SEGC = [(126, 638), (638, 1150), (1150, 1280)]

        def norm_a(XTt, XSt, GBC, col, xt_key, xs_key):
            ss = SSC[:, col : col + 1]
            rs = RSC[:, col : col + 1]
            ACT(XSt, XTt, AF.Square, [xt_key], [xs_key, ("ss", col)], scale=float(D ** -0.5), accum_out=ss)
            ACT(rs, ss, AF.Ln, [("ss", col), "CP"], [("rs", col)], bias=EPSC, scale=1.0)
            ACT(rs, rs, AF.Exp, [("rs", col)], [("rs", col)], scale=-0.5)
            STT("dve", XSt, XTt, rs, GBC, ALU.mult, ALU.mult, [xt_key, ("rs", col), "GBC"], [xs_key])

        def norm_b(XSt, xs_key, htcol):
            b = xbank2()
            pv = PS[:, b : b + 2, :].rearrange("p a b -> p (a b)").bitcast(BF16)
            for k in range(16):
                TR(pv[:, k * 128 : (k + 1) * 128], XSt[:, k * 128 : (k + 1) * 128], IDB,
                   [xs_key, "IDB"], [psk(b), psk(b + 1)], signal=(k == 15))
            ACT(HT[:, :, htcol : htcol + 128], pv.rearrange("p (k n) -> p k n", n=128), AF.Copy,
                [psk(b), psk(b + 1)], [("HT", htcol // 128)])

        m0 = ar.off
        HTALL0 = [('HT', i) for i in range(NT)]
        XT = [ar.alloc([128, D], F32) for _ in range(4)]
        XS = [ar.alloc([128, D], BF16) for _ in range(4)]
        GBC = ar.alloc([128, D], F32)
        S.dma("sp", "gbc", GBC, g_mix.partition_broadcast(128), writes=["GBC"])
        for i in range(NT + 2):
            if i < NT:
                sl = i % 4
                S.dma("sp", f"xt{sl}", XT[sl], xin[i * 128 : (i + 1) * 128, :], writes=[("XT", sl)])
                norm_a(XT[sl], XS[sl], GBC, i, ("XT", sl), ("XS", sl))
            if i >= 2:
                norm_b(XS[(i - 2) % 4], ("XS", (i - 2) % 4), (i - 2) * 128)
        dbg_dump('HT0', HT[:, :, :], HTALL0)
        ar.off = m0
        S.barrier()
        stop_at('0')

        MIXT = ar.alloc([128, 16, NCOL], BF16)
        mA = ar.off
        WTt = ar.alloc([128, 8, 128], BF16)
        WSt = ar.alloc([128, 8, 128], BF16)
        mA2 = ar.off
        W8 = ar.alloc([128, 8, 8], F32)
        WSF = ar.alloc([128, 8, 128], F32)
        C8 = ar.alloc([128, 128], F32)
        S.dma("sp", "wsf", WSF, w_s.rearrange("h t s -> t h s"), writes=["WSF"])
        for h in range(8):
            b = xbank()
            TR(PS[:, b, 0:128], WSF[:, h, :], IDF, ["WSF", "CP"], [psk(b)], signal=True)
            TT("dve", WTt[:, h, :], PS[:, b, 0:128], MT, ALU.mult, [psk(b), "CP"], [("WT", h)])
        S.dma("sp", "w8", W8[0:8, :, :], w_s[:, 0:8, 0:8].rearrange("h t s -> t h s"), writes=["W8"])
        for h in range(8):
            b = xbank()
            MM(PS[0:8, b, 0:128], W8[0:8, h, :], EM, True, True, ["W8", "CP"], [psk(b)], signal=True)
            CPY("dve", C8[0:8, :], PS[0:8, b, 0:128], [psk(b)], ["C8"])
            b2 = xbank()
            MM(PS[:, b2, 0:128], EM, C8[0:8, :], True, True, ["C8", "CP"], [psk(b2)], signal=True)
            TT("dve", WSt[:, h, :], PS[:, b2, 0:128], BDM, ALU.mult, [psk(b2), "CP"], [("WS", h)])

        ar.off = mA2
        S.barrier()
        stop_at('prep')
        BSH = [ar.alloc([128, 2, 128], F32) for _ in range(2)]
        Abuf = [ar.alloc([128, NCOL], F32) for _ in range(2)]
        ASX = [ar.alloc([128, 16, 38], F32) for _ in range(2)]
        SCAT = [ar.alloc([128, 4, 128], F32) for _ in range(2)]
        SIG = ar.alloc([128, NCOL], F32)
        CV = [ar.alloc([128, 1154], F32) for _ in range(2)]
        SQ = ar.alloc([128, NCOL], BF16)
        SQ2 = ar.alloc([128, NCOL], BF16)
        GV = ar.alloc([128, NCOL], F32)
        VNH = [ar.alloc([128, 10, 128], BF16) for _ in range(2)]
        CVO = [ar.alloc([128, 2, 128], F32) for _ in range(2)]
        CAS = ar.alloc([128, CA], F32)
        CAPt = ar.alloc([128, CA], F32)
        TMPB = [ar.alloc([128, 512], F32) for _ in range(1)]
        PTMP3 = ar.alloc([128, 16, 8], F32)
        CVB = ar.alloc([128, 1026], F32)
        print('phaseA arena', ar.off, ar.nbytes)
        S.op("pool", lambda e: e.memset(MIXT[:, 0:8, 0:126], 0.0), [], [("MIXa", j) for j in range(8)])

        S.dma("sp", "casd", o_cas[:, 0:22, :], sca.rearrange("(s r) c -> s r c", r=30)[:, 8:30, :], writes=["o_cas_hist"])

        HTALL = [("HT", i) for i in range(NT)]
        CVSEG_A = [(0, 512), (512, 1024), (1024, 1154)]

        def chain_ln(buf, segs, keys, SQx, sqn):
            for (c0, c1) in segs:
                w = c1 - c0
                b = xbank()
                MM(PS[:, b, 0:w], ONESF, buf[:, c0:c1], True, True, keys + ["CP"], [psk(b)], signal=True)
                ACT(SQx[:, c0:c1], PS[:, b, 0:w], AF.Square, [psk(b)], [(sqn, c0)])
                ACT(buf[:, c0:c1], PS[:, b, 0:w], AF.Copy, [psk(b)] + keys, keys)
                yield
                b2 = xbank()
                MM(PS[:, b2, 0:w], ONESB, SQx[:, c0:c1], True, True, [(sqn, c0), "ONESB"], [psk(b2)], signal=True)
                ACT(PS[:, b2, 0:w], PS[:, b2, 0:w], AF.Ln, [psk(b2), "CP"], [psk(b2)], bias=EPSC, scale=1.0)
                ACT(PS[:, b2, 0:w], PS[:, b2, 0:w], AF.Exp, [psk(b2)], [psk(b2)], scale=-0.5)
                TT("dve", buf[:, c0:c1], buf[:, c0:c1], PS[:, b2, 0:w], ALU.mult, keys + [psk(b2)], keys)
                yield

        def a_main(j):
            sa, su, sv = st_a[j]
            sl = j % 2
            A = Abuf[sl]
            akey = ("A", sl)
            g0 = W.use(sa)
            Wv = W.view(g0, GRAN).rearrange("p (k n) -> p k n", n=128)
            Wg = W.view(g0 + 1, GRAN).rearrange("p (k n) -> p k n", n=128)
            for s_, (c0, c1) in enumerate(SEGA):
                w = c1 - c0
                bv = mbank()
                mmgroup(PS[:, bv, 0:w], 16, lambda k: Wv[:, k, :], lambda k: HT[:, k, c0:c1], HTALL + [("wr", g0)], bv)
                bg = mbank()
                mmgroup(PS[:, bg, 0:w], 16, lambda k: Wg[:, k, :], lambda k: HT[:, k, c0:c1], HTALL + [("wr", g0 + 1)], bg)
                ACT(SIG[:, c0:c1], PS[:, bg, 0:w], AF.Sigmoid, [psk(bg)], [("SIG", s_)])
                ACT(A[:, c0:c1], PS[:, bv, 0:w], AF.Copy, [psk(bv)], [akey])
                yield
                yield
            W.release(sa)
            TT("dve", A[:, :], A[:, :], SIG[:, :], ALU.mult, [akey] + [("SIG", s_) for s_ in range(3)], [akey])
            g0 = W.use(su)
            Wu = W.view(g0, GRAN).rearrange("p (k n) -> p k n", n=128)
            for s_, (c0, c1) in enumerate(SEGA):
                w = c1 - c0
                bu = mbank()
                mmgroup(PS[:, bu, 0:w], 16, lambda k: Wu[:, k, :], lambda k: HT[:, k, c0:c1], HTALL + [("wr", g0)], bu)
                ACT(MIXT[:, 8 + j, c0:c1], PS[:, bu, 0:w], AF.Gelu_apprx_tanh, [psk(bu)], [("MIXb", j, s_)])
                yield
            W.release(su)
            g0 = W.use(sv)
            Wvv = W.view(g0, GRAN).rearrange("p (k n) -> p k n", n=128)
            for s_, (c0, c1) in enumerate(SEGA):
                w = c1 - c0
                bvv = mbank()
                mmgroup(PS[:, bvv, 0:w], 16, lambda k: Wvv[:, k, :], lambda k: HT[:, k, c0:c1], HTALL + [("wr", g0)], bvv)
                ACT(GV[:, c0:c1], PS[:, bvv, 0:w], AF.Gelu_apprx_tanh, [psk(bvv)], ["GV"])
                yield
            W.release(sv)

        def a_conv(j):
            sl = j % 2
            A = Abuf[sl]
            akey = ("A", sl)
            S.dma("sp", f"scat{sl}", SCAT[sl][0:120, :, :],
                  sca[:, j * 128 : (j + 1) * 128].rearrange("(t r) c -> r t c", r=120), writes=[("SCAT", sl)])
            b = xbank()
            for t in range(4):
                TR(PS[:, b, t * 120 : (t + 1) * 120], SCAT[sl][0:120, t, :], IDF[0:120, 0:120],
                   [("SCAT", sl), "CP"], [psk(b)], signal=(t == 3))
            ACT(ASX[sl][:, :, 0:30], PS[:, b, 0:480].rearrange("p (s r) -> p s r", r=30), AF.Copy, [psk(b)], [("ASXh", sl)])
            CPY("dve", ASX[sl][:, :, 30:38], A[:, 1152:1280].rearrange("p (s t) -> p s t", t=8), [akey], [("ASXn", sl)])
            b = xbank()
            TR(PS[0:30, b, 0:128], A[:, 1122:1152], IDF, [akey, "CP"], [psk(b)], signal=True)
            ACT(CAPt[0:30, j * 128 : (j + 1) * 128], PS[0:30, b, 0:128], AF.Copy, [psk(b)], [("CAP", j)])
            b = xbank()
            TR(PS[:, b, 0:128], A[:, 1152:1280], IDF, [akey, "CP"], [psk(b)], signal=True)
            ACT(CAS[:, j * 128 : (j + 1) * 128], PS[:, b, 0:128], AF.Copy, [psk(b)], [("CAS", j)])
            cv = CV[sl]
            ckm = ("CV", sl, "m")
            cks = ("CV", sl, "s")
            cvm = cv[:, 0:1026]
            cvs = cv[:, 1026:1154].rearrange("p (s t) -> p s t", t=8)
            ax = ASX[sl]
            skeys = [("ASXn", sl), ("ASXh", sl), "CP"]
            ACT(cvm, A[:, 126:1152], AF.Identity, [akey, "CP"], [ckm], scale=DWA[:, j, 30:31], bias=BDWA[:, j : j + 1])
            TS("dve", cvs, ax[:, :, 30:38], DWA[:, j, 30:31], BDWA[:, j : j + 1], ALU.mult, ALU.add, skeys, [cks])
            ACT(CVB, A[:, 96 + 29 : 96 + 29 + 1026], AF.Copy, [akey, "CP"], ["CVB"], scale=DWA[:, j, 29:30])
            STT("dve", cvs, ax[:, :, 29:37], DWA[:, j, 29:30], cvs, ALU.mult, ALU.add, skeys + [cks], [cks])
            yield
            for k in range(29):
                if k % 2 == 0:
                    STT("dve", cvm, A[:, 96 + k : 96 + k + 1026], DWA[:, j, k : k + 1], cvm, ALU.mult, ALU.add, [akey, "CP", ckm], [ckm])
                else:
                    STT("dve", CVB, A[:, 96 + k : 96 + k + 1026], DWA[:, j, k : k + 1], CVB, ALU.mult, ALU.add, [akey, "CP", "CVB"], ["CVB"])
                STT("dve", cvs, ax[:, :, k : k + 8], DWA[:, j, k : k + 1], cvs, ALU.mult, ALU.add, skeys + [cks], [cks])
                if k % 3 == 2:
                    yield
            TT("dve", cvm, cvm, CVB, ALU.add, [ckm, "CVB"], [ckm])
            yield

        def chain_a(j):
            sl = j % 2
            cv = CV[sl]
            ckm = ("CV", sl, "m")
            cks = ("CV", sl, "s")
            yield from chain_ln(cv, CVSEG_A, [ckm, cks], SQ, 'SQa')
            ACT(MIXT[:, j, 126:1280], cv[:, 0:1154], AF.Silu, [ckm, cks, "CP"], [("MIXa", j)],
                scale=GLNA[:, j : j + 1], bias=BLNA[:, j : j + 1])
            yield

        def chain_v(j):
            yield from chain_ln(GV, SEGA, ["GV"], SQ2, "SQv")
            ACT(GV[:, :], GV[:, :], AF.Identity, ["GV", "CP"], ["GV"], scale=GLNV[:, j : j + 1], bias=BLNV[:, j : j + 1])
            vs = j % 2
            vnh = VNH[vs]
            S.dma("sp", f"bsh{vs}a", BSH[vs][:, 0, :], bs_p[j * 128 : (j + 1) * 128].partition_broadcast(128), writes=[("BSH", vs, 0)])
            S.dma("sp", f"bsh{vs}b", BSH[vs][:, 1, :], bs_s[j * 128 : (j + 1) * 128].partition_broadcast(128), writes=[("BSH", vs, 1)])
            for (i0, i1) in ((0, 4), (4, 8), (8, 10)):
                n = i1 - i0
                b = xbank()
                for i in range(i0, i1):
                    TR(PS[:, b, (i - i0) * 128 : (i - i0 + 1) * 128], GV[:, i * 128 : (i + 1) * 128], IDF,
                       ["GV", "CP"], [psk(b)], signal=(i == i1 - 1))
                ACT(vnh[:, i0:i1, :], PS[:, b, 0 : n * 128].rearrange("p (a c) -> p a c", c=128), AF.Copy,
                    [psk(b)], [("VNH", vs, i0)])
                if i0 == 8:
                    ACT(CVO[vs][:, :, :], PS[:, b, 0:256].rearrange("p (a c) -> p a c", c=128), AF.Copy, [psk(b)], [("CVO", vs)])
                    S.dma("sp", f"cvo{vs}a", o_cvp[:, j * 128 : (j + 1) * 128], CVO[vs][:, 0, :], reads=[("CVO", vs)], writes=[("o_cvp", j)])
                    S.dma("sp", f"cvo{vs}b", o_cvs[:, j * 128 : (j + 1) * 128], CVO[vs][:, 1, :], reads=[("CVO", vs)], writes=[("o_cvs", j)])
                yield
            for gi, (i0, i1) in enumerate(((0, 4), (4, 8), (8, 10))):
                n = i1 - i0
                b = xbank()
                for i in range(i0, i1):
                    rhs = WTt[:, j, :] if i < 9 else WSt[:, j, :]
                    MM(PS[:, b, (i - i0) * 128 : (i - i0 + 1) * 128], vnh[:, i, :], rhs, True, True,
                       [("VNH", vs, i0), ("WT", j), ("WS", j)], [psk(b)], signal=(i == i1 - 1))
                tb = TMPB[0]
                tkey = ("TMPB", 0)
                if i0 < 8:
                    TT("dve", tb[:, 0 : n * 128].rearrange("p (a c) -> p a c", c=128),
                       PS[:, b, 0 : n * 128].rearrange("p (a c) -> p a c", c=128),
                       BSH[vs][:, 0:1, :].to_broadcast([128, n, 128]), ALU.add, [psk(b), ("BSH", vs, 0)], [tkey])
                else:
                    TT("dve", tb[:, 0:256].rearrange("p (a c) -> p a c", c=128),
                       PS[:, b, 0:256].rearrange("p (a c) -> p a c", c=128), BSH[vs][:, :, :], ALU.add,
                       [psk(b), ("BSH", vs, 0), ("BSH", vs, 1)], [tkey])
                mk = [("MIXb", j, s_) for s_ in range(3)]
                TT("dve", MIXT[:, 8 + j, i0 * 128 : i1 * 128], tb[:, 0 : n * 128], MIXT[:, 8 + j, i0 * 128 : i1 * 128],
                   ALU.mult, [tkey] + mk, mk)
                yield

        def advance(gens):
            for g in list(gens):
                try:
                    next(g)
                except StopIteration:
                    gens.remove(g)

        pending = []
        for it in range(8 + 2):
            if 1 <= it <= 8:
                pending.append(a_conv(it - 1))
                pending.append(chain_v(it - 1))
            if 2 <= it <= 9:
                pending.append(chain_a(it - 2))
            if it < 8:
                for _ in a_main(it):
                    advance(pending)
            while pending:
                advance(pending)

        stop_at('A10')
        S.dma("sp", "capo", o_cap[:, :], CAPt[0:30, :], reads=[("CAP", j) for j in range(8)], writes=["o_cap"])
        for s in range(16):
            S.dma("sp", f"caso{s % 4}", o_cas[s, 22:30, :], CAS[s * 8 : (s + 1) * 8, :],
                  reads=[("CAS", j) for j in range(8)], writes=[("o_cas", s)])

        dbg_dump('MIXT', MIXT[:, :, :], [('MIXa', j) for j in range(8)] + [('MIXb', j, s) for j in range(8) for s in range(3)])
        ar.off = mA
        S.barrier()
        stop_at('A')

        XB = [ar.alloc([128, 512], F32) for _ in range(3)]
        X1B = [ar.alloc([128, 512], F32) for _ in range(3)]
        XT = [ar.alloc([128, D], F32) for _ in range(3)]
        XS = [ar.alloc([128, D], BF16) for _ in range(6)]
        GBC = ar.alloc([128, D], F32)
        S.dma("sp", "gbc", GBC, g_ffn.partition_broadcast(128), writes=["GBC"])
        MIXALL = [("MIXa", j) for j in range(8)] + [("MIXb", j, s) for j in range(8) for s in range(3)]
        cnt = 0

        def phaseB_norm_a(i):
            sl = i % 3
            S.dma("sp", f"xt{sl}", XT[sl], X1[i * 128 : (i + 1) * 128, :], reads=[("X1", i, cg) for cg in range(4)], writes=[("XT", sl)])
            norm_a(XT[sl], XS[i % 6], GBC, 10 + i, ("XT", sl), ("XS", i % 6))

        def phaseB_norm_b(i):
            norm_b(XS[i % 6], ("XS", i % 6), i * 128)

        for cg in range(4):
            g0 = W.use(st_b[cg])
            Wo = W.view(g0, 4 * GRAN).rearrange("p (k n) -> p k n", n=512)
            wkeys = [("wr", g0 + i) for i in range(4)]
            for i in range(NT):
                r = cnt % 3
                cnt += 1
                S.dma("sp", f"xb{r}", XB[r], xin[i * 128 : (i + 1) * 128, cg * 512 : (cg + 1) * 512], writes=[("XB", r)])
                b = mbank()
                mmgroup(PS[:, b, :], 16, lambda k: MIXT[:, k, i * 128 : (i + 1) * 128], lambda k: Wo[:, k, :], MIXALL + wkeys, b)
                TT("dve", X1B[r], PS[:, b, :], XB[r], ALU.add, [psk(b), ("XB", r)], [("X1B", r)])
                S.dma("act", f"x1b{r}", X1[i * 128 : (i + 1) * 128, cg * 512 : (cg + 1) * 512], X1B[r],
                      reads=[("X1B", r)], writes=[("X1", i, cg)])
                if cg == 3 and i >= 1:
                    phaseB_norm_a(i - 1)
                if cg == 3 and i >= 5:
                    phaseB_norm_b(i - 5)
            W.release(st_b[cg])
        phaseB_norm_a(NT - 1)
        for i in range(NT - 5, NT):
            phaseB_norm_b(i)

        ar.off = static_end
        S.barrier()

        stop_at('B')
        G = ar.alloc([128, 44, 1152], BF16)
        mG = ar.off
        CX = [ar.alloc([128, 1152], F32) for _ in range(2)]
        SX = [ar.alloc([128, 16, 10], F32) for _ in range(2)]
        UPS = [ar.alloc([128, 34], F32) for _ in range(2)]
        SFB = [ar.alloc([128, 128], F32) for _ in range(2)]
        FFO = [ar.alloc([128, 128], F32) for _ in range(2)]
        xb_list[0] = [6, 7]
        g0c = {}

        def c_info(n):
            j, half = n // 2, n % 2
            blk = j + 44 * half
            pb = 3 * half
            P = PS[:, pb : pb + 3, :].rearrange("p a b -> p (a b)")
            pkeys = [psk(pb), psk(pb + 1), psk(pb + 2)]
            return j, half, blk, pb, P, pkeys

        def c_pre(n):
            j, half, blk, pb, P, pkeys = c_info(n)
            bs = n % 2
            S.dma("sp", f"sfb{bs}", SFB[bs][0:32, :], sff[:, blk * 128 : (blk + 1) * 128], writes=[("SFB", bs)])

        def c_tr(n):
            bs = n % 2
            xb = xbank()
            TR(PS[:, xb, 0:32], SFB[bs][0:32, :], IDF[0:32, 0:32], [("SFB", bs), "CP"], [psk(xb)], signal=True)
            ACT(SX[bs][:, :, 0:2], PS[:, xb, 0:32].rearrange("p (s r) -> p s r", r=2), AF.Copy, [psk(xb)], [("SXh", bs)])

        def c_mm(n):
            j, half, blk, pb, P, pkeys = c_info(n)
            bs = n % 2
            if half == 0:
                g0c[j] = W.use(st_c[j])
            g0 = g0c[j]
            Wb = W.view(g0 + half, GRAN).rearrange("p (k n) -> p k n", n=128)
            for s_, (c0, c1) in enumerate(SEGC):
                w = c1 - c0
                mmgroup(PS[:, pb + s_, 0:w], 16, lambda k: Wb[:, k, :], lambda k: HT[:, k, c0:c1], HTALL + [("wr", g0 + half)], pb + s_)
            if half == 1:
                W.release(st_c[j])
            TS("dve", P[:, 0:2], P[:, 0:2], HM, None, ALU.mult, None, [psk(pb), "CP"], [psk(pb)])

        def c_post(n):
            j, half, blk, pb, P, pkeys = c_info(n)
            bs = n % 2
            ACT(UPS[bs][:, :].rearrange("p (i r) -> p i r", r=2),
                P[:, 1024 : 1024 + 136].rearrange("p (i r) -> p i r", r=8)[:, :, 0:2], AF.Copy, pkeys, [("UPS", bs)])
            sx = SX[bs]
            ACT(sx[:, :, 2:10], P[:, 1026:1154].rearrange("p (s t) -> p s t", t=8), AF.Copy, pkeys, [("SXn", bs)])
            xb = xbank()
            TR(PS[0:34, xb, 0:128], UPS[bs][:, :], IDF, [("UPS", bs), "CP"], [psk(xb)], signal=True)
            ACT(FFO[bs][0:34, :], PS[0:34, xb, 0:128], AF.Copy, [psk(xb)], [("FFO", bs)])
            S.dma("act", f"ffo{bs}a", o_ffp[:, blk * 128 : (blk + 1) * 128], FFO[bs][0:2, :], reads=[("FFO", bs)], writes=[("o_ffp", blk)])
            S.dma("act", f"ffo{bs}b", o_ffs[:, blk * 128 : (blk + 1) * 128], FFO[bs][2:34, :], reads=[("FFO", bs)], writes=[("o_ffs", blk)])
            cx = CX[half]
            ckm = ("CX", half, "m")
            cks = ("CX", half, "s")
            cxs = cx[:, 1024:1152].rearrange("p (s t) -> p s t", t=8)
            skeys = [("SXh", bs), ("SXn", bs), "CP"]
            TS("dve", cx[:, 0:1024], P[:, 2:1026], DWF[:, blk, 2:3], BDWF[:, blk : blk + 1], ALU.mult, ALU.add, pkeys + ["CP"], [ckm])
            TS("dve", cxs, sx[:, :, 2:10], DWF[:, blk, 2:3], BDWF[:, blk : blk + 1], ALU.mult, ALU.add, skeys, [cks])
            STT("dve", cx[:, 0:1024], P[:, 1:1025], DWF[:, blk, 1:2], cx[:, 0:1024], ALU.mult, ALU.add, pkeys + ["CP", ckm], [ckm])
            STT("dve", cxs, sx[:, :, 1:9], DWF[:, blk, 1:2], cxs, ALU.mult, ALU.add, skeys + [cks], [cks])
            STT("dve", cx[:, 0:1024], P[:, 0:1024], DWF[:, blk, 0:1], cx[:, 0:1024], ALU.mult, ALU.add, pkeys + ["CP", ckm], [ckm])
            STT("dve", cxs, sx[:, :, 0:8], DWF[:, blk, 0:1], cxs, ALU.mult, ALU.add, skeys + [cks], [cks])
            if half == 0:
                ACT(cx[:, :], cx[:, :], AF.Silu, [ckm, cks], [ckm, cks])
            else:
                TT("dve", G[:, j, :], CX[0][:, :], CX[1][:, :], ALU.mult,
                   [("CX", 0, "m"), ("CX", 0, "s"), ("CX", 1, "m"), ("CX", 1, "s")], [("G", j)])

        c_pre(0)
        for n in range(89):
            if n < 88:
                if n + 1 < 88:
                    c_pre(n + 1)
                c_tr(n)
                c_mm(n)
            if n >= 1:
                c_post(n - 1)

        stop_at('C')
        xb_list[0] = [4, 5, 6, 7]
        ar.off = mG
        S.barrier()
        X1R = [ar.alloc([128, 256], F32) for _ in range(4)]
        X2B = [ar.alloc([128, 256], F32) for _ in range(4)]
        GPB = [ar.alloc([128, 256], F32) for _ in range(2)]
        XGB = [ar.alloc([128, 256], BF16) for _ in range(3)]
        SQJ = ar.alloc([128, 256], BF16)
        PIN1 = ar.alloc([128, DPLE], F32)
        PB1 = ar.alloc([128, DPLE], BF16)
        GALL = [("G", j) for j in range(44)]

        def d_pload(t):
            S.dma("sp", "pin1", PIN1, pin[t * 128 : (t + 1) * 128, :], writes=["PIN1"])
            CPY("dve", PB1, PIN1, ["PIN1"], ["PB1"])

        def d_ptr(t):
            xb = xbank()
            pv = PS[:, xb, 0:128].bitcast(BF16)
            for k in range(2):
                TR(pv[:, k * 128 : (k + 1) * 128], PB1[:, k * 128 : (k + 1) * 128], IDB, ["PB1", "IDB"], [psk(xb)], signal=(k == 1))
            if t < 8:
                ACT(HT[:, 2 * t : 2 * t + 2, 0:128], pv.rearrange("p (k n) -> p k n", n=128), AF.Copy, [psk(xb)], [("HT", 0), ("PTH", t)])
            else:
                ACT(PT8[:, :, :], pv.rearrange("p (k n) -> p k n", n=128), AF.Copy, [psk(xb)], [("PTH", t)])

        cnt = 0
        dq = []

        def d_tr(n, cg, t):
            slot = n % 3
            xb = xbank()
            pv = PS[:, xb, 0:128].bitcast(BF16)
            for k in range(2):
                TR(pv[:, k * 128 : (k + 1) * 128], XGB[slot][:, k * 128 : (k + 1) * 128], IDB, [("XGB", slot), "IDB"], [psk(xb)], signal=(k == 1))
            ACT(HT[:, 2 * cg : 2 * cg + 2, (t + 1) * 128 : (t + 2) * 128], pv.rearrange("p (k n) -> p k n", n=128), AF.Copy,
                [psk(xb)], [("HT", t + 1)])

        for cg in range(8):
            S.dma("sp", f"gpb{cg % 2}", GPB[cg % 2], g_ple[cg * 256 : (cg + 1) * 256].partition_broadcast(128), writes=[("GPB", cg % 2)])
            g0s = [W.use(st_d[cg][kh]) for kh in range(2)]
            Wd = [W.view(g0s[kh], 22 * 256).rearrange("p (k n) -> p k n", n=256) for kh in range(2)]
            wkeys = [("wr", g0s[kh] + i) for kh in range(2) for i in range(3)]
            for t in range(9):
                r = cnt % 4
                cnt += 1
                row0 = (t + 1) * 128
                S.dma("sp", f"x1r{r}", X1R[r], X1[row0 : row0 + 128, cg * 256 : (cg + 1) * 256],
                      reads=[("X1", t + 1, cg // 2)], writes=[("X1R", r)])
                b = mbank()
                mmgroup(PS[:, b, 0:256], 44, lambda k: G[:, k, t * 128 : (t + 1) * 128],
                        lambda k: Wd[k // 22][:, k % 22, :], GALL + wkeys, b)
                TT("dve", X2B[r], PS[:, b, 0:256], X1R[r], ALU.add, [psk(b), ("X1R", r)], [("X2B", r)])
                S.dma("act", f"x2b{r}", X2[t * 128 : (t + 1) * 128, cg * 256 : (cg + 1) * 256], X2B[r],
                      reads=[("X2B", r)], writes=[("X2", t, cg)])
                col = 32 + t * 8 + cg
                ACT(SQJ, X2B[r], AF.Square, [("X2B", r)], ["SQJ", ("ss", col)], scale=float(D ** -0.5), accum_out=SSC[:, col : col + 1])
                n = cg * 9 + t
                TT("dve", XGB[n % 3], X2B[r], GPB[cg % 2], ALU.mult, [("X2B", r), ("GPB", cg % 2)], [("XGB", n % 3)])
                dq.append((n, cg, t))
                if len(dq) > 2:
                    d_tr(*dq.pop(0))
                if cg == 1:
                    if t >= 1:
                        d_ptr(t - 1)
                    d_pload(t)
                if cg == 2 and t == 0:
                    d_ptr(8)
            W.release(st_d[cg][0])
            W.release(st_d[cg][1])
        while dq:
            d_tr(*dq.pop(0))

        ar.off = static_end
        S.barrier()

        stop_at('D')
        X3 = ar.alloc([128, 9, D], F32)
        GBC = ar.alloc([128, D], F32)
        XS = [ar.alloc([128, D], BF16) for _ in range(1)] * 2
        WP = ar.alloc([128, 2, D], BF16)
        SG = [ar.alloc([128, 512], F32) for _ in range(1)] * 2
        TP = [ar.alloc([128, 512], F32) for _ in range(1)] * 2
        S.dma("pool", "wp", WP, w_pp.rearrange("(k p) n -> p k n", p=128), writes=["WP"])
        S.dma("sp", "gbc", GBC, g_fin.partition_broadcast(128), writes=["GBC"])
        for t in range(9):
            ACT(RSC[:, 120:128], SSC[:, 32 + t * 8 : 40 + t * 8], AF.Copy, [("ss", 32 + t * 8 + c) for c in range(8)],
                ["junk8", ("ss", 104 + t)], accum_out=SSC[:, 104 + t : 105 + t])
        ACT(RSC[:, 104:113], SSC[:, 104:113], AF.Ln, [("ss", 104 + t) for t in range(9)] + ["CP"], ["rs3"], bias=EPSC, scale=1.0)
        ACT(RSC[:, 104:113], RSC[:, 104:113], AF.Exp, ["rs3"], ["rs3"], scale=-0.5)
        def e_x3(t):
            S.dma("sp", f"x3l{t}", X3[:, t, :], X2[t * 128 : (t + 1) * 128, :], reads=[("X2", t, cg) for cg in range(8)], writes=[("X3", t)])

        e_x3(0)
        e_x3(1)
        cnt = 0
        for cg in range(4):
            g0 = W.use(st_e[cg])
            Wg = W.view(g0, 4 * GRAN).rearrange("p (k n) -> p k n", n=512)
            wkeys = [("wr", g0 + i) for i in range(4)]
            for t in range(9):
                if cg == 0 and t + 2 <= 8:
                    e_x3(t + 2)
                r = cnt % 2
                cnt += 1
                hc = (t + 1) * 128
                b = mbank()
                mmgroup(PS[:, b, :], 16, lambda k: HT[:, k, hc : hc + 128], lambda k: Wg[:, k, :], [("HT", t + 1)] + wkeys, b)
                b2 = mbank()
                mmgroup(PS[:, b2, :], 2, lambda k: (HT[:, 2 * t + k, 0:128] if t < 8 else PT8[:, k, :]),
                        lambda k: WP[:, k, cg * 512 : (cg + 1) * 512], [("PTH", t), "WP"], b2)
                ACT(SG[r], PS[:, b, :], AF.Sigmoid, [psk(b), "rs3"], [("SG", 0)], scale=RSC[:, 104 + t : 105 + t])
                TT("dve", TP[r], SG[r], PS[:, b2, :], ALU.mult, [("SG", 0), psk(b2)], [("TP", 0)])
                x3s = X3[:, t, cg * 512 : (cg + 1) * 512]
                TT("dve", x3s, x3s, TP[r], ALU.add, [("X3", t), ("TP", 0)], [("X3", t)])
                if cg == 3:
                    col = 20 + t
                    ss = SSC[:, col : col + 1]
                    ACT(XS[t % 2], X3[:, t, :], AF.Square, [("X3", t)], [("XS", t % 2), ("ss", col)], scale=float(D ** -0.5), accum_out=ss)
            W.release(st_e[cg])
        for t in range(9):
            col = 20 + t
            ACT(RSC[:, col : col + 1], SSC[:, col : col + 1], AF.Ln, [("ss", col), "CP"], [("rs", col)], bias=EPSC, scale=1.0)
        for t in range(9):
            col = 20 + t
            ACT(RSC[:, col : col + 1], RSC[:, col : col + 1], AF.Exp, [("rs", col)], [("rs", col)], scale=-0.5)
        for t in range(9):
            col = 20 + t
            x3t = X3[:, t, :]
            STT("dve", x3t, x3t, RSC[:, col : col + 1], GBC, ALU.mult, ALU.mult, [("X3", t), ("rs", col), "GBC"], [("X3", t)])
            S.dma("sp", f"yo{t % 2}", y[t * 128 : (t + 1) * 128, :], x3t, reads=[("X3", t)], writes=[("y", t)])
    except _Stop:
        pass

    S.finish("sp")
    print("n semaphores", len(S.sems))

    with nc.Block() as block:
        @block.tensor
        def _(e):
            S.replay("pe", e)

        @block.scalar
        def _(e):
            S.replay("act", e)

        @block.vector
        def _(e):
            S.replay("dve", e)

        @block.gpsimd
        def _(e):
            S.replay("pool", e)

        @block.sync
        def _(e):
            S.replay("sp", e)

    es.close()
    return nc


def _cpack(inp, half):
    cp = np.zeros((128, NCP), np.float32)
    cp[:, C_ID : C_ID + 128] = np.eye(128, dtype=np.float32)
    cp[:, C_ONES : C_ONES + 128] = 1.0 / 128.0
    s = np.arange(128)
    cp[:, C_MT : C_MT + 128] = (s[:, None] <= s[None, :]).astype(np.float32)
    cp[:, C_BDM : C_BDM + 128] = ((s[:, None] // 8 == s[None, :] // 8) & (s[:, None] % 8 <= s[None, :] % 8)).astype(np.float32)
    cp[0:8, C_E : C_E + 128] = (s[None, :] % 8 == np.arange(8)[:, None]).astype(np.float32)
    cp[:, C_DWA : C_DWA + 248] = inp["w_dw_a"][0].reshape(31, 8, 128).transpose(2, 1, 0).reshape(128, 248)

    def v8(v):
        return v.reshape(8, 128).T

    cp[:, C_BDWA : C_BDWA + 8] = v8(inp["b_dw_a"][0])
    cp[:, C_GLNA : C_GLNA + 8] = v8(inp["g_ln_a"][0])
    cp[:, C_BLNA : C_BLNA + 8] = v8(inp["b_ln_a"][0])
    cp[:, C_GLNV : C_GLNV + 8] = v8(inp["g_ln_v"][0])
    cp[:, C_BLNV : C_BLNV + 8] = v8(inp["b_ln_v"][0])
    cp[:, C_DWF : C_DWF + 264] = inp["w_dw_f"][0].reshape(3, 88, 128).transpose(2, 1, 0).reshape(128, 264)
    cp[:, C_BDWF : C_BDWF + 88] = inp["b_dw_f"][0].reshape(88, 128).T
    cp[:, C_HM] = float(half)
    cp[:, C_EPS] = EPS
    return cp


_NC_CACHE = {}


def kernel(**inp):
    inp = {k: np.asarray(v) for k, v in inp.items()}
    xp, xs = inp["x_prompt"], inp["x_sample"]
    pp, ps = inp["p_prompt"][0], inp["p_sample"][0]
    sca_all, sff_all = inp["state_conv_a"][0], inp["state_ffn_conv"][0]
    f32 = np.float32

    shared = {
        "g_mix": np.ascontiguousarray(inp["g_mix"][0], f32),
        "g_ffn": np.ascontiguousarray(inp["g_ffn"][0], f32),
        "g_ple": np.ascontiguousarray(inp["g_ple"][0], f32),
        "g_fin": np.ascontiguousarray(inp["g_final"], f32),
        "bs_p": np.ascontiguousarray(inp["b_s"][0].reshape(1024), f32),
        "bs_s": np.ascontiguousarray(np.tile(inp["b_s"][0][:, :8], (1, 16)).reshape(1024), f32),
        "w_s": np.ascontiguousarray(inp["w_s"][0], f32),
        "w_in": np.ascontiguousarray(inp["w_in"][0], f32),
        "w_out": np.ascontiguousarray(inp["w_out"][0], f32),
        "w_up": np.ascontiguousarray(inp["w_up"][0], f32),
        "w_down": np.ascontiguousarray(inp["w_down"][0], f32),
        "w_pg": np.ascontiguousarray(inp["w_ple_gate"][0], f32),
        "w_pp": np.ascontiguousarray(inp["w_ple_proj"][0], f32),
    }
    in_maps = []
    for c in range(8):
        b, half = c // 2, c % 2
        halo = xp[b, 896:1024] if half else np.zeros((128, D), f32)
        xin = np.concatenate([halo, xp[b, half * 1024 : (half + 1) * 1024], xs[c * 16 : (c + 1) * 16].reshape(128, D)], 0)
        pin = np.concatenate([pp[b, half * 1024 : (half + 1) * 1024], ps[c * 16 : (c + 1) * 16].reshape(128, DPLE)], 0)
        m = dict(shared)
        m["xin"] = np.ascontiguousarray(xin, f32)
        m["pin"] = np.ascontiguousarray(pin, f32)
        m["sca"] = np.ascontiguousarray(sca_all[c * 16 : (c + 1) * 16].reshape(480, CA), f32)
        m["sff"] = np.ascontiguousarray(sff_all[c * 16 : (c + 1) * 16].reshape(32, 2 * DFF), f32)
        m["cpack"] = _cpack(inp, half)
        in_maps.append(m)

    if "nc" not in _NC_CACHE:
        _NC_CACHE["nc"] = build_program()
    nc = _NC_CACHE["nc"]
    res = run_bass_kernel_spmd(nc, in_maps, core_ids=list(range(8)))
    R = res.results

    y_prompt = np.zeros((4, 2048, D), f32)
    y_sample = np.zeros((128, 8, D), f32)
    conv_a_p = np.zeros((1, 4, 30, CA), f32)
    conv_a_s = np.zeros((1, 128, 30, CA), f32)
    ffn_p = np.zeros((1, 4, 2, 2 * DFF), f32)
    ffn_s = np.zeros((1, 128, 2, 2 * DFF), f32)
    cv_p = np.zeros((1, 4, 128, CA), f32)
    cv_s = np.zeros((1, 128, 8, CA), f32)
    for c in range(8):
        b, half = c // 2, c % 2
        r = R[c]
        y_prompt[b, half * 1024 : (half + 1) * 1024] = r["y"][:1024]
        y_sample[c * 16 : (c + 1) * 16] = r["y"][1024:].reshape(16, 8, D)
        conv_a_s[0, c * 16 : (c + 1) * 16] = r["o_cas"]
        ffn_s[0, c * 16 : (c + 1) * 16] = r["o_ffs"].reshape(16, 2, 2 * DFF)
        cv_s[0, c * 16 : (c + 1) * 16] = r["o_cvs"].reshape(16, 8, CA)
        if half == 1:
            conv_a_p[0, b] = r["o_cap"]
            ffn_p[0, b] = r["o_ffp"]
            cv_p[0, b] = r["o_cvp"]
    return (y_prompt, y_sample, conv_a_p, conv_a_s, ffn_p, ffn_s, cv_p, cv_s)
```

```python
import numpy as np
from contextlib import ExitStack

import concourse.bass as bass
import concourse.mybir as mybir
from concourse.bass_utils import run_bass_kernel_spmd

F32 = mybir.dt.float32
BF16 = mybir.dt.bfloat16
AF = mybir.ActivationFunctionType
ALU = mybir.AluOpType

EPS = 1e-6
D = 2048
CA = 1024
DFF = 5632
DPLE = 256
NCOL = 1280
NT = 10
GRAN = 2048
NGRAN = 12
NPOOL_TAPS = 0

C_ID, C_ONES, C_MT, C_BDM, C_E = 0, 128, 256, 384, 512
C_DWA = 640
C_BDWA = C_DWA + 248
C_GLNA = C_BDWA + 8
C_BLNA = C_GLNA + 8
C_GLNV = C_BLNA + 8
C_BLNV = C_GLNV + 8
C_DWF = C_BLNV + 8
C_BDWF = C_DWF + 264
C_HM = C_BDWF + 88
C_EPS = C_HM + 1
NCP = C_EPS + 1 + 6


class Sched:
    ENG = ("pe", "act", "dve", "pool")
    SELF_DIST = 2

    def __init__(self, nc, es):
        self.nc = nc
        self.es = es
        self.engs = {"pe": nc.tensor, "act": nc.scalar, "dve": nc.vector, "pool": nc.gpsimd, "sp": nc.sync}
        self.prog = {k: [] for k in self.engs}
        self.sems = {}
        self.cnt = {}
        for k in self.ENG:
            self.sems[k] = es.enter_context(nc.semaphore("s_" + k))
        self.nops = {k: 0 for k in self.ENG}
        self.need = {k: set() for k in self.ENG}
        self.lastw = {}
        self.readers = {}
        self.waited = {k: {} for k in self.engs}
        self.base = {}
        self.phase = 0
        self.kphase = {}

    def _mksem(self, name):
        self.sems[name] = self.es.enter_context(self.nc.semaphore("s_" + name))
        self.cnt[name] = 0

    def _deps(self, reads, writes):
        deps = {}

        def add(s, v):
            if deps.get(s, -1) < v:
                deps[s] = v

        for k in reads:
            if k in self.lastw:
                add(*self.lastw[k])
        for k in writes:
            if k in self.lastw:
                add(*self.lastw[k])
            for s, v in self.readers.get(k, {}).items():
                add(s, v)
            if not self._static(k) and self.kphase.get(k, 0) != self.phase:
                self.kphase[k] = self.phase
                for s, v in self.base.items():
                    add(s, v)
        return deps

    STATIC = ("wr", "ps", "HT", "ss", "rs", "X1", "X2", "y", "CP", "IDB", "ONESB")

    def _static(self, k):
        kind = k[0] if isinstance(k, tuple) else k
        return kind in self.STATIC or kind.startswith("o_")

    def _waits(self, e, deps):
        for s, v in deps.items():
            if s in self.ENG:
                if s == e:
                    if e == "pe" or self.nops[e] - v >= self.SELF_DIST:
                        continue
                if self.waited[e].get(s, -1) >= v:
                    continue
                self.waited[e][s] = v
                self.need[s].add(v)
                self.prog[e].append(("we", s, v))
            else:
                if self.waited[e].get(s, 0) >= v:
                    continue
                self.waited[e][s] = v
                self.prog[e].append(("wd", s, v))

    def _record(self, rec, reads, writes):
        s, v = rec
        for k in reads:
            r = self.readers.setdefault(k, {})
            if r.get(s, -1) < v:
                r[s] = v
        for k in writes:
            self.lastw[k] = rec
            self.readers[k] = {}

    def op(self, e, fn, reads=(), writes=(), signal=True):
        self._waits(e, self._deps(reads, writes))
        idx = self.nops[e]
        self.nops[e] += 1
        self.prog[e].append(("op", fn, idx))
        self._record((e, idx), reads, writes)

    def dma(self, q, sem, out, in_, reads=(), writes=()):
        if sem not in self.sems:
            self._mksem(sem)
        self._waits(q, self._deps(reads, writes))
        self.cnt[sem] += 16
        v = self.cnt[sem]
        self.prog[q].append(("dma", out, in_, sem))
        self._record((sem, v), reads, writes)

    def _all(self):
        d = {s: v for s, v in self.cnt.items() if v > 0}
        for e in self.ENG:
            if self.nops[e] > 0:
                d[e] = self.nops[e] - 1
        return d

    def barrier(self):
        self.base = {s: v for s, v in self._all().items() if not s.startswith("wr")}
        self.phase += 1

    def finish(self, e="sp"):
        self._waits(e, self._all())

    def replay(self, e, eng):
        rank = {}
        for k in self.ENG:
            rank[k] = {idx: i + 1 for i, idx in enumerate(sorted(self.need[k]))}
        for item in self.prog[e]:
            if item[0] == "op":
                ins = item[1](eng)
                if item[2] in self.need[e]:
                    ins.then_inc(self.sems[e], 1)
            elif item[0] == "we":
                eng.wait_ge(self.sems[item[1]], rank[item[1]][item[2]])
            elif item[0] == "wd":
                eng.wait_ge(self.sems[item[1]], item[2])
            else:
                eng.dma_start(out=item[1], in_=item[2]).then_inc(self.sems[item[3]], 16)


class Arena:
    def __init__(self, nc, nbytes):
        self.t = nc.alloc_sbuf_tensor("arena", [128, nbytes // 4], F32)
        self.nbytes = nbytes
        self.off = 0

    def alloc(self, shape, dtype):
        n = int(np.prod(shape[1:]))
        esz = 4 if dtype == F32 else 2
        nb = (n * esz + 31) // 32 * 32
        assert self.off + nb <= self.nbytes, f"arena overflow {self.off + nb} > {self.nbytes}"
        ap = self.t[: shape[0], self.off // 4 : (self.off + nb) // 4]
        if dtype == BF16:
            ap = ap.bitcast(BF16)
        ap = ap[:, :n]
        self.off += nb
        if len(shape) == 3:
            ap = ap.rearrange("p (a b) -> p a b", b=shape[2])
        elif len(shape) == 4:
            ap = ap.rearrange("p (a b c) -> p a b c", b=shape[2], c=shape[3])
        return ap


class WStream:
    def __init__(self, S, wr):
        self.S = S
        self.wr = wr
        self.steps = []
        self.next = 0
        self.released = set()

    def add(self, n, emit):
        self.steps.append({"n": n, "emit": emit})
        return len(self.steps) - 1

    def plan(self):
        cur = 0
        occ = {}
        for i, st in enumerate(self.steps):
            if cur + st["n"] > NGRAN:
                cur = 0
            st["g0"] = cur
            prev = set()
            for g in range(cur, cur + st["n"]):
                if g in occ:
                    prev.add(occ[g])
                occ[g] = i
            st["prev"] = prev
            cur += st["n"]

    def pump(self):
        while self.next < len(self.steps) and self.steps[self.next]["prev"] <= self.released:
            st = self.steps[self.next]
            st["emit"](st["g0"])
            self.next += 1

    def use(self, i):
        self.pump()
        assert self.next > i, f"weight step {i} not loaded (next={self.next})"
        return self.steps[i]["g0"]

    def release(self, i):
        self.released.add(i)
        self.pump()

    def view(self, g0, nelem):
        return self.wr[:, g0 * GRAN : g0 * GRAN + nelem]


def build_program():
    nc = bass.Bass("TRN2", target_bir_lowering=False)

    def din(name, shape):
        return nc.dram_tensor(name, shape, F32, kind="ExternalInput").ap()

    def dout(name, shape):
        return nc.dram_tensor(name, shape, F32, kind="ExternalOutput").ap()

    xin = din("xin", [NCOL, D])
    pin = din("pin", [1152, DPLE])
    sca = din("sca", [480, CA])
    sff = din("sff", [32, 2 * DFF])
    cpack = din("cpack", [128, NCP])
    g_mix = din("g_mix", [D])
    g_ffn = din("g_ffn", [D])
    g_ple = din("g_ple", [D])
    g_fin = din("g_fin", [D])
    bs_p = din("bs_p", [1024])
    bs_s = din("bs_s", [1024])
    w_s = din("w_s", [8, 128, 128])
    w_in = din("w_in", [D, 4096])
    w_out = din("w_out", [D, D])
    w_up = din("w_up", [D, 2 * DFF])
    w_down = din("w_down", [DFF, D])
    w_pg = din("w_pg", [D, D])
    w_pp = din("w_pp", [DPLE, D])

    y = dout("y", [1152, D])
    o_cap = dout("o_cap", [30, CA])
    o_cas = dout("o_cas", [16, 30, CA])
    o_ffp = dout("o_ffp", [2, 2 * DFF])
    o_ffs = dout("o_ffs", [32, 2 * DFF])
    o_cvp = dout("o_cvp", [128, CA])
    o_cvs = dout("o_cvs", [128, CA])

    X1 = nc.dram_tensor("x1_scratch", [NCOL, D], F32).ap()
    X2 = nc.dram_tensor("x2_scratch", [1152, D], F32).ap()

    PS = nc.alloc_psum_tensor("PS", [128, 8, 512], F32)

    es = ExitStack()
    S = Sched(nc, es)

    total = nc.sbuf_bytes_remaining
    ar = Arena(nc, (total - 256) // 32 * 32)

    CP = ar.alloc([128, NCP], F32)
    IDF = CP[:, C_ID : C_ID + 128]
    ONESF = CP[:, C_ONES : C_ONES + 128]
    MT = CP[:, C_MT : C_MT + 128]
    BDM = CP[:, C_BDM : C_BDM + 128]
    EM = CP[0:8, C_E : C_E + 128]
    DWA = CP[:, C_DWA : C_DWA + 248].rearrange("p (j k) -> p j k", k=31)
    BDWA = CP[:, C_BDWA : C_BDWA + 8]
    GLNA = CP[:, C_GLNA : C_GLNA + 8]
    BLNA = CP[:, C_BLNA : C_BLNA + 8]
    GLNV = CP[:, C_GLNV : C_GLNV + 8]
    BLNV = CP[:, C_BLNV : C_BLNV + 8]
    DWF = CP[:, C_DWF : C_DWF + 264].rearrange("p (j k) -> p j k", k=3)
    BDWF = CP[:, C_BDWF : C_BDWF + 88]
    HM = CP[:, C_HM : C_HM + 1]
    EPSC = CP[:, C_EPS : C_EPS + 1]
    IDB = ar.alloc([128, 128], BF16)
    ONESB = ar.alloc([128, 128], BF16)
    SSC = ar.alloc([128, 128], F32)
    RSC = ar.alloc([128, 128], F32)
    HT = ar.alloc([128, 16, NCOL], BF16)
    PT8 = ar.alloc([128, 2, 128], BF16)
    WR = ar.alloc([128, NGRAN * GRAN], BF16)
    static_end = ar.off

    W = WStream(S, WR)

    mb_state = [0]
    xb_state = [0]

    def mbank():
        b = mb_state[0]
        mb_state[0] = (b + 1) % 4
        return b

    xb_list = [[4, 5, 6, 7]]
    xb2_state = [0]

    def xbank():
        lst = xb_list[0]
        b = lst[xb_state[0] % len(lst)]
        xb_state[0] += 1
        return b

    def xbank2():
        b = 4 + 2 * (xb2_state[0] % 2)
        xb2_state[0] += 1
        return b

    def psk(b):
        return ("ps", b)

    def ACT(out, in_, func, reads, writes, **kw):
        S.op("act", lambda e: e.activation(out=out, in_=in_, func=func, **kw), reads, writes)

    def TT(eng, out, in0, in1, op, reads, writes):
        S.op(eng, lambda e: e.tensor_tensor(out=out, in0=in0, in1=in1, op=op), reads, writes)

    def TS(eng, out, in0, s1, s2, op0, op1, reads, writes):
        if s2 is None:
            S.op(eng, lambda e: e.tensor_scalar(out=out, in0=in0, scalar1=s1, scalar2=None, op0=op0), reads, writes)
        else:
            S.op(eng, lambda e: e.tensor_scalar(out=out, in0=in0, scalar1=s1, scalar2=s2, op0=op0, op1=op1), reads, writes)

    def STT(eng, out, in0, sc, in1, op0, op1, reads, writes):
        S.op(eng, lambda e: e.scalar_tensor_tensor(out=out, in0=in0, scalar=sc, in1=in1, op0=op0, op1=op1), reads, writes)

    def PSTT(out, in0, sc, in1, tmp, tkey, reads, writes):
        TS("pool", tmp, in0, sc, None, ALU.mult, None, reads, [tkey])
        TT("pool", out, tmp, in1, ALU.add, reads + [tkey], writes)

    def CPY(eng, out, in_, reads, writes):
        S.op(eng, lambda e: e.tensor_copy(out=out, in_=in_), reads, writes)

    def MM(out, lhsT, rhs, start, stop, reads, writes, signal):
        S.op("pe", lambda e: e.matmul(out, lhsT=lhsT, rhs=rhs, start=start, stop=stop), reads, writes, signal=signal)

    def TR(out, in_, ident, reads, writes, signal):
        S.op("pe", lambda e: e.transpose(out=out, in_=in_, identity=ident), reads, writes, signal=signal)

    def mmgroup(out, K, lhs_fn, rhs_fn, reads, bank):
        for k in range(K):
            MM(out, lhs_fn(k), rhs_fn(k), k == 0, k == K - 1, reads, [psk(bank)], signal=(k == K - 1))

    import os
    STOP = os.environ.get("KSTOP", "")

    class _Stop(Exception):
        pass

    DUMP = os.environ.get("KDUMP", "").split(",")
    dbg_outs = []

    def dbg_dump(label, ap, reads):
        if label not in DUMP:
            return
        shp = [int(x) for x in ap.shape]
        dt = nc.dram_tensor("dbg_" + label, shp, ap.dtype, kind="ExternalOutput").ap()
        S.dma("sp", "dbg_" + label, dt, ap, reads=reads, writes=["o_dbg_" + label])

    def stop_at(name):
        if STOP == name:
            raise _Stop()

    try:
        S.dma("sp", "cp", CP, cpack[:, :], writes=["CP"])
        CPY("dve", IDB, IDF, ["CP"], ["IDB"])
        CPY("dve", ONESB, ONESF, ["CP"], ["ONESB"])
        TT("dve", ONESF, IDF, ONESF, ALU.subtract, ["CP", "ONESB"], ["CP"])

        def wload(dst, src, keys):
            S.dma("pool", f"wr{keys[0][1]}", dst, src, writes=keys)

        def mk_in(c0):
            def emit(g0):
                wload(W.view(g0, GRAN).rearrange("p (k n) -> p k n", n=128),
                      w_in[:, c0 : c0 + 128].rearrange("(k p) n -> p k n", p=128), [("wr", g0)])
            return emit

        def mk_in_pair(c0, c1):
            def emit(g0):
                wload(W.view(g0, GRAN).rearrange("p (k n) -> p k n", n=128),
                      w_in[:, c0 : c0 + 128].rearrange("(k p) n -> p k n", p=128), [("wr", g0)])
                wload(W.view(g0 + 1, GRAN).rearrange("p (k n) -> p k n", n=128),
                      w_in[:, c1 : c1 + 128].rearrange("(k p) n -> p k n", p=128), [("wr", g0 + 1)])
            return emit

        def mk_sq(wt, cg):
            def emit(g0):
                wload(W.view(g0, 4 * GRAN).rearrange("p (k n) -> p k n", n=512),
                      wt[:, cg * 512 : (cg + 1) * 512].rearrange("(k p) n -> p k n", p=128),
                      [("wr", g0 + i) for i in range(4)])
            return emit

        def mk_up_pair(j):
            def emit(g0):
                for i, c0 in enumerate((j * 128, DFF + j * 128)):
                    wload(W.view(g0 + i, GRAN).rearrange("p (k n) -> p k n", n=128),
                          w_up[:, c0 : c0 + 128].rearrange("(k p) n -> p k n", p=128), [("wr", g0 + i)])
            return emit

        def mk_down(cg, kh):
            def emit(g0):
                wload(W.view(g0, 22 * 256).rearrange("p (k n) -> p k n", n=256),
                      w_down[kh * 2816 : (kh + 1) * 2816, cg * 256 : (cg + 1) * 256].rearrange("(k p) n -> p k n", p=128),
                      [("wr", g0 + i) for i in range(3)])
            return emit

        st_a = []
        for j in range(8):
            sa = W.add(2, mk_in_pair(j * 128, 1024 + j * 128))
            su = W.add(1, mk_in(2048 + j * 128))
            sv = W.add(1, mk_in(3072 + j * 128))
            st_a.append((sa, su, sv))
        st_b = [W.add(4, mk_sq(w_out, cg)) for cg in range(4)]
        st_c = [W.add(2, mk_up_pair(j)) for j in range(44)]
        st_d = [[W.add(3, mk_down(cg, kh)) for kh in range(2)] for cg in range(8)]
        st_e = [W.add(4, mk_sq(w_pg, cg)) for cg in range(4)]
        W.plan()
        W.pump()

        SEGA = [(0, 512), (512, 1024), (1024, 1280)]
        SEGC = [(126, 638), (638, 1150), (1150, 1280)]

        def norm_a(XTt, XSt, GBC, col, xt_key, xs_key):
            ss = SSC[:, col : col + 1]
            rs = RSC[:, col : col + 1]
            ACT(XSt, XTt, AF.Square, [xt_key], [xs_key, ("ss", col)], scale=float(D ** -0.5), accum_out=ss)
            ACT(rs, ss, AF.Ln, [("ss", col), "CP"], [("rs", col)], bias=EPSC, scale=1.0)
            ACT(rs, rs, AF.Exp, [("rs", col)], [("rs", col)], scale=-0.5)
            STT("dve", XSt, XTt, rs, GBC, ALU.mult, ALU.mult, [xt_key, ("rs", col), "GBC"], [xs_key])

        def norm_b(XSt, xs_key, htcol):
            b = xbank2()
            pv = PS[:, b : b + 2, :].rearrange("p a b -> p (a b)").bitcast(BF16)
            for k in range(16):
                TR(pv[:, k * 128 : (k + 1) * 128], XSt[:, k * 128 : (k + 1) * 128], IDB,
                   [xs_key, "IDB"], [psk(b), psk(b + 1)], signal=(k == 15))
            ACT(HT[:, :, htcol : htcol + 128], pv.rearrange("p (k n) -> p k n", n=128), AF.Copy,
                [psk(b), psk(b + 1)], [("HT", htcol // 128)])

        m0 = ar.off
        HTALL0 = [('HT', i) for i in range(NT)]
        XT = [ar.alloc([128, D], F32) for _ in range(4)]
        XS = [ar.alloc([128, D], BF16) for _ in range(4)]
        GBC = ar.alloc([128, D], F32)
        S.dma("sp", "gbc", GBC, g_mix.partition_broadcast(128), writes=["GBC"])
        for i in range(NT + 2):
            if i < NT:
                sl = i % 4
                S.dma("sp", f"xt{sl}", XT[sl], xin[i * 128 : (i + 1) * 128, :], writes=[("XT", sl)])
                norm_a(XT[sl], XS[sl], GBC, i, ("XT", sl), ("XS", sl))
            if i >= 2:
                norm_b(XS[(i - 2) % 4], ("XS", (i - 2) % 4), (i - 2) * 128)
        dbg_dump('HT0', HT[:, :, :], HTALL0)
        ar.off = m0
        S.barrier()
        stop_at('0')

        MIXT = ar.alloc([128, 16, NCOL], BF16)
        mA = ar.off
        WTt = ar.alloc([128, 8, 128], BF16)
        WSt = ar.alloc([128, 8, 128], BF16)
        mA2 = ar.off
        W8 = ar.alloc([128, 8, 8], F32)
        WSF = ar.alloc([128, 8, 128], F32)
        C8 = ar.alloc([128, 128], F32)
        S.dma("sp", "wsf", WSF, w_s.rearrange("h t s -> t h s"), writes=["WSF"])
        for h in range(8):
            b = xbank()
            TR(PS[:, b, 0:128], WSF[:, h, :], IDF, ["WSF", "CP"], [psk(b)], signal=True)
            TT("dve", WTt[:, h, :], PS[:, b, 0:128], MT, ALU.mult, [psk(b), "CP"], [("WT", h)])
        S.dma("sp", "w8", W8[0:8, :, :], w_s[:, 0:8, 0:8].rearrange("h t s -> t h s"), writes=["W8"])
        for h in range(8):
            b = xbank()
            MM(PS[0:8, b, 0:128], W8[0:8, h, :], EM, True, True, ["W8", "CP"], [psk(b)], signal=True)
            CPY("dve", C8[0:8, :], PS[0:8, b, 0:128], [psk(b)], ["C8"])
            b2 = xbank()
            MM(PS[:, b2, 0:128], EM, C8[0:8, :], True, True, ["C8", "CP"], [psk(b2)], signal=True)
            TT("dve", WSt[:, h, :], PS[:, b2, 0:128], BDM, ALU.mult, [psk(b2), "CP"], [("WS", h)])

        ar.off = mA2
        S.barrier()
        stop_at('prep')
        BSH = [ar.alloc([128, 2, 128], F32) for _ in range(2)]
        Abuf = [ar.alloc([128, NCOL], F32) for _ in range(2)]
        ASX = [ar.alloc([128, 16, 38], F32) for _ in range(2)]
        SCAT = [ar.alloc([128, 4, 128], F32) for _ in range(2)]
        SIG = ar.alloc([128, NCOL], F32)
        CV = [ar.alloc([128, 1154], F32) for _ in range(2)]
        SQ = ar.alloc([128, NCOL], BF16)
        SQ2 = ar.alloc([128, NCOL], BF16)
        GV = ar.alloc([128, NCOL], F32)
        VNH = [ar.alloc([128, 10, 128], BF16) for _ in range(2)]
        CVO = [ar.alloc([128, 2, 128], F32) for _ in range(2)]
        CAS = ar.alloc([128, CA], F32)
        CAPt = ar.alloc([128, CA], F32)
        TMPB = [ar.alloc([128, 512], F32) for _ in range(1)]
        PTMP3 = ar.alloc([128, 16, 8], F32)
        CVB = ar.alloc([128, 1026], F32)
        print('phaseA arena', ar.off, ar.nbytes)
        S.op("pool", lambda e: e.memset(MIXT[:, 0:8, 0:126], 0.0), [], [("MIXa", j) for j in range(8)])

        S.dma("sp", "casd", o_cas[:, 0:22, :], sca.rearrange("(s r) c -> s r c", r=30)[:, 8:30, :], writes=["o_cas_hist"])

        HTALL = [("HT", i) for i in range(NT)]
        CVSEG_A = [(0, 512), (512, 1024), (1024, 1154)]

        def chain_ln(buf, segs, keys, SQx, sqn):
            for (c0, c1) in segs:
                w = c1 - c0
                b = xbank()
                MM(PS[:, b, 0:w], ONESF, buf[:, c0:c1], True, True, keys + ["CP"], [psk(b)], signal=True)
                ACT(SQx[:, c0:c1], PS[:, b, 0:w], AF.Square, [psk(b)], [(sqn, c0)])
                ACT(buf[:, c0:c1], PS[:, b, 0:w], AF.Copy, [psk(b)] + keys, keys)
                yield
                b2 = xbank()
                MM(PS[:, b2, 0:w], ONESB, SQx[:, c0:c1], True, True, [(sqn, c0), "ONESB"], [psk(b2)], signal=True)
                ACT(PS[:, b2, 0:w], PS[:, b2, 0:w], AF.Ln, [psk(b2), "CP"], [psk(b2)], bias=EPSC, scale=1.0)
                ACT(PS[:, b2, 0:w], PS[:, b2, 0:w], AF.Exp, [psk(b2)], [psk(b2)], scale=-0.5)
                TT("dve", buf[:, c0:c1], buf[:, c0:c1], PS[:, b2, 0:w], ALU.mult, keys + [psk(b2)], keys)
                yield

        def a_main(j):
            sa, su, sv = st_a[j]
            sl = j % 2
            A = Abuf[sl]
            akey = ("A", sl)
            g0 = W.use(sa)
            Wv = W.view(g0, GRAN).rearrange("p (k n) -> p k n", n=128)
            Wg = W.view(g0 + 1, GRAN).rearrange("p (k n) -> p k n", n=128)
            for s_, (c0, c1) in enumerate(SEGA):
                w = c1 - c0
                bv = mbank()
                mmgroup(PS[:, bv, 0:w], 16, lambda k: Wv[:, k, :], lambda k: HT[:, k, c0:c1], HTALL + [("wr", g0)], bv)
                bg = mbank()
                mmgroup(PS[:, bg, 0:w], 16, lambda k: Wg[:, k, :], lambda k: HT[:, k, c0:c1], HTALL + [("wr", g0 + 1)], bg)
                ACT(SIG[:, c0:c1], PS[:, bg, 0:w], AF.Sigmoid, [psk(bg)], [("SIG", s_)])
                ACT(A[:, c0:c1], PS[:, bv, 0:w], AF.Copy, [psk(bv)], [akey])
                yield
                yield
            W.release(sa)
            TT("dve", A[:, :], A[:, :], SIG[:, :], ALU.mult, [akey] + [("SIG", s_) for s_ in range(3)], [akey])
            g0 = W.use(su)
            Wu = W.view(g0, GRAN).rearrange("p (k n) -> p k n", n=128)
            for s_, (c0, c1) in enumerate(SEGA):
                w = c1 - c0
                bu = mbank()
                mmgroup(PS[:, bu, 0:w], 16, lambda k: Wu[:, k, :], lambda k: HT[:, k, c0:c1], HTALL + [("wr", g0)], bu)
                ACT(MIXT[:, 8 + j, c0:c1], PS[:, bu, 0:w], AF.Gelu_apprx_tanh, [psk(bu)], [("MIXb", j, s_)])
                yield
            W.release(su)
            g0 = W.use(sv)
            Wvv = W.view(g0, GRAN).rearrange("p (k n) -> p k n", n=128)
            for s_, (c0, c1) in enumerate(SEGA):
                w = c1 - c0
                bvv = mbank()
                mmgroup(PS[:, bvv, 0:w], 16, lambda k: Wvv[:, k, :], lambda k: HT[:, k, c0:c1], HTALL + [("wr", g0)], bvv)
                ACT(GV[:, c0:c1], PS[:, bvv, 0:w], AF.Gelu_apprx_tanh, [psk(bvv)], ["GV"])
                yield
            W.release(sv)

        def a_conv(j):
            sl = j % 2
            A = Abuf[sl]
            akey = ("A", sl)
            S.dma("sp", f"scat{sl}", SCAT[sl][0:120, :, :],
                  sca[:, j * 128 : (j + 1) * 128].rearrange("(t r) c -> r t c", r=120), writes=[("SCAT", sl)])
            b = xbank()
            for t in range(4):
                TR(PS[:, b, t * 120 : (t + 1) * 120], SCAT[sl][0:120, t, :], IDF[0:120, 0:120],
                   [("SCAT", sl), "CP"], [psk(b)], signal=(t == 3))
            ACT(ASX[sl][:, :, 0:30], PS[:, b, 0:480].rearrange("p (s r) -> p s r", r=30), AF.Copy, [psk(b)], [("ASXh", sl)])
            CPY("dve", ASX[sl][:, :, 30:38], A[:, 1152:1280].rearrange("p (s t) -> p s t", t=8), [akey], [("ASXn", sl)])
            b = xbank()
            TR(PS[0:30, b, 0:128], A[:, 1122:1152], IDF, [akey, "CP"], [psk(b)], signal=True)
            ACT(CAPt[0:30, j * 128 : (j + 1) * 128], PS[0:30, b, 0:128], AF.Copy, [psk(b)], [("CAP", j)])
            b = xbank()
            TR(PS[:, b, 0:128], A[:, 1152:1280], IDF, [akey, "CP"], [psk(b)], signal=True)
            ACT(CAS[:, j * 128 : (j + 1) * 128], PS[:, b, 0:128], AF.Copy, [psk(b)], [("CAS", j)])
            cv = CV[sl]
            ckm = ("CV", sl, "m")
            cks = ("CV", sl, "s")
            cvm = cv[:, 0:1026]
            cvs = cv[:, 1026:1154].rearrange("p (s t) -> p s t", t=8)
            ax = ASX[sl]
            skeys = [("ASXn", sl), ("ASXh", sl), "CP"]
            ACT(cvm, A[:, 126:1152], AF.Identity, [akey, "CP"], [ckm], scale=DWA[:, j, 30:31], bias=BDWA[:, j : j + 1])
            TS("dve", cvs, ax[:, :, 30:38], DWA[:, j, 30:31], BDWA[:, j : j + 1], ALU.mult, ALU.add, skeys, [cks])
            ACT(CVB, A[:, 96 + 29 : 96 + 29 + 1026], AF.Copy, [akey, "CP"], ["CVB"], scale=DWA[:, j, 29:30])
            STT("dve", cvs, ax[:, :, 29:37], DWA[:, j, 29:30], cvs, ALU.mult, ALU.add, skeys + [cks], [cks])
            yield
            for k in range(29):
                if k % 2 == 0:
                    STT("dve", cvm, A[:, 96 + k : 96 + k + 1026], DWA[:, j, k : k + 1], cvm, ALU.mult, ALU.add, [akey, "CP", ckm], [ckm])
                else:
                    STT("dve", CVB, A[:, 96 + k : 96 + k + 1026], DWA[:, j, k : k + 1], CVB, ALU.mult, ALU.add, [akey, "CP", "CVB"], ["CVB"])
                STT("dve", cvs, ax[:, :, k : k + 8], DWA[:, j, k : k + 1], cvs, ALU.mult, ALU.add, skeys + [cks], [cks])
                if k % 3 == 2:
                    yield
            TT("dve", cvm, cvm, CVB, ALU.add, [ckm, "CVB"], [ckm])
            yield

        def chain_a(j):
            sl = j % 2
            cv = CV[sl]
            ckm = ("CV", sl, "m")
            cks = ("CV", sl, "s")
            yield from chain_ln(cv, CVSEG_A, [ckm, cks], SQ, 'SQa')
            ACT(MIXT[:, j, 126:1280], cv[:, 0:1154], AF.Silu, [ckm, cks, "CP"], [("MIXa", j)],
                scale=GLNA[:, j : j + 1], bias=BLNA[:, j : j + 1])
            yield

        def chain_v(j):
            yield from chain_ln(GV, SEGA, ["GV"], SQ2, "SQv")
            ACT(GV[:, :], GV[:, :], AF.Identity, ["GV", "CP"], ["GV"], scale=GLNV[:, j : j + 1], bias=BLNV[:, j : j + 1])
            vs = j % 2
            vnh = VNH[vs]
            S.dma("sp", f"bsh{vs}a", BSH[vs][:, 0, :], bs_p[j * 128 : (j + 1) * 128].partition_broadcast(128), writes=[("BSH", vs, 0)])
            S.dma("sp", f"bsh{vs}b", BSH[vs][:, 1, :], bs_s[j * 128 : (j + 1) * 128].partition_broadcast(128), writes=[("BSH", vs, 1)])
            for (i0, i1) in ((0, 4), (4, 8), (8, 10)):
                n = i1 - i0
                b = xbank()
                for i in range(i0, i1):
                    TR(PS[:, b, (i - i0) * 128 : (i - i0 + 1) * 128], GV[:, i * 128 : (i + 1) * 128], IDF,
                       ["GV", "CP"], [psk(b)], signal=(i == i1 - 1))
                ACT(vnh[:, i0:i1, :], PS[:, b, 0 : n * 128].rearrange("p (a c) -> p a c", c=128), AF.Copy,
                    [psk(b)], [("VNH", vs, i0)])
                if i0 == 8:
                    ACT(CVO[vs][:, :, :], PS[:, b, 0:256].rearrange("p (a c) -> p a c", c=128), AF.Copy, [psk(b)], [("CVO", vs)])
                    S.dma("sp", f"cvo{vs}a", o_cvp[:, j * 128 : (j + 1) * 128], CVO[vs][:, 0, :], reads=[("CVO", vs)], writes=[("o_cvp", j)])
                    S.dma("sp", f"cvo{vs}b", o_cvs[:, j * 128 : (j + 1) * 128], CVO[vs][:, 1, :], reads=[("CVO", vs)], writes=[("o_cvs", j)])
                yield
            for gi, (i0, i1) in enumerate(((0, 4), (4, 8), (8, 10))):
                n = i1 - i0
                b = xbank()
                for i in range(i0, i1):
                    rhs = WTt[:, j, :] if i < 9 else WSt[:, j, :]
                    MM(PS[:, b, (i - i0) * 128 : (i - i0 + 1) * 128], vnh[:, i, :], rhs, True, True,
                       [("VNH", vs, i0), ("WT", j), ("WS", j)], [psk(b)], signal=(i == i1 - 1))
                tb = TMPB[0]
                tkey = ("TMPB", 0)
                if i0 < 8:
                    TT("dve", tb[:, 0 : n * 128].rearrange("p (a c) -> p a c", c=128),
                       PS[:, b, 0 : n * 128].rearrange("p (a c) -> p a c", c=128),
                       BSH[vs][:, 0:1, :].to_broadcast([128, n, 128]), ALU.add, [psk(b), ("BSH", vs, 0)], [tkey])
                else:
                    TT("dve", tb[:, 0:256].rearrange("p (a c) -> p a c", c=128),
                       PS[:, b, 0:256].rearrange("p (a c) -> p a c", c=128), BSH[vs][:, :, :], ALU.add,
                       [psk(b), ("BSH", vs, 0), ("BSH", vs, 1)], [tkey])
                mk = [("MIXb", j, s_) for s_ in range(3)]
                TT("dve", MIXT[:, 8 + j, i0 * 128 : i1 * 128], tb[:, 0 : n * 128], MIXT[:, 8 + j, i0 * 128 : i1 * 128],
                   ALU.mult, [tkey] + mk, mk)
                yield

        def advance(gens):
            for g in list(gens):
                try:
                    next(g)
                except StopIteration:
                    gens.remove(g)

        pending = []
        for it in range(8 + 2):
            if 1 <= it <= 8:
                pending.append(a_conv(it - 1))
                pending.append(chain_v(it - 1))
            if 2 <= it <= 9:
                pending.append(chain_a(it - 2))
            if it < 8:
                for _ in a_main(it):
                    advance(pending)
            while pending:
                advance(pending)

        stop_at('A10')
        S.dma("sp", "capo", o_cap[:, :], CAPt[0:30, :], reads=[("CAP", j) for j in range(8)], writes=["o_cap"])
        for s in range(16):
            S.dma("sp", f"caso{s % 4}", o_cas[s, 22:30, :], CAS[s * 8 : (s + 1) * 8, :],
                  reads=[("CAS", j) for j in range(8)], writes=[("o_cas", s)])

        dbg_dump('MIXT', MIXT[:, :, :], [('MIXa', j) for j in range(8)] + [('MIXb', j, s) for j in range(8) for s in range(3)])
        ar.off = mA
        S.barrier()
        stop_at('A')

        XB = [ar.alloc([128, 512], F32) for _ in range(3)]
        X1B = [ar.alloc([128, 512], F32) for _ in range(3)]
        XT = [ar.alloc([128, D], F32) for _ in range(3)]
        XS = [ar.alloc([128, D], BF16) for _ in range(6)]
        GBC = ar.alloc([128, D], F32)
        S.dma("sp", "gbc", GBC, g_ffn.partition_broadcast(128), writes=["GBC"])
        MIXALL = [("MIXa", j) for j in range(8)] + [("MIXb", j, s) for j in range(8) for s in range(3)]
        cnt = 0

        def phaseB_norm_a(i):
            sl = i % 3
            S.dma("sp", f"xt{sl}", XT[sl], X1[i * 128 : (i + 1) * 128, :], reads=[("X1", i, cg) for cg in range(4)], writes=[("XT", sl)])
            norm_a(XT[sl], XS[i % 6], GBC, 10 + i, ("XT", sl), ("XS", i % 6))

        def phaseB_norm_b(i):
            norm_b(XS[i % 6], ("XS", i % 6), i * 128)

        for cg in range(4):
            g0 = W.use(st_b[cg])
            Wo = W.view(g0, 4 * GRAN).rearrange("p (k n) -> p k n", n=512)
            wkeys = [("wr", g0 + i) for i in range(4)]
            for i in range(NT):
                r = cnt % 3
                cnt += 1
                S.dma("sp", f"xb{r}", XB[r], xin[i * 128 : (i + 1) * 128, cg * 512 : (cg + 1) * 512], writes=[("XB", r)])
                b = mbank()
                mmgroup(PS[:, b, :], 16, lambda k: MIXT[:, k, i * 128 : (i + 1) * 128], lambda k: Wo[:, k, :], MIXALL + wkeys, b)
                TT("dve", X1B[r], PS[:, b, :], XB[r], ALU.add, [psk(b), ("XB", r)], [("X1B", r)])
                S.dma("act", f"x1b{r}", X1[i * 128 : (i + 1) * 128, cg * 512 : (cg + 1) * 512], X1B[r],
                      reads=[("X1B", r)], writes=[("X1", i, cg)])
                if cg == 3 and i >= 1:
                    phaseB_norm_a(i - 1)
                if cg == 3 and i >= 5:
                    phaseB_norm_b(i - 5)
            W.release(st_b[cg])
        phaseB_norm_a(NT - 1)
        for i in range(NT - 5, NT):
            phaseB_norm_b(i)

        ar.off = static_end
        S.barrier()

        stop_at('B')
        G = ar.alloc([128, 44, 1152], BF16)
        mG = ar.off
        CX = [ar.alloc([128, 1152], F32) for _ in range(2)]
        SX = [ar.alloc([128, 16, 10], F32) for _ in range(2)]
        UPS = [ar.alloc([128, 34], F32) for _ in range(2)]
        SFB = [ar.alloc([128, 128], F32) for _ in range(2)]
        FFO = [ar.alloc([128, 128], F32) for _ in range(2)]
        xb_list[0] = [6, 7]
        g0c = {}

        def c_info(n):
            j, half = n // 2, n % 2
            blk = j + 44 * half
            pb = 3 * half
            P = PS[:, pb : pb + 3, :].rearrange("p a b -> p (a b)")
            pkeys = [psk(pb), psk(pb + 1), psk(pb + 2)]
            return j, half, blk, pb, P, pkeys

        def c_pre(n):
            j, half, blk, pb, P, pkeys = c_info(n)
            bs = n % 2
            S.dma("sp", f"sfb{bs}", SFB[bs][0:32, :], sff[:, blk * 128 : (blk + 1) * 128], writes=[("SFB", bs)])

        def c_tr(n):
            bs = n % 2
            xb = xbank()
            TR(PS[:, xb, 0:32], SFB[bs][0:32, :], IDF[0:32, 0:32], [("SFB", bs), "CP"], [psk(xb)], signal=True)
            ACT(SX[bs][:, :, 0:2], PS[:, xb, 0:32].rearrange("p (s r) -> p s r", r=2), AF.Copy, [psk(xb)], [("SXh", bs)])

        def c_mm(n):
            j, half, blk, pb, P, pkeys = c_info(n)
            bs = n % 2
            if half == 0:
                g0c[j] = W.use(st_c[j])
            g0 = g0c[j]
            Wb = W.view(g0 + half, GRAN).rearrange("p (k n) -> p k n", n=128)
            for s_, (c0, c1) in enumerate(SEGC):
                w = c1 - c0
                mmgroup(PS[:, pb + s_, 0:w], 16, lambda k: Wb[:, k, :], lambda k: HT[:, k, c0:c1], HTALL + [("wr", g0 + half)], pb + s_)
            if half == 1:
                W.release(st_c[j])
            TS("dve", P[:, 0:2], P[:, 0:2], HM, None, ALU.mult, None, [psk(pb), "CP"], [psk(pb)])

        def c_post(n):
            j, half, blk, pb, P, pkeys = c_info(n)
            bs = n % 2
            ACT(UPS[bs][:, :].rearrange("p (i r) -> p i r", r=2),
                P[:, 1024 : 1024 + 136].rearrange("p (i r) -> p i r", r=8)[:, :, 0:2], AF.Copy, pkeys, [("UPS", bs)])
            sx = SX[bs]
            ACT(sx[:, :, 2:10], P[:, 1026:1154].rearrange("p (s t) -> p s t", t=8), AF.Copy, pkeys, [("SXn", bs)])
            xb = xbank()
            TR(PS[0:34, xb, 0:128], UPS[bs][:, :], IDF, [("UPS", bs), "CP"], [psk(xb)], signal=True)
            ACT(FFO[bs][0:34, :], PS[0:34, xb, 0:128], AF.Copy, [psk(xb)], [("FFO", bs)])
            S.dma("act", f"ffo{bs}a", o_ffp[:, blk * 128 : (blk + 1) * 128], FFO[bs][0:2, :], reads=[("FFO", bs)], writes=[("o_ffp", blk)])
            S.dma("act", f"ffo{bs}b", o_ffs[:, blk * 128 : (blk + 1) * 128], FFO[bs][2:34, :], reads=[("FFO", bs)], writes=[("o_ffs", blk)])
            cx = CX[half]
            ckm = ("CX", half, "m")
            cks = ("CX", half, "s")
            cxs = cx[:, 1024:1152].rearrange("p (s t) -> p s t", t=8)
            skeys = [("SXh", bs), ("SXn", bs), "CP"]
            TS("dve", cx[:, 0:1024], P[:, 2:1026], DWF[:, blk, 2:3], BDWF[:, blk : blk + 1], ALU.mult, ALU.add, pkeys + ["CP"], [ckm])
            TS("dve", cxs, sx[:, :, 2:10], DWF[:, blk, 2:3], BDWF[:, blk : blk + 1], ALU.mult, ALU.add, skeys, [cks])
            STT("dve", cx[:, 0:1024], P[:, 1:1025], DWF[:, blk, 1:2], cx[:, 0:1024], ALU.mult, ALU.add, pkeys + ["CP", ckm], [ckm])
            STT("dve", cxs, sx[:, :, 1:9], DWF[:, blk, 1:2], cxs, ALU.mult, ALU.add, skeys + [cks], [cks])
            STT("dve", cx[:, 0:1024], P[:, 0:1024], DWF[:, blk, 0:1], cx[:, 0:1024], ALU.mult, ALU.add, pkeys + ["CP", ckm], [ckm])
            STT("dve", cxs, sx[:, :, 0:8], DWF[:, blk, 0:1], cxs, ALU.mult, ALU.add, skeys + [cks], [cks])
            if half == 0:
                ACT(cx[:, :], cx[:, :], AF.Silu, [ckm, cks], [ckm, cks])
            else:
                TT("dve", G[:, j, :], CX[0][:, :], CX[1][:, :], ALU.mult,
                   [("CX", 0, "m"), ("CX", 0, "s"), ("CX", 1, "m"), ("CX", 1, "s")], [("G", j)])

        c_pre(0)
        for n in range(89):
            if n < 88:
                if n + 1 < 88:
                    c_pre(n + 1)
                c_tr(n)
                c_mm(n)
            if n >= 1:
                c_post(n - 1)

        stop_at('C')
        xb_list[0] = [4, 5, 6, 7]
        ar.off = mG
        S.barrier()
        X1R = [ar.alloc([128, 256], F32) for _ in range(4)]
        X2B = [ar.alloc([128, 256], F32) for _ in range(4)]
        GPB = [ar.alloc([128, 256], F32) for _ in range(2)]
        XGB = [ar.alloc([128, 256], BF16) for _ in range(3)]
        SQJ = ar.alloc([128, 256], BF16)
        PIN1 = ar.alloc([128, DPLE], F32)
        PB1 = ar.alloc([128, DPLE], BF16)
        GALL = [("G", j) for j in range(44)]

        def d_pload(t):
            S.dma("sp", "pin1", PIN1, pin[t * 128 : (t + 1) * 128, :], writes=["PIN1"])
            CPY("dve", PB1, PIN1, ["PIN1"], ["PB1"])

        def d_ptr(t):
            xb = xbank()
            pv = PS[:, xb, 0:128].bitcast(BF16)
            for k in range(2):
                TR(pv[:, k * 128 : (k + 1) * 128], PB1[:, k * 128 : (k + 1) * 128], IDB, ["PB1", "IDB"], [psk(xb)], signal=(k == 1))
            if t < 8:
                ACT(HT[:, 2 * t : 2 * t + 2, 0:128], pv.rearrange("p (k n) -> p k n", n=128), AF.Copy, [psk(xb)], [("HT", 0), ("PTH", t)])
            else:
                ACT(PT8[:, :, :], pv.rearrange("p (k n) -> p k n", n=128), AF.Copy, [psk(xb)], [("PTH", t)])

        cnt = 0
        dq = []

        def d_tr(n, cg, t):
            slot = n % 3
            xb = xbank()
            pv = PS[:, xb, 0:128].bitcast(BF16)
            for k in range(2):
                TR(pv[:, k * 128 : (k + 1) * 128], XGB[slot][:, k * 128 : (k + 1) * 128], IDB, [("XGB", slot), "IDB"], [psk(xb)], signal=(k == 1))
            ACT(HT[:, 2 * cg : 2 * cg + 2, (t + 1) * 128 : (t + 2) * 128], pv.rearrange("p (k n) -> p k n", n=128), AF.Copy,
                [psk(xb)], [("HT", t + 1)])

        for cg in range(8):
            S.dma("sp", f"gpb{cg % 2}", GPB[cg % 2], g_ple[cg * 256 : (cg + 1) * 256].partition_broadcast(128), writes=[("GPB", cg % 2)])
            g0s = [W.use(st_d[cg][kh]) for kh in range(2)]
            Wd = [W.view(g0s[kh], 22 * 256).rearrange("p (k n) -> p k n", n=256) for kh in range(2)]
            wkeys = [("wr", g0s[kh] + i) for kh in range(2) for i in range(3)]
            for t in range(9):
                r = cnt % 4
                cnt += 1
                row0 = (t + 1) * 128
                S.dma("sp", f"x1r{r}", X1R[r], X1[row0 : row0 + 128, cg * 256 : (cg + 1) * 256],
                      reads=[("X1", t + 1, cg // 2)], writes=[("X1R", r)])
                b = mbank()
                mmgroup(PS[:, b, 0:256], 44, lambda k: G[:, k, t * 128 : (t + 1) * 128],
                        lambda k: Wd[k // 22][:, k % 22, :], GALL + wkeys, b)
                TT("dve", X2B[r], PS[:, b, 0:256], X1R[r], ALU.add, [psk(b), ("X1R", r)], [("X2B", r)])
                S.dma("act", f"x2b{r}", X2[t * 128 : (t + 1) * 128, cg * 256 : (cg + 1) * 256], X2B[r],
                      reads=[("X2B", r)], writes=[("X2", t, cg)])
                col = 32 + t * 8 + cg
                ACT(SQJ, X2B[r], AF.Square, [("X2B", r)], ["SQJ", ("ss", col)], scale=float(D ** -0.5), accum_out=SSC[:, col : col + 1])
                n = cg * 9 + t
                TT("dve", XGB[n % 3], X2B[r], GPB[cg % 2], ALU.mult, [("X2B", r), ("GPB", cg % 2)], [("XGB", n % 3)])
                dq.append((n, cg, t))
                if len(dq) > 2:
                    d_tr(*dq.pop(0))
                if cg == 1:
                    if t >= 1:
                        d_ptr(t - 1)
                    d_pload(t)
                if cg == 2 and t == 0:
                    d_ptr(8)
            if cg < 7:
                W.release(st_d[cg][0])
                W.release(st_d[cg][1])
        while dq:
            d_tr(*dq.pop(0))

        ar.off = static_end
        S.barrier()

        stop_at('D')
        X3 = ar.alloc([128, 9, D], F32)
        GBC = ar.alloc([128, D], F32)
        XS = [ar.alloc([128, D], BF16) for _ in range(1)] * 2
        WP = ar.alloc([128, 2, D], BF16)
        SG = [ar.alloc([128, 512], F32) for _ in range(1)] * 2
        TP = [ar.alloc([128, 512], F32) for _ in range(1)] * 2
        S.dma("pool", "wp", WP, w_pp.rearrange("(k p) n -> p k n", p=128), writes=["WP"])
        W.release(st_d[7][0])
        W.release(st_d[7][1])
        S.dma("sp", "gbc", GBC, g_fin.partition_broadcast(128), writes=["GBC"])
        for t in range(9):
            ACT(RSC[:, 120:128], SSC[:, 32 + t * 8 : 40 + t * 8], AF.Copy, [("ss", 32 + t * 8 + c) for c in range(8)],
                ["junk8", ("ss", 104 + t)], accum_out=SSC[:, 104 + t : 105 + t])
        ACT(RSC[:, 104:113], SSC[:, 104:113], AF.Ln, [("ss", 104 + t) for t in range(9)] + ["CP"], ["rs3"], bias=EPSC, scale=1.0)
        ACT(RSC[:, 104:113], RSC[:, 104:113], AF.Exp, ["rs3"], ["rs3"], scale=-0.5)
        def e_x3(t):
            S.dma("sp", f"x3l{t}", X3[:, t, :], X2[t * 128 : (t + 1) * 128, :], reads=[("X2", t, cg) for cg in range(8)], writes=[("X3", t)])

        e_x3(0)
        e_x3(1)
        cnt = 0
        for cg in range(4):
            g0 = W.use(st_e[cg])
            Wg = W.view(g0, 4 * GRAN).rearrange("p (k n) -> p k n", n=512)
            wkeys = [("wr", g0 + i) for i in range(4)]
            for t in range(9):
                if cg == 0 and t + 2 <= 8:
                    e_x3(t + 2)
                r = cnt % 2
                cnt += 1
                hc = (t + 1) * 128
                b = mbank()
                mmgroup(PS[:, b, :], 16, lambda k: HT[:, k, hc : hc + 128], lambda k: Wg[:, k, :], [("HT", t + 1)] + wkeys, b)
                b2 = mbank()
                mmgroup(PS[:, b2, :], 2, lambda k: (HT[:, 2 * t + k, 0:128] if t < 8 else PT8[:, k, :]),
                        lambda k: WP[:, k, cg * 512 : (cg + 1) * 512], [("PTH", t), "WP"], b2)
                ACT(SG[r], PS[:, b, :], AF.Sigmoid, [psk(b), "rs3"], [("SG", 0)], scale=RSC[:, 104 + t : 105 + t])
                TT("dve", TP[r], SG[r], PS[:, b2, :], ALU.mult, [("SG", 0), psk(b2)], [("TP", 0)])
                x3s = X3[:, t, cg * 512 : (cg + 1) * 512]
                TT("dve", x3s, x3s, TP[r], ALU.add, [("X3", t), ("TP", 0)], [("X3", t)])
                if cg == 3:
                    col = 20 + t
                    ss = SSC[:, col : col + 1]
                    ACT(XS[t % 2], X3[:, t, :], AF.Square, [("X3", t)], [("XS", t % 2), ("ss", col)], scale=float(D ** -0.5), accum_out=ss)
            W.release(st_e[cg])
        for t in range(9):
            col = 20 + t
            ACT(RSC[:, col : col + 1], SSC[:, col : col + 1], AF.Ln, [("ss", col), "CP"], [("rs", col)], bias=EPSC, scale=1.0)
        for t in range(9):
            col = 20 + t
            ACT(RSC[:, col : col + 1], RSC[:, col : col + 1], AF.Exp, [("rs", col)], [("rs", col)], scale=-0.5)
        for t in range(9):
            col = 20 + t
            x3t = X3[:, t, :]
            STT("dve", x3t, x3t, RSC[:, col : col + 1], GBC, ALU.mult, ALU.mult, [("X3", t), ("rs", col), "GBC"], [("X3", t)])
            S.dma("sp", f"yo{t % 2}", y[t * 128 : (t + 1) * 128, :], x3t, reads=[("X3", t)], writes=[("y", t)])
    except _Stop:
        pass

    S.finish("sp")
    print("n semaphores", len(S.sems))

    with nc.Block() as block:
        @block.tensor
        def _(e):
            S.replay("pe", e)

        @block.scalar
        def _(e):
            S.replay("act", e)

        @block.vector
        def _(e):
            S.replay("dve", e)

        @block.gpsimd
        def _(e):
            S.replay("pool", e)

        @block.sync
        def _(e):
            S.replay("sp", e)

    es.close()
    return nc


def _cpack(inp, half):
    cp = np.zeros((128, NCP), np.float32)
    cp[:, C_ID : C_ID + 128] = np.eye(128, dtype=np.float32)
    cp[:, C_ONES : C_ONES + 128] = 1.0 / 128.0
    s = np.arange(128)
    cp[:, C_MT : C_MT + 128] = (s[:, None] <= s[None, :]).astype(np.float32)
    cp[:, C_BDM : C_BDM + 128] = ((s[:, None] // 8 == s[None, :] // 8) & (s[:, None] % 8 <= s[None, :] % 8)).astype(np.float32)
    cp[0:8, C_E : C_E + 128] = (s[None, :] % 8 == np.arange(8)[:, None]).astype(np.float32)
    cp[:, C_DWA : C_DWA + 248] = inp["w_dw_a"][0].reshape(31, 8, 128).transpose(2, 1, 0).reshape(128, 248)

    def v8(v):
        return v.reshape(8, 128).T

    cp[:, C_BDWA : C_BDWA + 8] = v8(inp["b_dw_a"][0])
    cp[:, C_GLNA : C_GLNA + 8] = v8(inp["g_ln_a"][0])
    cp[:, C_BLNA : C_BLNA + 8] = v8(inp["b_ln_a"][0])
    cp[:, C_GLNV : C_GLNV + 8] = v8(inp["g_ln_v"][0])
    cp[:, C_BLNV : C_BLNV + 8] = v8(inp["b_ln_v"][0])
    cp[:, C_DWF : C_DWF + 264] = inp["w_dw_f"][0].reshape(3, 88, 128).transpose(2, 1, 0).reshape(128, 264)
    cp[:, C_BDWF : C_BDWF + 88] = inp["b_dw_f"][0].reshape(88, 128).T
    cp[:, C_HM] = float(half)
    cp[:, C_EPS] = EPS
    return cp


_NC_CACHE = {}


def kernel(**inp):
    inp = {k: np.asarray(v) for k, v in inp.items()}
    xp, xs = inp["x_prompt"], inp["x_sample"]
    pp, ps = inp["p_prompt"][0], inp["p_sample"][0]
    sca_all, sff_all = inp["state_conv_a"][0], inp["state_ffn_conv"][0]
    f32 = np.float32

    shared = {
        "g_mix": np.ascontiguousarray(inp["g_mix"][0], f32),
        "g_ffn": np.ascontiguousarray(inp["g_ffn"][0], f32),
        "g_ple": np.ascontiguousarray(inp["g_ple"][0], f32),
        "g_fin": np.ascontiguousarray(inp["g_final"], f32),
        "bs_p": np.ascontiguousarray(inp["b_s"][0].reshape(1024), f32),
        "bs_s": np.ascontiguousarray(np.tile(inp["b_s"][0][:, :8], (1, 16)).reshape(1024), f32),
        "w_s": np.ascontiguousarray(inp["w_s"][0], f32),
        "w_in": np.ascontiguousarray(inp["w_in"][0], f32),
        "w_out": np.ascontiguousarray(inp["w_out"][0], f32),
        "w_up": np.ascontiguousarray(inp["w_up"][0], f32),
        "w_down": np.ascontiguousarray(inp["w_down"][0], f32),
        "w_pg": np.ascontiguousarray(inp["w_ple_gate"][0], f32),
        "w_pp": np.ascontiguousarray(inp["w_ple_proj"][0], f32),
    }
    in_maps = []
    for c in range(8):
        b, half = c // 2, c % 2
        halo = xp[b, 896:1024] if half else np.zeros((128, D), f32)
        xin = np.concatenate([halo, xp[b, half * 1024 : (half + 1) * 1024], xs[c * 16 : (c + 1) * 16].reshape(128, D)], 0)
        pin = np.concatenate([pp[b, half * 1024 : (half + 1) * 1024], ps[c * 16 : (c + 1) * 16].reshape(128, DPLE)], 0)
        m = dict(shared)
        m["xin"] = np.ascontiguousarray(xin, f32)
        m["pin"] = np.ascontiguousarray(pin, f32)
        m["sca"] = np.ascontiguousarray(sca_all[c * 16 : (c + 1) * 16].reshape(480, CA), f32)
        m["sff"] = np.ascontiguousarray(sff_all[c * 16 : (c + 1) * 16].reshape(32, 2 * DFF), f32)
        m["cpack"] = _cpack(inp, half)
        in_maps.append(m)

    if "nc" not in _NC_CACHE:
        _NC_CACHE["nc"] = build_program()
    nc = _NC_CACHE["nc"]
    res = run_bass_kernel_spmd(nc, in_maps, core_ids=list(range(8)))
    R = res.results

    y_prompt = np.zeros((4, 2048, D), f32)
    y_sample = np.zeros((128, 8, D), f32)
    conv_a_p = np.zeros((1, 4, 30, CA), f32)
    conv_a_s = np.zeros((1, 128, 30, CA), f32)
    ffn_p = np.zeros((1, 4, 2, 2 * DFF), f32)
    ffn_s = np.zeros((1, 128, 2, 2 * DFF), f32)
    cv_p = np.zeros((1, 4, 128, CA), f32)
    cv_s = np.zeros((1, 128, 8, CA), f32)
    for c in range(8):
        b, half = c // 2, c % 2
        r = R[c]
        y_prompt[b, half * 1024 : (half + 1) * 1024] = r["y"][:1024]
        y_sample[c * 16 : (c + 1) * 16] = r["y"][1024:].reshape(16, 8, D)
        conv_a_s[0, c * 16 : (c + 1) * 16] = r["o_cas"]
        ffn_s[0, c * 16 : (c + 1) * 16] = r["o_ffs"].reshape(16, 2, 2 * DFF)
        cv_s[0, c * 16 : (c + 1) * 16] = r["o_cvs"].reshape(16, 8, CA)
        if half == 1:
            conv_a_p[0, b] = r["o_cap"]
            ffn_p[0, b] = r["o_ffp"]
            cv_p[0, b] = r["o_cvp"]
    return (y_prompt, y_sample, conv_a_p, conv_a_s, ffn_p, ffn_s, cv_p, cv_s)
```

```python
import numpy as np
from contextlib import ExitStack

import concourse.bass as bass
import concourse.mybir as mybir
from concourse.bass_utils import run_bass_kernel_spmd

F32 = mybir.dt.float32
BF16 = mybir.dt.bfloat16
AF = mybir.ActivationFunctionType
ALU = mybir.AluOpType

EPS = 1e-6
D = 2048
CA = 1024
DFF = 5632
DPLE = 256
NCOL = 1280
NT = 10
GRAN = 2048
NGRAN = 12
NPOOL_TAPS = 0

C_ID, C_ONES, C_MT, C_BDM, C_E = 0, 128, 256, 384, 512
C_DWA = 640
C_BDWA = C_DWA + 248
C_GLNA = C_BDWA + 8
C_BLNA = C_GLNA + 8
C_GLNV = C_BLNA + 8
C_BLNV = C_GLNV + 8
C_DWF = C_BLNV + 8
C_BDWF = C_DWF + 264
C_HM = C_BDWF + 88
C_EPS = C_HM + 1
NCP = C_EPS + 1 + 6


class Sched:
    ENG = ("pe", "act", "dve", "pool")
    SELF_DIST = 2

    def __init__(self, nc, es):
        self.nc = nc
        self.es = es
        self.engs = {"pe": nc.tensor, "act": nc.scalar, "dve": nc.vector, "pool": nc.gpsimd, "sp": nc.sync}
        self.prog = {k: [] for k in self.engs}
        self.sems = {}
        self.cnt = {}
        for k in self.ENG:
            self.sems[k] = es.enter_context(nc.semaphore("s_" + k))
        self.nops = {k: 0 for k in self.ENG}
        self.need = {k: set() for k in self.ENG}
        self.lastw = {}
        self.readers = {}
        self.waited = {k: {} for k in self.engs}
        self.base = {}
        self.phase = 0
        self.kphase = {}

    def _mksem(self, name):
        self.sems[name] = self.es.enter_context(self.nc.semaphore("s_" + name))
        self.cnt[name] = 0

    def _deps(self, reads, writes):
        deps = {}

        def add(s, v):
            if deps.get(s, -1) < v:
                deps[s] = v

        for k in reads:
            if k in self.lastw:
                add(*self.lastw[k])
        for k in writes:
            if k in self.lastw:
                add(*self.lastw[k])
            for s, v in self.readers.get(k, {}).items():
                add(s, v)
            if not self._static(k) and self.kphase.get(k, 0) != self.phase:
                self.kphase[k] = self.phase
                for s, v in self.base.items():
                    add(s, v)
        return deps

    STATIC = ("wr", "ps", "HT", "ss", "rs", "X1", "X2", "y", "CP", "IDB", "ONESB")

    def _static(self, k):
        kind = k[0] if isinstance(k, tuple) else k
        return kind in self.STATIC or kind.startswith("o_")

    def _waits(self, e, deps):
        for s, v in deps.items():
            if s in self.ENG:
                if s == e:
                    if e == "pe" or self.nops[e] - v >= self.SELF_DIST:
                        continue
                if self.waited[e].get(s, -1) >= v:
                    continue
                self.waited[e][s] = v
                self.need[s].add(v)
                self.prog[e].append(("we", s, v))
            else:
                if self.waited[e].get(s, 0) >= v:
                    continue
                self.waited[e][s] = v
                self.prog[e].append(("wd", s, v))

    def _record(self, rec, reads, writes):
        s, v = rec
        for k in reads:
            r = self.readers.setdefault(k, {})
            if r.get(s, -1) < v:
                r[s] = v
        for k in writes:
            self.lastw[k] = rec
            self.readers[k] = {}

    def op(self, e, fn, reads=(), writes=(), signal=True):
        self._waits(e, self._deps(reads, writes))
        idx = self.nops[e]
        self.nops[e] += 1
        self.prog[e].append(("op", fn, idx))
        self._record((e, idx), reads, writes)

    def dma(self, q, sem, out, in_, reads=(), writes=()):
        if sem not in self.sems:
            self._mksem(sem)
        self._waits(q, self._deps(reads, writes))
        self.cnt[sem] += 16
        v = self.cnt[sem]
        self.prog[q].append(("dma", out, in_, sem))
        self._record((sem, v), reads, writes)

    def _all(self):
        d = {s: v for s, v in self.cnt.items() if v > 0}
        for e in self.ENG:
            if self.nops[e] > 0:
                d[e] = self.nops[e] - 1
        return d

    def barrier(self):
        self.base = {s: v for s, v in self._all().items() if not s.startswith("wr")}
        self.phase += 1

    def finish(self, e="sp"):
        self._waits(e, self._all())

    def replay(self, e, eng):
        rank = {}
        for k in self.ENG:
            rank[k] = {idx: i + 1 for i, idx in enumerate(sorted(self.need[k]))}
        for item in self.prog[e]:
            if item[0] == "op":
                ins = item[1](eng)
                if item[2] in self.need[e]:
                    ins.then_inc(self.sems[e], 1)
            elif item[0] == "we":
                eng.wait_ge(self.sems[item[1]], rank[item[1]][item[2]])
            elif item[0] == "wd":
                eng.wait_ge(self.sems[item[1]], item[2])
            else:
                eng.dma_start(out=item[1], in_=item[2]).then_inc(self.sems[item[3]], 16)


class Arena:
    def __init__(self, nc, nbytes):
        self.t = nc.alloc_sbuf_tensor("arena", [128, nbytes // 4], F32)
        self.nbytes = nbytes
        self.off = 0

    def alloc(self, shape, dtype):
        n = int(np.prod(shape[1:]))
        esz = 4 if dtype == F32 else 2
        nb = (n * esz + 31) // 32 * 32
        assert self.off + nb <= self.nbytes, f"arena overflow {self.off + nb} > {self.nbytes}"
        ap = self.t[: shape[0], self.off // 4 : (self.off + nb) // 4]
        if dtype == BF16:
            ap = ap.bitcast(BF16)
        ap = ap[:, :n]
        self.off += nb
        if len(shape) == 3:
            ap = ap.rearrange("p (a b) -> p a b", b=shape[2])
        elif len(shape) == 4:
            ap = ap.rearrange("p (a b c) -> p a b c", b=shape[2], c=shape[3])
        return ap


class WStream:
    def __init__(self, S, wr):
        self.S = S
        self.wr = wr
        self.steps = []
        self.next = 0
        self.released = set()

    def add(self, n, emit):
        self.steps.append({"n": n, "emit": emit})
        return len(self.steps) - 1

    def plan(self):
        cur = 0
        occ = {}
        for i, st in enumerate(self.steps):
            if cur + st["n"] > NGRAN:
                cur = 0
            st["g0"] = cur
            prev = set()
            for g in range(cur, cur + st["n"]):
                if g in occ:
                    prev.add(occ[g])
                occ[g] = i
            st["prev"] = prev
            cur += st["n"]

    def pump(self):
        while self.next < len(self.steps) and self.steps[self.next]["prev"] <= self.released:
            st = self.steps[self.next]
            st["emit"](st["g0"])
            self.next += 1

    def use(self, i):
        self.pump()
        assert self.next > i, f"weight step {i} not loaded (next={self.next})"
        return self.steps[i]["g0"]

    def release(self, i):
        self.released.add(i)
        self.pump()

    def view(self, g0, nelem):
        return self.wr[:, g0 * GRAN : g0 * GRAN + nelem]


def build_program():
    nc = bass.Bass("TRN2", target_bir_lowering=False)

    def din(name, shape):
        return nc.dram_tensor(name, shape, F32, kind="ExternalInput").ap()

    def dout(name, shape):
        return nc.dram_tensor(name, shape, F32, kind="ExternalOutput").ap()

    xin = din("xin", [NCOL, D])
    pin = din("pin", [1152, DPLE])
    sca = din("sca", [480, CA])
    sff = din("sff", [32, 2 * DFF])
    cpack = din("cpack", [128, NCP])
    g_mix = din("g_mix", [D])
    g_ffn = din("g_ffn", [D])
    g_ple = din("g_ple", [D])
    g_fin = din("g_fin", [D])
    bs_p = din("bs_p", [1024])
    bs_s = din("bs_s", [1024])
    w_s = din("w_s", [8, 128, 128])
    w_in = din("w_in", [D, 4096])
    w_out = din("w_out", [D, D])
    w_up = din("w_up", [D, 2 * DFF])
    w_down = din("w_down", [DFF, D])
    w_pg = din("w_pg", [D, D])
    w_pp = din("w_pp", [DPLE, D])

    y = dout("y", [1152, D])
    o_cap = dout("o_cap", [30, CA])
    o_cas = dout("o_cas", [16, 30, CA])
    o_ffp = dout("o_ffp", [2, 2 * DFF])
    o_ffs = dout("o_ffs", [32, 2 * DFF])
    o_cvp = dout("o_cvp", [128, CA])
    o_cvs = dout("o_cvs", [128, CA])

    X1 = nc.dram_tensor("x1_scratch", [NCOL, D], F32).ap()
    X2 = nc.dram_tensor("x2_scratch", [1152, D], F32).ap()

    PS = nc.alloc_psum_tensor("PS", [128, 8, 512], F32)

    es = ExitStack()
    S = Sched(nc, es)

    total = nc.sbuf_bytes_remaining
    ar = Arena(nc, (total - 256) // 32 * 32)

    CP = ar.alloc([128, NCP], F32)
    IDF = CP[:, C_ID : C_ID + 128]
    ONESF = CP[:, C_ONES : C_ONES + 128]
    MT = CP[:, C_MT : C_MT + 128]
    BDM = CP[:, C_BDM : C_BDM + 128]
    EM = CP[0:8, C_E : C_E + 128]
    DWA = CP[:, C_DWA : C_DWA + 248].rearrange("p (j k) -> p j k", k=31)
    BDWA = CP[:, C_BDWA : C_BDWA + 8]
    GLNA = CP[:, C_GLNA : C_GLNA + 8]
    BLNA = CP[:, C_BLNA : C_BLNA + 8]
    GLNV = CP[:, C_GLNV : C_GLNV + 8]
    BLNV = CP[:, C_BLNV : C_BLNV + 8]
    DWF = CP[:, C_DWF : C_DWF + 264].rearrange("p (j k) -> p j k", k=3)
    BDWF = CP[:, C_BDWF : C_BDWF + 88]
    HM = CP[:, C_HM : C_HM + 1]
    EPSC = CP[:, C_EPS : C_EPS + 1]
    IDB = ar.alloc([128, 128], BF16)
    ONESB = ar.alloc([128, 128], BF16)
    SSC = ar.alloc([128, 128], F32)
    RSC = ar.alloc([128, 128], F32)
    HT = ar.alloc([128, 16, NCOL], BF16)
    PT8 = ar.alloc([128, 2, 128], BF16)
    WR = ar.alloc([128, NGRAN * GRAN], BF16)
    static_end = ar.off

    W = WStream(S, WR)

    mb_state = [0]
    xb_state = [0]

    def mbank():
        b = mb_state[0]
        mb_state[0] = (b + 1) % 4
        return b

    xb_list = [[4, 5, 6, 7]]
    xb2_state = [0]

    def xbank():
        lst = xb_list[0]
        b = lst[xb_state[0] % len(lst)]
        xb_state[0] += 1
        return b

    def xbank2():
        b = 4 + 2 * (xb2_state[0] % 2)
        xb2_state[0] += 1
        return b

    def psk(b):
        return ("ps", b)

    def ACT(out, in_, func, reads, writes, **kw):
        S.op("act", lambda e: e.activation(out=out, in_=in_, func=func, **kw), reads, writes)

    def TT(eng, out, in0, in1, op, reads, writes):
        S.op(eng, lambda e: e.tensor_tensor(out=out, in0=in0, in1=in1, op=op), reads, writes)

    def TS(eng, out, in0, s1, s2, op0, op1, reads, writes):
        if s2 is None:
            S.op(eng, lambda e: e.tensor_scalar(out=out, in0=in0, scalar1=s1, scalar2=None, op0=op0), reads, writes)
        else:
            S.op(eng, lambda e: e.tensor_scalar(out=out, in0=in0, scalar1=s1, scalar2=s2, op0=op0, op1=op1), reads, writes)

    def STT(eng, out, in0, sc, in1, op0, op1, reads, writes):
        S.op(eng, lambda e: e.scalar_tensor_tensor(out=out, in0=in0, scalar=sc, in1=in1, op0=op0, op1=op1), reads, writes)

    def PSTT(out, in0, sc, in1, tmp, tkey, reads, writes):
        TS("pool", tmp, in0, sc, None, ALU.mult, None, reads, [tkey])
        TT("pool", out, tmp, in1, ALU.add, reads + [tkey], writes)

    def CPY(eng, out, in_, reads, writes):
        S.op(eng, lambda e: e.tensor_copy(out=out, in_=in_), reads, writes)

    def MM(out, lhsT, rhs, start, stop, reads, writes, signal):
        S.op("pe", lambda e: e.matmul(out, lhsT=lhsT, rhs=rhs, start=start, stop=stop), reads, writes, signal=signal)

    def TR(out, in_, ident, reads, writes, signal):
        S.op("pe", lambda e: e.transpose(out=out, in_=in_, identity=ident), reads, writes, signal=signal)

    def mmgroup(out, K, lhs_fn, rhs_fn, reads, bank):
        for k in range(K):
            MM(out, lhs_fn(k), rhs_fn(k), k == 0, k == K - 1, reads, [psk(bank)], signal=(k == K - 1))

    import os
    STOP = os.environ.get("KSTOP", "")

    class _Stop(Exception):
        pass

    DUMP = os.environ.get("KDUMP", "").split(",")
    dbg_outs = []

    def dbg_dump(label, ap, reads):
        if label not in DUMP:
            return
        shp = [int(x) for x in ap.shape]
        dt = nc.dram_tensor("dbg_" + label, shp, ap.dtype, kind="ExternalOutput").ap()
        S.dma("sp", "dbg_" + label, dt, ap, reads=reads, writes=["o_dbg_" + label])

    def stop_at(name):
        if STOP == name:
            raise _Stop()

    try:
        S.dma("sp", "cp", CP, cpack[:, :], writes=["CP"])
        CPY("dve", IDB, IDF, ["CP"], ["IDB"])
        CPY("dve", ONESB, ONESF, ["CP"], ["ONESB"])
        TT("dve", ONESF, IDF, ONESF, ALU.subtract, ["CP", "ONESB"], ["CP"])

        def wload(dst, src, keys):
            S.dma("pool", f"wr{keys[0][1]}", dst, src, writes=keys)

        def mk_in(c0):
            def emit(g0):
                wload(W.view(g0, GRAN).rearrange("p (k n) -> p k n", n=128),
                      w_in[:, c0 : c0 + 128].rearrange("(k p) n -> p k n", p=128), [("wr", g0)])
            return emit

        def mk_in_pair(c0, c1):
            def emit(g0):
                wload(W.view(g0, GRAN).rearrange("p (k n) -> p k n", n=128),
                      w_in[:, c0 : c0 + 128].rearrange("(k p) n -> p k n", p=128), [("wr", g0)])
                wload(W.view(g0 + 1, GRAN).rearrange("p (k n) -> p k n", n=128),
                      w_in[:, c1 : c1 + 128].rearrange("(k p) n -> p k n", p=128), [("wr", g0 + 1)])
            return emit

        def mk_sq(wt, cg):
            def emit(g0):
                wload(W.view(g0, 4 * GRAN).rearrange("p (k n) -> p k n", n=512),
                      wt[:, cg * 512 : (cg + 1) * 512].rearrange("(k p) n -> p k n", p=128),
                      [("wr", g0 + i) for i in range(4)])
            return emit

        def mk_up_pair(j):
            def emit(g0):
                for i, c0 in enumerate((j * 128, DFF + j * 128)):
                    wload(W.view(g0 + i, GRAN).rearrange("p (k n) -> p k n", n=128),
                          w_up[:, c0 : c0 + 128].rearrange("(k p) n -> p k n", p=128), [("wr", g0 + i)])
            return emit

        def mk_down(cg, kh):
            def emit(g0):
                wload(W.view(g0, 22 * 256).rearrange("p (k n) -> p k n", n=256),
                      w_down[kh * 2816 : (kh + 1) * 2816, cg * 256 : (cg + 1) * 256].rearrange("(k p) n -> p k n", p=128),
                      [("wr", g0 + i) for i in range(3)])
            return emit

        st_a = []
        for j in range(8):
            sa = W.add(2, mk_in_pair(j * 128, 1024 + j * 128))
            su = W.add(1, mk_in(2048 + j * 128))
            sv = W.add(1, mk_in(3072 + j * 128))
            st_a.append((sa, su, sv))
        st_b = [W.add(4, mk_sq(w_out, cg)) for cg in range(4)]
        st_c = [W.add(2, mk_up_pair(j)) for j in range(44)]
        st_d = [[W.add(3, mk_down(cg, kh)) for kh in range(2)] for cg in range(8)]
        st_e = [W.add(4, mk_sq(w_pg, cg)) for cg in range(4)]
        W.plan()
        W.pump()

        SEGA = [(0, 512), (512, 1024), (1024, 1280)]
        SEGC = [(126, 638), (638, 1150), (1150, 1280)]

        def norm_a(XTt, XSt, GBC, col, xt_key, xs_key):
            ss = SSC[:, col : col + 1]
            rs = RSC[:, col : col + 1]
            ACT(XSt, XTt, AF.Square, [xt_key], [xs_key, ("ss", col)], scale=float(D ** -0.5), accum_out=ss)
            ACT(rs, ss, AF.Ln, [("ss", col), "CP"], [("rs", col)], bias=EPSC, scale=1.0)
            ACT(rs, rs, AF.Exp, [("rs", col)], [("rs", col)], scale=-0.5)
            STT("dve", XSt, XTt, rs, GBC, ALU.mult, ALU.mult, [xt_key, ("rs", col), "GBC"], [xs_key])

        def norm_b(XSt, xs_key, htcol, evac="act"):
            b = xbank2()
            pv = PS[:, b : b + 2, :].rearrange("p a b -> p (a b)").bitcast(BF16)
            for k in range(16):
                TR(pv[:, k * 128 : (k + 1) * 128], XSt[:, k * 128 : (k + 1) * 128], IDB,
                   [xs_key, "IDB"], [psk(b), psk(b + 1)], signal=(k == 15))
            if evac == "act":
                ACT(HT[:, :, htcol : htcol + 128], pv.rearrange("p (k n) -> p k n", n=128), AF.Copy,
                    [psk(b), psk(b + 1)], [("HT", htcol // 128)])
            else:
                CPY("dve", HT[:, :, htcol : htcol + 128], pv.rearrange("p (k n) -> p k n", n=128),
                    [psk(b), psk(b + 1)], [("HT", htcol // 128)])

        m0 = ar.off
        HTALL0 = [('HT', i) for i in range(NT)]
        XT = [ar.alloc([128, D], F32) for _ in range(4)]
        XS = [ar.alloc([128, D], BF16) for _ in range(4)]
        GBC = ar.alloc([128, D], F32)
        S.dma("sp", "gbc", GBC, g_mix.partition_broadcast(128), writes=["GBC"])
        for i in range(NT + 2):
            if i < NT:
                sl = i % 4
                S.dma("sp", f"xt{sl}", XT[sl], xin[i * 128 : (i + 1) * 128, :], writes=[("XT", sl)])
                norm_a(XT[sl], XS[sl], GBC, i, ("XT", sl), ("XS", sl))
            if i >= 2:
                norm_b(XS[(i - 2) % 4], ("XS", (i - 2) % 4), (i - 2) * 128, evac=("dve" if (i % 2) else "act"))
        dbg_dump('HT0', HT[:, :, :], HTALL0)
        ar.off = m0
        S.barrier()
        stop_at('0')

        MIXT = ar.alloc([128, 16, NCOL], BF16)
        mA = ar.off
        WTt = ar.alloc([128, 8, 128], BF16)
        WSt = ar.alloc([128, 8, 128], BF16)
        mA2 = ar.off
        W8 = ar.alloc([128, 8, 8], F32)
        WSF = ar.alloc([128, 8, 128], F32)
        C8 = ar.alloc([128, 128], F32)
        S.dma("sp", "wsf", WSF, w_s.rearrange("h t s -> t h s"), writes=["WSF"])
        for h in range(8):
            b = xbank()
            TR(PS[:, b, 0:128], WSF[:, h, :], IDF, ["WSF", "CP"], [psk(b)], signal=True)
            TT("dve", WTt[:, h, :], PS[:, b, 0:128], MT, ALU.mult, [psk(b), "CP"], [("WT", h)])
        S.dma("sp", "w8", W8[0:8, :, :], w_s[:, 0:8, 0:8].rearrange("h t s -> t h s"), writes=["W8"])
        for h in range(8):
            b = xbank()
            MM(PS[0:8, b, 0:128], W8[0:8, h, :], EM, True, True, ["W8", "CP"], [psk(b)], signal=True)
            CPY("dve", C8[0:8, :], PS[0:8, b, 0:128], [psk(b)], ["C8"])
            b2 = xbank()
            MM(PS[:, b2, 0:128], EM, C8[0:8, :], True, True, ["C8", "CP"], [psk(b2)], signal=True)
            TT("dve", WSt[:, h, :], PS[:, b2, 0:128], BDM, ALU.mult, [psk(b2), "CP"], [("WS", h)])

        ar.off = mA2
        S.barrier()
        stop_at('prep')
        BSH = [ar.alloc([128, 2, 128], F32) for _ in range(2)]
        Abuf = [ar.alloc([128, NCOL], F32) for _ in range(2)]
        ASX = [ar.alloc([128, 16, 38], F32) for _ in range(2)]
        SCAT = [ar.alloc([128, 4, 128], F32) for _ in range(2)]
        SIG = ar.alloc([128, NCOL], F32)
        CV = [ar.alloc([128, 1154], F32) for _ in range(2)]
        SQ = ar.alloc([128, NCOL], BF16)
        SQ2 = ar.alloc([128, NCOL], BF16)
        GV = ar.alloc([128, NCOL], F32)
        VNH = [ar.alloc([128, 10, 128], BF16) for _ in range(2)]
        CVO = [ar.alloc([128, 2, 128], F32) for _ in range(2)]
        CAS = ar.alloc([128, CA], F32)
        CAPt = ar.alloc([128, CA], F32)
        TMPB = [ar.alloc([128, 512], F32) for _ in range(1)]
        PTMP3 = ar.alloc([128, 16, 8], F32)
        CVB = ar.alloc([128, 1026], F32)
        print('phaseA arena', ar.off, ar.nbytes)
        S.op("pool", lambda e: e.memset(MIXT[:, 0:8, 0:126], 0.0), [], [("MIXa", j) for j in range(8)])

        S.dma("sp", "casd", o_cas[:, 0:22, :], sca.rearrange("(s r) c -> s r c", r=30)[:, 8:30, :], writes=["o_cas_hist"])

        HTALL = [("HT", i) for i in range(NT)]
        CVSEG_A = [(0, 512), (512, 1024), (1024, 1154)]

        def chain_ln(buf, segs, keys, SQx, sqn):
            for (c0, c1) in segs:
                w = c1 - c0
                b = xbank()
                MM(PS[:, b, 0:w], ONESF, buf[:, c0:c1], True, True, keys + ["CP"], [psk(b)], signal=True)
                ACT(SQx[:, c0:c1], PS[:, b, 0:w], AF.Square, [psk(b)], [(sqn, c0)])
                ACT(buf[:, c0:c1], PS[:, b, 0:w], AF.Copy, [psk(b)] + keys, keys)
                yield
                b2 = xbank()
                MM(PS[:, b2, 0:w], ONESB, SQx[:, c0:c1], True, True, [(sqn, c0), "ONESB"], [psk(b2)], signal=True)
                ACT(PS[:, b2, 0:w], PS[:, b2, 0:w], AF.Ln, [psk(b2), "CP"], [psk(b2)], bias=EPSC, scale=1.0)
                ACT(PS[:, b2, 0:w], PS[:, b2, 0:w], AF.Exp, [psk(b2)], [psk(b2)], scale=-0.5)
                TT("dve", buf[:, c0:c1], buf[:, c0:c1], PS[:, b2, 0:w], ALU.mult, keys + [psk(b2)], keys)
                yield

        def a_main(j):
            sa, su, sv = st_a[j]
            sl = j % 2
            A = Abuf[sl]
            akey = ("A", sl)
            g0 = W.use(sa)
            Wv = W.view(g0, GRAN).rearrange("p (k n) -> p k n", n=128)
            Wg = W.view(g0 + 1, GRAN).rearrange("p (k n) -> p k n", n=128)
            for s_, (c0, c1) in enumerate(SEGA):
                w = c1 - c0
                bv = mbank()
                mmgroup(PS[:, bv, 0:w], 16, lambda k: Wv[:, k, :], lambda k: HT[:, k, c0:c1], HTALL + [("wr", g0)], bv)
                bg = mbank()
                mmgroup(PS[:, bg, 0:w], 16, lambda k: Wg[:, k, :], lambda k: HT[:, k, c0:c1], HTALL + [("wr", g0 + 1)], bg)
                ACT(SIG[:, c0:c1], PS[:, bg, 0:w], AF.Sigmoid, [psk(bg)], [("SIG", s_)])
                ACT(A[:, c0:c1], PS[:, bv, 0:w], AF.Copy, [psk(bv)], [akey])
                yield
                yield
            W.release(sa)
            TT("dve", A[:, :], A[:, :], SIG[:, :], ALU.mult, [akey] + [("SIG", s_) for s_ in range(3)], [akey])
            g0 = W.use(su)
            Wu = W.view(g0, GRAN).rearrange("p (k n) -> p k n", n=128)
            for s_, (c0, c1) in enumerate(SEGA):
                w = c1 - c0
                bu = mbank()
                mmgroup(PS[:, bu, 0:w], 16, lambda k: Wu[:, k, :], lambda k: HT[:, k, c0:c1], HTALL + [("wr", g0)], bu)
                ACT(MIXT[:, 8 + j, c0:c1], PS[:, bu, 0:w], AF.Gelu_apprx_tanh, [psk(bu)], [("MIXb", j, s_)])
                yield
            W.release(su)
            g0 = W.use(sv)
            Wvv = W.view(g0, GRAN).rearrange("p (k n) -> p k n", n=128)
            for s_, (c0, c1) in enumerate(SEGA):
                w = c1 - c0
                bvv = mbank()
                mmgroup(PS[:, bvv, 0:w], 16, lambda k: Wvv[:, k, :], lambda k: HT[:, k, c0:c1], HTALL + [("wr", g0)], bvv)
                ACT(GV[:, c0:c1], PS[:, bvv, 0:w], AF.Gelu_apprx_tanh, [psk(bvv)], ["GV"])
                yield
            W.release(sv)

        def a_conv(j):
            sl = j % 2
            A = Abuf[sl]
            akey = ("A", sl)
            S.dma("sp", f"scat{sl}", SCAT[sl][0:120, :, :],
                  sca[:, j * 128 : (j + 1) * 128].rearrange("(t r) c -> r t c", r=120), writes=[("SCAT", sl)])
            b = xbank()
            for t in range(4):
                TR(PS[:, b, t * 120 : (t + 1) * 120], SCAT[sl][0:120, t, :], IDF[0:120, 0:120],
                   [("SCAT", sl), "CP"], [psk(b)], signal=(t == 3))
            ACT(ASX[sl][:, :, 0:30], PS[:, b, 0:480].rearrange("p (s r) -> p s r", r=30), AF.Copy, [psk(b)], [("ASXh", sl)])
            CPY("dve", ASX[sl][:, :, 30:38], A[:, 1152:1280].rearrange("p (s t) -> p s t", t=8), [akey], [("ASXn", sl)])
            b = xbank()
            TR(PS[0:30, b, 0:128], A[:, 1122:1152], IDF, [akey, "CP"], [psk(b)], signal=True)
            ACT(CAPt[0:30, j * 128 : (j + 1) * 128], PS[0:30, b, 0:128], AF.Copy, [psk(b)], [("CAP", j)])
            b = xbank()
            TR(PS[:, b, 0:128], A[:, 1152:1280], IDF, [akey, "CP"], [psk(b)], signal=True)
            ACT(CAS[:, j * 128 : (j + 1) * 128], PS[:, b, 0:128], AF.Copy, [psk(b)], [("CAS", j)])
            cv = CV[sl]
            ckm = ("CV", sl, "m")
            cks = ("CV", sl, "s")
            cvm = cv[:, 0:1026]
            cvs = cv[:, 1026:1154].rearrange("p (s t) -> p s t", t=8)
            ax = ASX[sl]
            skeys = [("ASXn", sl), ("ASXh", sl), "CP"]
            ACT(cvm, A[:, 126:1152], AF.Identity, [akey, "CP"], [ckm], scale=DWA[:, j, 30:31], bias=BDWA[:, j : j + 1])
            TS("dve", cvs, ax[:, :, 30:38], DWA[:, j, 30:31], BDWA[:, j : j + 1], ALU.mult, ALU.add, skeys, [cks])
            ACT(CVB, A[:, 96 + 29 : 96 + 29 + 1026], AF.Copy, [akey, "CP"], ["CVB"], scale=DWA[:, j, 29:30])
            STT("dve", cvs, ax[:, :, 29:37], DWA[:, j, 29:30], cvs, ALU.mult, ALU.add, skeys + [cks], [cks])
            yield
            for k in range(29):
                if k % 2 == 0:
                    STT("dve", cvm, A[:, 96 + k : 96 + k + 1026], DWA[:, j, k : k + 1], cvm, ALU.mult, ALU.add, [akey, "CP", ckm], [ckm])
                else:
                    STT("dve", CVB, A[:, 96 + k : 96 + k + 1026], DWA[:, j, k : k + 1], CVB, ALU.mult, ALU.add, [akey, "CP", "CVB"], ["CVB"])
                STT("dve", cvs, ax[:, :, k : k + 8], DWA[:, j, k : k + 1], cvs, ALU.mult, ALU.add, skeys + [cks], [cks])
                if k % 3 == 2:
                    yield
            TT("dve", cvm, cvm, CVB, ALU.add, [ckm, "CVB"], [ckm])
            yield

        def chain_a(j):
            sl = j % 2
            cv = CV[sl]
            ckm = ("CV", sl, "m")
            cks = ("CV", sl, "s")
            yield from chain_ln(cv, CVSEG_A, [ckm, cks], SQ, 'SQa')
            ACT(MIXT[:, j, 126:1280], cv[:, 0:1154], AF.Silu, [ckm, cks, "CP"], [("MIXa", j)],
                scale=GLNA[:, j : j + 1], bias=BLNA[:, j : j + 1])
            yield

        def chain_v(j):
            yield from chain_ln(GV, SEGA, ["GV"], SQ2, "SQv")
            ACT(GV[:, :], GV[:, :], AF.Identity, ["GV", "CP"], ["GV"], scale=GLNV[:, j : j + 1], bias=BLNV[:, j : j + 1])
            vs = j % 2
            vnh = VNH[vs]
            S.dma("sp", f"bsh{vs}a", BSH[vs][:, 0, :], bs_p[j * 128 : (j + 1) * 128].partition_broadcast(128), writes=[("BSH", vs, 0)])
            S.dma("sp", f"bsh{vs}b", BSH[vs][:, 1, :], bs_s[j * 128 : (j + 1) * 128].partition_broadcast(128), writes=[("BSH", vs, 1)])
            for (i0, i1) in ((0, 4), (4, 8), (8, 10)):
                n = i1 - i0
                b = xbank()
                for i in range(i0, i1):
                    TR(PS[:, b, (i - i0) * 128 : (i - i0 + 1) * 128], GV[:, i * 128 : (i + 1) * 128], IDF,
                       ["GV", "CP"], [psk(b)], signal=(i == i1 - 1))
                ACT(vnh[:, i0:i1, :], PS[:, b, 0 : n * 128].rearrange("p (a c) -> p a c", c=128), AF.Copy,
                    [psk(b)], [("VNH", vs, i0)])
                if i0 == 8:
                    ACT(CVO[vs][:, :, :], PS[:, b, 0:256].rearrange("p (a c) -> p a c", c=128), AF.Copy, [psk(b)], [("CVO", vs)])
                    S.dma("sp", f"cvo{vs}a", o_cvp[:, j * 128 : (j + 1) * 128], CVO[vs][:, 0, :], reads=[("CVO", vs)], writes=[("o_cvp", j)])
                    S.dma("sp", f"cvo{vs}b", o_cvs[:, j * 128 : (j + 1) * 128], CVO[vs][:, 1, :], reads=[("CVO", vs)], writes=[("o_cvs", j)])
                yield
            for gi, (i0, i1) in enumerate(((0, 4), (4, 8), (8, 10))):
                n = i1 - i0
                b = xbank()
                for i in range(i0, i1):
                    rhs = WTt[:, j, :] if i < 9 else WSt[:, j, :]
                    MM(PS[:, b, (i - i0) * 128 : (i - i0 + 1) * 128], vnh[:, i, :], rhs, True, True,
                       [("VNH", vs, i0), ("WT", j), ("WS", j)], [psk(b)], signal=(i == i1 - 1))
                tb = TMPB[0]
                tkey = ("TMPB", 0)
                if i0 < 8:
                    TT("dve", tb[:, 0 : n * 128].rearrange("p (a c) -> p a c", c=128),
                       PS[:, b, 0 : n * 128].rearrange("p (a c) -> p a c", c=128),
                       BSH[vs][:, 0:1, :].to_broadcast([128, n, 128]), ALU.add, [psk(b), ("BSH", vs, 0)], [tkey])
                else:
                    TT("dve", tb[:, 0:256].rearrange("p (a c) -> p a c", c=128),
                       PS[:, b, 0:256].rearrange("p (a c) -> p a c", c=128), BSH[vs][:, :, :], ALU.add,
                       [psk(b), ("BSH", vs, 0), ("BSH", vs, 1)], [tkey])
                mk = [("MIXb", j, s_) for s_ in range(3)]
                TT("dve", MIXT[:, 8 + j, i0 * 128 : i1 * 128], tb[:, 0 : n * 128], MIXT[:, 8 + j, i0 * 128 : i1 * 128],
                   ALU.mult, [tkey] + mk, mk)
                yield

        def advance(gens):
            for g in list(gens):
                try:
                    next(g)
                except StopIteration:
                    gens.remove(g)

        pending = []
        for it in range(8 + 2):
            if 1 <= it <= 8:
                pending.append(a_conv(it - 1))
                pending.append(chain_v(it - 1))
            if 2 <= it <= 9:
                pending.append(chain_a(it - 2))
            if it < 8:
                for _ in a_main(it):
                    advance(pending)
            while pending:
                advance(pending)

        stop_at('A10')
        S.dma("sp", "capo", o_cap[:, :], CAPt[0:30, :], reads=[("CAP", j) for j in range(8)], writes=["o_cap"])
        for s in range(16):
            S.dma("sp", f"caso{s % 4}", o_cas[s, 22:30, :], CAS[s * 8 : (s + 1) * 8, :],
                  reads=[("CAS", j) for j in range(8)], writes=[("o_cas", s)])

        dbg_dump('MIXT', MIXT[:, :, :], [('MIXa', j) for j in range(8)] + [('MIXb', j, s) for j in range(8) for s in range(3)])
        ar.off = mA
        S.barrier()
        stop_at('A')

        XB = [ar.alloc([128, 512], F32) for _ in range(3)]
        X1B = [ar.alloc([128, 512], F32) for _ in range(3)]
        XT = [ar.alloc([128, D], F32) for _ in range(3)]
        XS = [ar.alloc([128, D], BF16) for _ in range(6)]
        GBC = ar.alloc([128, D], F32)
        S.dma("sp", "gbc", GBC, g_ffn.partition_broadcast(128), writes=["GBC"])
        MIXALL = [("MIXa", j) for j in range(8)] + [("MIXb", j, s) for j in range(8) for s in range(3)]
        cnt = 0

        def phaseB_norm_a(i):
            sl = i % 3
            S.dma("sp", f"xt{sl}", XT[sl], X1[i * 128 : (i + 1) * 128, :], reads=[("X1", i, cg) for cg in range(4)], writes=[("XT", sl)])
            norm_a(XT[sl], XS[i % 6], GBC, 10 + i, ("XT", sl), ("XS", i % 6))

        def phaseB_norm_b(i):
            norm_b(XS[i % 6], ("XS", i % 6), i * 128)

        for cg in range(4):
            g0 = W.use(st_b[cg])
            Wo = W.view(g0, 4 * GRAN).rearrange("p (k n) -> p k n", n=512)
            wkeys = [("wr", g0 + i) for i in range(4)]
            for i in range(NT):
                r = cnt % 3
                cnt += 1
                S.dma("sp", f"xb{r}", XB[r], xin[i * 128 : (i + 1) * 128, cg * 512 : (cg + 1) * 512], writes=[("XB", r)])
                b = mbank()
                mmgroup(PS[:, b, :], 16, lambda k: MIXT[:, k, i * 128 : (i + 1) * 128], lambda k: Wo[:, k, :], MIXALL + wkeys, b)
                TT("dve", X1B[r], PS[:, b, :], XB[r], ALU.add, [psk(b), ("XB", r)], [("X1B", r)])
                S.dma("act", f"x1b{r}", X1[i * 128 : (i + 1) * 128, cg * 512 : (cg + 1) * 512], X1B[r],
                      reads=[("X1B", r)], writes=[("X1", i, cg)])
                if cg == 3 and i >= 1:
                    phaseB_norm_a(i - 1)
                if cg == 3 and i >= 5:
                    phaseB_norm_b(i - 5)
            W.release(st_b[cg])
        phaseB_norm_a(NT - 1)
        for i in range(NT - 5, NT):
            phaseB_norm_b(i)

        ar.off = static_end
        S.barrier()

        stop_at('B')
        G = ar.alloc([128, 44, 1152], BF16)
        mG = ar.off
        CX = [ar.alloc([128, 1152], F32) for _ in range(2)]
        SX = [ar.alloc([128, 16, 10], F32) for _ in range(2)]
        UPS = [ar.alloc([128, 34], F32) for _ in range(2)]
        SFB = [ar.alloc([128, 128], F32) for _ in range(2)]
        FFO = [ar.alloc([128, 128], F32) for _ in range(2)]
        xb_list[0] = [6, 7]
        g0c = {}

        def c_info(n):
            j, half = n // 2, n % 2
            blk = j + 44 * half
            pb = 3 * half
            P = PS[:, pb : pb + 3, :].rearrange("p a b -> p (a b)")
            pkeys = [psk(pb), psk(pb + 1), psk(pb + 2)]
            return j, half, blk, pb, P, pkeys

        def c_pre(n):
            j, half, blk, pb, P, pkeys = c_info(n)
            bs = n % 2
            S.dma("sp", f"sfb{bs}", SFB[bs][0:32, :], sff[:, blk * 128 : (blk + 1) * 128], writes=[("SFB", bs)])

        def c_tr(n):
            bs = n % 2
            xb = xbank()
            TR(PS[:, xb, 0:32], SFB[bs][0:32, :], IDF[0:32, 0:32], [("SFB", bs), "CP"], [psk(xb)], signal=True)
            ACT(SX[bs][:, :, 0:2], PS[:, xb, 0:32].rearrange("p (s r) -> p s r", r=2), AF.Copy, [psk(xb)], [("SXh", bs)])

        def c_mm(n):
            j, half, blk, pb, P, pkeys = c_info(n)
            bs = n % 2
            if half == 0:
                g0c[j] = W.use(st_c[j])
            g0 = g0c[j]
            Wb = W.view(g0 + half, GRAN).rearrange("p (k n) -> p k n", n=128)
            for s_, (c0, c1) in enumerate(SEGC):
                w = c1 - c0
                mmgroup(PS[:, pb + s_, 0:w], 16, lambda k: Wb[:, k, :], lambda k: HT[:, k, c0:c1], HTALL + [("wr", g0 + half)], pb + s_)
            if half == 1:
                W.release(st_c[j])
            TS("dve", P[:, 0:2], P[:, 0:2], HM, None, ALU.mult, None, [psk(pb), "CP"], [psk(pb)])

        def c_post(n):
            j, half, blk, pb, P, pkeys = c_info(n)
            bs = n % 2
            ACT(UPS[bs][:, :].rearrange("p (i r) -> p i r", r=2),
                P[:, 1024 : 1024 + 136].rearrange("p (i r) -> p i r", r=8)[:, :, 0:2], AF.Copy, pkeys, [("UPS", bs)])
            sx = SX[bs]
            ACT(sx[:, :, 2:10], P[:, 1026:1154].rearrange("p (s t) -> p s t", t=8), AF.Copy, pkeys, [("SXn", bs)])
            xb = xbank()
            TR(PS[0:34, xb, 0:128], UPS[bs][:, :], IDF, [("UPS", bs), "CP"], [psk(xb)], signal=True)
            ACT(FFO[bs][0:34, :], PS[0:34, xb, 0:128], AF.Copy, [psk(xb)], [("FFO", bs)])
            S.dma("act", f"ffo{bs}a", o_ffp[:, blk * 128 : (blk + 1) * 128], FFO[bs][0:2, :], reads=[("FFO", bs)], writes=[("o_ffp", blk)])
            S.dma("act", f"ffo{bs}b", o_ffs[:, blk * 128 : (blk + 1) * 128], FFO[bs][2:34, :], reads=[("FFO", bs)], writes=[("o_ffs", blk)])
            cx = CX[half]
            ckm = ("CX", half, "m")
            cks = ("CX", half, "s")
            cxs = cx[:, 1024:1152].rearrange("p (s t) -> p s t", t=8)
            skeys = [("SXh", bs), ("SXn", bs), "CP"]
            TS("dve", cx[:, 0:1024], P[:, 2:1026], DWF[:, blk, 2:3], BDWF[:, blk : blk + 1], ALU.mult, ALU.add, pkeys + ["CP"], [ckm])
            TS("dve", cxs, sx[:, :, 2:10], DWF[:, blk, 2:3], BDWF[:, blk : blk + 1], ALU.mult, ALU.add, skeys, [cks])
            STT("dve", cx[:, 0:1024], P[:, 1:1025], DWF[:, blk, 1:2], cx[:, 0:1024], ALU.mult, ALU.add, pkeys + ["CP", ckm], [ckm])
            STT("dve", cxs, sx[:, :, 1:9], DWF[:, blk, 1:2], cxs, ALU.mult, ALU.add, skeys + [cks], [cks])
            STT("dve", cx[:, 0:1024], P[:, 0:1024], DWF[:, blk, 0:1], cx[:, 0:1024], ALU.mult, ALU.add, pkeys + ["CP", ckm], [ckm])
            STT("dve", cxs, sx[:, :, 0:8], DWF[:, blk, 0:1], cxs, ALU.mult, ALU.add, skeys + [cks], [cks])
            if half == 0:
                ACT(cx[:, :], cx[:, :], AF.Silu, [ckm, cks], [ckm, cks])
            else:
                TT("dve", G[:, j, :], CX[0][:, :], CX[1][:, :], ALU.mult,
                   [("CX", 0, "m"), ("CX", 0, "s"), ("CX", 1, "m"), ("CX", 1, "s")], [("G", j)])

        c_pre(0)
        for n in range(89):
            if n < 88:
                if n + 1 < 88:
                    c_pre(n + 1)
                c_tr(n)
                c_mm(n)
            if n >= 1:
                c_post(n - 1)

        stop_at('C')
        xb_list[0] = [4, 5, 6, 7]
        ar.off = mG
        S.barrier()
        X1R = [ar.alloc([128, 256], F32) for _ in range(4)]
        X2B = [ar.alloc([128, 256], F32) for _ in range(4)]
        GPB = [ar.alloc([128, 256], F32) for _ in range(2)]
        XGB = [ar.alloc([128, 256], BF16) for _ in range(3)]
        SQJ = ar.alloc([128, 256], BF16)
        PIN1 = ar.alloc([128, DPLE], F32)
        PB1 = ar.alloc([128, DPLE], BF16)
        GALL = [("G", j) for j in range(44)]

        def d_pload(t):
            S.dma("sp", "pin1", PIN1, pin[t * 128 : (t + 1) * 128, :], writes=["PIN1"])
            CPY("dve", PB1, PIN1, ["PIN1"], ["PB1"])

        def d_ptr(t):
            xb = xbank()
            pv = PS[:, xb, 0:128].bitcast(BF16)
            for k in range(2):
                TR(pv[:, k * 128 : (k + 1) * 128], PB1[:, k * 128 : (k + 1) * 128], IDB, ["PB1", "IDB"], [psk(xb)], signal=(k == 1))
            if t < 8:
                ACT(HT[:, 2 * t : 2 * t + 2, 0:128], pv.rearrange("p (k n) -> p k n", n=128), AF.Copy, [psk(xb)], [("HT", 0), ("PTH", t)])
            else:
                ACT(PT8[:, :, :], pv.rearrange("p (k n) -> p k n", n=128), AF.Copy, [psk(xb)], [("PTH", t)])

        cnt = 0
        dq = []

        def d_tr(n, cg, t):
            slot = n % 3
            xb = xbank()
            pv = PS[:, xb, 0:128].bitcast(BF16)
            for k in range(2):
                TR(pv[:, k * 128 : (k + 1) * 128], XGB[slot][:, k * 128 : (k + 1) * 128], IDB, [("XGB", slot), "IDB"], [psk(xb)], signal=(k == 1))
            ACT(HT[:, 2 * cg : 2 * cg + 2, (t + 1) * 128 : (t + 2) * 128], pv.rearrange("p (k n) -> p k n", n=128), AF.Copy,
                [psk(xb)], [("HT", t + 1)])

        for cg in range(8):
            S.dma("sp", f"gpb{cg % 2}", GPB[cg % 2], g_ple[cg * 256 : (cg + 1) * 256].partition_broadcast(128), writes=[("GPB", cg % 2)])
            g0s = [W.use(st_d[cg][kh]) for kh in range(2)]
            Wd = [W.view(g0s[kh], 22 * 256).rearrange("p (k n) -> p k n", n=256) for kh in range(2)]
            wkeys = [("wr", g0s[kh] + i) for kh in range(2) for i in range(3)]
            for t in range(9):
                r = cnt % 4
                cnt += 1
                row0 = (t + 1) * 128
                S.dma("sp", f"x1r{r}", X1R[r], X1[row0 : row0 + 128, cg * 256 : (cg + 1) * 256],
                      reads=[("X1", t + 1, cg // 2)], writes=[("X1R", r)])
                b = mbank()
                mmgroup(PS[:, b, 0:256], 44, lambda k: G[:, k, t * 128 : (t + 1) * 128],
                        lambda k: Wd[k // 22][:, k % 22, :], GALL + wkeys, b)
                TT("dve", X2B[r], PS[:, b, 0:256], X1R[r], ALU.add, [psk(b), ("X1R", r)], [("X2B", r)])
                S.dma("act", f"x2b{r}", X2[t * 128 : (t + 1) * 128, cg * 256 : (cg + 1) * 256], X2B[r],
                      reads=[("X2B", r)], writes=[("X2", t, cg)])
                col = 32 + t * 8 + cg
                ACT(SQJ, X2B[r], AF.Square, [("X2B", r)], ["SQJ", ("ss", col)], scale=float(D ** -0.5), accum_out=SSC[:, col : col + 1])
                n = cg * 9 + t
                TT("dve", XGB[n % 3], X2B[r], GPB[cg % 2], ALU.mult, [("X2B", r), ("GPB", cg % 2)], [("XGB", n % 3)])
                dq.append((n, cg, t))
                if len(dq) > 2:
                    d_tr(*dq.pop(0))
                if cg == 1:
                    if t >= 1:
                        d_ptr(t - 1)
                    d_pload(t)
                if cg == 2 and t == 0:
                    d_ptr(8)
            if cg < 7:
                W.release(st_d[cg][0])
                W.release(st_d[cg][1])
        while dq:
            d_tr(*dq.pop(0))

        ar.off = static_end
        S.barrier()

        stop_at('D')
        X3 = ar.alloc([128, 9, D], F32)
        GBC = ar.alloc([128, D], F32)
        XS = [ar.alloc([128, D], BF16) for _ in range(1)] * 2
        WP = ar.alloc([128, 2, D], BF16)
        SG = [ar.alloc([128, 512], F32) for _ in range(1)] * 2
        TP = [ar.alloc([128, 512], F32) for _ in range(1)] * 2
        S.dma("pool", "wp", WP, w_pp.rearrange("(k p) n -> p k n", p=128), writes=["WP"])
        W.release(st_d[7][0])
        W.release(st_d[7][1])
        S.dma("sp", "gbc", GBC, g_fin.partition_broadcast(128), writes=["GBC"])
        for t in range(9):
            ACT(RSC[:, 120:128], SSC[:, 32 + t * 8 : 40 + t * 8], AF.Copy, [("ss", 32 + t * 8 + c) for c in range(8)],
                ["junk8", ("ss", 104 + t)], accum_out=SSC[:, 104 + t : 105 + t])
        ACT(RSC[:, 104:113], SSC[:, 104:113], AF.Ln, [("ss", 104 + t) for t in range(9)] + ["CP"], ["rs3"], bias=EPSC, scale=1.0)
        ACT(RSC[:, 104:113], RSC[:, 104:113], AF.Exp, ["rs3"], ["rs3"], scale=-0.5)
        def e_x3(t):
            S.dma("sp", f"x3l{t}", X3[:, t, :], X2[t * 128 : (t + 1) * 128, :], reads=[("X2", t, cg) for cg in range(8)], writes=[("X3", t)])

        e_x3(0)
        e_x3(1)
        cnt = 0
        for cg in range(4):
            g0 = W.use(st_e[cg])
            Wg = W.view(g0, 4 * GRAN).rearrange("p (k n) -> p k n", n=512)
            wkeys = [("wr", g0 + i) for i in range(4)]
            for t in range(9):
                if cg == 0 and t + 2 <= 8:
                    e_x3(t + 2)
                r = cnt % 2
                cnt += 1
                hc = (t + 1) * 128
                b = mbank()
                mmgroup(PS[:, b, :], 16, lambda k: HT[:, k, hc : hc + 128], lambda k: Wg[:, k, :], [("HT", t + 1)] + wkeys, b)
                b2 = mbank()
                mmgroup(PS[:, b2, :], 2, lambda k: (HT[:, 2 * t + k, 0:128] if t < 8 else PT8[:, k, :]),
                        lambda k: WP[:, k, cg * 512 : (cg + 1) * 512], [("PTH", t), "WP"], b2)
                ACT(SG[r], PS[:, b, :], AF.Sigmoid, [psk(b), "rs3"], [("SG", 0)], scale=RSC[:, 104 + t : 105 + t])
                TT("dve", TP[r], SG[r], PS[:, b2, :], ALU.mult, [("SG", 0), psk(b2)], [("TP", 0)])
                x3s = X3[:, t, cg * 512 : (cg + 1) * 512]
                TT("dve", x3s, x3s, TP[r], ALU.add, [("X3", t), ("TP", 0)], [("X3", t)])
                if cg == 3:
                    col = 20 + t
                    ss = SSC[:, col : col + 1]
                    ACT(XS[t % 2], X3[:, t, :], AF.Square, [("X3", t)], [("XS", t % 2), ("ss", col)], scale=float(D ** -0.5), accum_out=ss)
            W.release(st_e[cg])
        for t in range(9):
            col = 20 + t
            ACT(RSC[:, col : col + 1], SSC[:, col : col + 1], AF.Ln, [("ss", col), "CP"], [("rs", col)], bias=EPSC, scale=1.0)
        for t in range(9):
            col = 20 + t
            ACT(RSC[:, col : col + 1], RSC[:, col : col + 1], AF.Exp, [("rs", col)], [("rs", col)], scale=-0.5)
        for t in range(9):
            col = 20 + t
            x3t = X3[:, t, :]
            STT("dve", x3t, x3t, RSC[:, col : col + 1], GBC, ALU.mult, ALU.mult, [("X3", t), ("rs", col), "GBC"], [("X3", t)])
            S.dma("sp", f"yo{t % 2}", y[t * 128 : (t + 1) * 128, :], x3t, reads=[("X3", t)], writes=[("y", t)])
    except _Stop:
        pass

    S.finish("sp")
    print("n semaphores", len(S.sems))

    with nc.Block() as block:
        @block.tensor
        def _(e):
            S.replay("pe", e)

        @block.scalar
        def _(e):
            S.replay("act", e)

        @block.vector
        def _(e):
            S.replay("dve", e)

        @block.gpsimd
        def _(e):
            S.replay("pool", e)

        @block.sync
        def _(e):
            S.replay("sp", e)

    es.close()
    return nc


def _cpack(inp, half):
    cp = np.zeros((128, NCP), np.float32)
    cp[:, C_ID : C_ID + 128] = np.eye(128, dtype=np.float32)
    cp[:, C_ONES : C_ONES + 128] = 1.0 / 128.0
    s = np.arange(128)
    cp[:, C_MT : C_MT + 128] = (s[:, None] <= s[None, :]).astype(np.float32)
    cp[:, C_BDM : C_BDM + 128] = ((s[:, None] // 8 == s[None, :] // 8) & (s[:, None] % 8 <= s[None, :] % 8)).astype(np.float32)
    cp[0:8, C_E : C_E + 128] = (s[None, :] % 8 == np.arange(8)[:, None]).astype(np.float32)
    cp[:, C_DWA : C_DWA + 248] = inp["w_dw_a"][0].reshape(31, 8, 128).transpose(2, 1, 0).reshape(128, 248)

    def v8(v):
        return v.reshape(8, 128).T

    cp[:, C_BDWA : C_BDWA + 8] = v8(inp["b_dw_a"][0])
    cp[:, C_GLNA : C_GLNA + 8] = v8(inp["g_ln_a"][0])
    cp[:, C_BLNA : C_BLNA + 8] = v8(inp["b_ln_a"][0])
    cp[:, C_GLNV : C_GLNV + 8] = v8(inp["g_ln_v"][0])
    cp[:, C_BLNV : C_BLNV + 8] = v8(inp["b_ln_v"][0])
    cp[:, C_DWF : C_DWF + 264] = inp["w_dw_f"][0].reshape(3, 88, 128).transpose(2, 1, 0).reshape(128, 264)
    cp[:, C_BDWF : C_BDWF + 88] = inp["b_dw_f"][0].reshape(88, 128).T
    cp[:, C_HM] = float(half)
    cp[:, C_EPS] = EPS
    return cp


_NC_CACHE = {}


def kernel(**inp):
    inp = {k: np.asarray(v) for k, v in inp.items()}
    xp, xs = inp["x_prompt"], inp["x_sample"]
    pp, ps = inp["p_prompt"][0], inp["p_sample"][0]
    sca_all, sff_all = inp["state_conv_a"][0], inp["state_ffn_conv"][0]
    f32 = np.float32

    shared = {
        "g_mix": np.ascontiguousarray(inp["g_mix"][0], f32),
        "g_ffn": np.ascontiguousarray(inp["g_ffn"][0], f32),
        "g_ple": np.ascontiguousarray(inp["g_ple"][0], f32),
        "g_fin": np.ascontiguousarray(inp["g_final"], f32),
        "bs_p": np.ascontiguousarray(inp["b_s"][0].reshape(1024), f32),
        "bs_s": np.ascontiguousarray(np.tile(inp["b_s"][0][:, :8], (1, 16)).reshape(1024), f32),
        "w_s": np.ascontiguousarray(inp["w_s"][0], f32),
        "w_in": np.ascontiguousarray(inp["w_in"][0], f32),
        "w_out": np.ascontiguousarray(inp["w_out"][0], f32),
        "w_up": np.ascontiguousarray(inp["w_up"][0], f32),
        "w_down": np.ascontiguousarray(inp["w_down"][0], f32),
        "w_pg": np.ascontiguousarray(inp["w_ple_gate"][0], f32),
        "w_pp": np.ascontiguousarray(inp["w_ple_proj"][0], f32),
    }
    in_maps = []
    for c in range(8):
        b, half = c // 2, c % 2
        halo = xp[b, 896:1024] if half else np.zeros((128, D), f32)
        xin = np.concatenate([halo, xp[b, half * 1024 : (half + 1) * 1024], xs[c * 16 : (c + 1) * 16].reshape(128, D)], 0)
        pin = np.concatenate([pp[b, half * 1024 : (half + 1) * 1024], ps[c * 16 : (c + 1) * 16].reshape(128, DPLE)], 0)
        m = dict(shared)
        m["xin"] = np.ascontiguousarray(xin, f32)
        m["pin"] = np.ascontiguousarray(pin, f32)
        m["sca"] = np.ascontiguousarray(sca_all[c * 16 : (c + 1) * 16].reshape(480, CA), f32)
        m["sff"] = np.ascontiguousarray(sff_all[c * 16 : (c + 1) * 16].reshape(32, 2 * DFF), f32)
        m["cpack"] = _cpack(inp, half)
        in_maps.append(m)

    if "nc" not in _NC_CACHE:
        _NC_CACHE["nc"] = build_program()
    nc = _NC_CACHE["nc"]
    res = run_bass_kernel_spmd(nc, in_maps, core_ids=list(range(8)))
    R = res.results

    y_prompt = np.zeros((4, 2048, D), f32)
    y_sample = np.zeros((128, 8, D), f32)
    conv_a_p = np.zeros((1, 4, 30, CA), f32)
    conv_a_s = np.zeros((1, 128, 30, CA), f32)
    ffn_p = np.zeros((1, 4, 2, 2 * DFF), f32)
    ffn_s = np.zeros((1, 128, 2, 2 * DFF), f32)
    cv_p = np.zeros((1, 4, 128, CA), f32)
    cv_s = np.zeros((1, 128, 8, CA), f32)
    for c in range(8):
        b, half = c // 2, c % 2
        r = R[c]
        y_prompt[b, half * 1024 : (half + 1) * 1024] = r["y"][:1024]
        y_sample[c * 16 : (c + 1) * 16] = r["y"][1024:].reshape(16, 8, D)
        conv_a_s[0, c * 16 : (c + 1) * 16] = r["o_cas"]
        ffn_s[0, c * 16 : (c + 1) * 16] = r["o_ffs"].reshape(16, 2, 2 * DFF)
        cv_s[0, c * 16 : (c + 1) * 16] = r["o_cvs"].reshape(16, 8, CA)
        if half == 1:
            conv_a_p[0, b] = r["o_cap"]
            ffn_p[0, b] = r["o_ffp"]
            cv_p[0, b] = r["o_cvp"]
    return (y_prompt, y_sample, conv_a_p, conv_a_s, ffn_p, ffn_s, cv_p, cv_s)
```
